# Optimizing a Trainium2 kernel written in Bass

```python
import math
import jax, jax.numpy as jnp
from jax import lax
import numpy as np

D_MODEL = 1024
BATCH = 8
SEQ = 4096
DEPTH = 1
DEC_BATCH = 32
DEC_SEQ = 64
PAST_LEN = 4096

CHUNK = 64
SSM_EXPAND = 2
D_INNER = SSM_EXPAND * D_MODEL
SSM_HEAD_DIM = 64
SSM_HEADS = D_INNER // SSM_HEAD_DIM
N_GROUPS = 4
HEADS_PER_GROUP = SSM_HEADS // N_GROUPS
D_STATE = 128
CONV_W = 4
CONV_DIM = D_INNER + 2 * N_GROUPS * D_STATE
SSD_CHUNK = 64
ATT_HEADS = 16
ATT_HEAD_DIM = 64
D_ATT = ATT_HEADS * ATT_HEAD_DIM
LEFT_CHUNKS = 8
ATT_WINDOW = LEFT_CHUNKS * CHUNK
BAND = ATT_WINDOW + CHUNK
REL_CLIP = 256
SPLIT_SIZES = (D_INNER, CONV_DIM, SSM_HEADS, D_ATT, D_ATT, D_ATT, D_ATT, D_MODEL, D_MODEL)
D_IN_PROJ = sum(SPLIT_SIZES)
NORM_EPS = 1e-5
NEG_INF = -1e30

kernel_name = 'hybrid_ssd_chunkband_attn_stream_step'


def rmsnorm(x, w):
    xf = x.astype(jnp.float32)
    xf = xf * lax.rsqrt(jnp.mean(xf * xf, axis=-1, keepdims=True) + NORM_EPS)
    return xf.astype(x.dtype) * w


def split_projection(x, norm_w, w_in):
    xn = rmsnorm(x, norm_w)
    proj = jnp.einsum('bld,de->ble', xn, w_in)
    points, acc = [], 0
    for s in SPLIT_SIZES[:-1]:
        acc += s
        points.append(acc)
    return jnp.split(proj, points, axis=-1)


def causal_conv(u, buf, conv_w, conv_b):
    L = u.shape[1]
    up = jnp.concatenate([buf.astype(u.dtype), u], axis=1)
    y = conv_b
    for j in range(CONV_W):
        y = y + conv_w[j] * up[:, j:j + L]
    return jax.nn.silu(y), up[:, -(CONV_W - 1):]


def ssd_scan(xdt, adt, bm, cm, h0):
    b, L = xdt.shape[:2]
    n_chunks = -(-L // SSD_CHUNK)
    pad = n_chunks * SSD_CHUNK - L

    def blocks(t):
        t = jnp.pad(t, [(0, 0), (0, pad)] + [(0, 0)] * (t.ndim - 2))
        t = t.reshape((b, n_chunks, SSD_CHUNK) + t.shape[2:])
        return jnp.moveaxis(t, 1, 0)

    tri = jnp.tril(jnp.ones((SSD_CHUNK, SSD_CHUNK), bool))[None, :, :, None, None]

    def step(h, inp):
        xc, ac, bc, cc = inp
        acs = jnp.cumsum(ac, axis=1)
        seg = acs[:, :, None] - acs[:, None, :]
        decay = jnp.exp(jnp.where(tri, seg, -jnp.inf))
        cb = jnp.einsum('blgn,bsgn->blsg', cc, bc)
        y_diag = jnp.einsum('blsgr,bsgrp->blgrp', cb[..., None] * decay, xc)
        y_off = jnp.einsum('blgn,bgrpn->blgrp', cc, h) * jnp.exp(acs)[..., None]
        to_end = jnp.exp(acs[:, -1:] - acs)
        h_new = h * jnp.exp(acs[:, -1])[..., None, None] + jnp.einsum(
            'blgn,blgrp->bgrpn', bc, xc * to_end[..., None])
        return h_new, y_diag + y_off

    h_final, ys = lax.scan(step, h0, (blocks(xdt), blocks(adt), blocks(bm), blocks(cm)))
    ys = jnp.moveaxis(ys, 0, 1).reshape((b, n_chunks * SSD_CHUNK) + ys.shape[3:])[:, :L]
    return ys, h_final


def ssm_branch(z, xbc, dt_raw, conv_buf, h0, conv_w, conv_b, dt_bias, a_log, d_skip,
               ssm_norm_w, w_out_ssm):
    b, L, _ = z.shape
    f32 = jnp.float32
    xbc_c, new_buf = causal_conv(xbc, conv_buf, conv_w, conv_b)
    xs, bm, cm = jnp.split(xbc_c, [D_INNER, D_INNER + N_GROUPS * D_STATE], axis=-1)
    xs = xs.astype(f32).reshape(b, L, N_GROUPS, HEADS_PER_GROUP, SSM_HEAD_DIM)
    bm = bm.astype(f32).reshape(b, L, N_GROUPS, D_STATE)
    cm = cm.astype(f32).reshape(b, L, N_GROUPS, D_STATE)
    dt = jax.nn.softplus((dt_raw + dt_bias).astype(f32)).reshape(b, L, N_GROUPS, HEADS_PER_GROUP)
    a = -jnp.exp(a_log.astype(f32)).reshape(N_GROUPS, HEADS_PER_GROUP)
    h0 = h0.astype(f32).reshape(b, N_GROUPS, HEADS_PER_GROUP, SSM_HEAD_DIM, D_STATE)
    y, h_final = ssd_scan(xs * dt[..., None], dt * a, bm, cm, h0)
    y = y + d_skip.astype(f32).reshape(N_GROUPS, HEADS_PER_GROUP)[:, :, None] * xs
    yz = y.reshape(b, L, D_INNER) * jax.nn.silu(z.astype(f32))
    yz = yz.reshape(b, L, N_GROUPS, D_INNER // N_GROUPS)
    yz = yz * lax.rsqrt(jnp.mean(yz * yz, axis=-1, keepdims=True) + NORM_EPS)
    yz = yz.reshape(b, L, D_INNER).astype(z.dtype) * ssm_norm_w
    y_out = jnp.einsum('bld,de->ble', yz, w_out_ssm)
    h_final = h_final.reshape(b, SSM_HEADS, SSM_HEAD_DIM, D_STATE).astype(z.dtype)
    return y_out, new_buf, h_final


def rel_bias_matrix(rel_bias, n_q, n_k, offset):
    d = offset + jnp.arange(n_q)[:, None] - jnp.arange(n_k)[None, :]
    idx = jnp.clip(d, -REL_CLIP, REL_CLIP) + REL_CLIP
    return rel_bias[:, idx].astype(jnp.float32)


def band_attention(q, k, v, bias, valid):
    s = jnp.einsum('bthd,bshd->bhts', q, k, preferred_element_type=jnp.float32)
    s = s * (ATT_HEAD_DIM ** -0.5) + bias
    if valid is not None:
        s = jnp.where(valid, s, NEG_INF)
    p = jax.nn.softmax(s, axis=-1)
    return jnp.einsum('bhts,bshd->bthd', p.astype(v.dtype), v)


def chunk_band_attention_prompt(q, k, v, rel_bias):
    b, L = q.shape[:2]
    n_chunks = L // CHUNK
    kp = jnp.pad(k, ((0, 0), (ATT_WINDOW, 0), (0, 0), (0, 0)))
    vp = jnp.pad(v, ((0, 0), (ATT_WINDOW, 0), (0, 0), (0, 0)))
    bias = rel_bias_matrix(rel_bias, CHUNK, BAND, ATT_WINDOW)
    j_idx = jnp.arange(BAND)

    def one_chunk(c):
        start = c * CHUNK
        qc = lax.dynamic_slice_in_dim(q, start, CHUNK, axis=1)
        kc = lax.dynamic_slice_in_dim(kp, start, BAND, axis=1)
        vc = lax.dynamic_slice_in_dim(vp, start, BAND, axis=1)
        valid = (start - ATT_WINDOW + j_idx >= 0)[None, :]
        return band_attention(qc, kc, vc, bias, valid)

    o = lax.map(one_chunk, jnp.arange(n_chunks))
    o = jnp.moveaxis(o, 0, 1).reshape(b, L, D_ATT)
    keep = min(ATT_WINDOW, L)
    return o, k[:, -keep:], v[:, -keep:]


def chunk_band_attention_step(q, k, v, k_past, v_past, rel_bias):
    b, T = q.shape[:2]
    w = k_past.shape[1]
    kk = jnp.concatenate([k_past.astype(k.dtype), k], axis=1)
    vv = jnp.concatenate([v_past.astype(v.dtype), v], axis=1)
    bias = rel_bias_matrix(rel_bias, T, w + T, w)
    o = band_attention(q, kk, vv, bias, None).reshape(b, T, D_ATT)
    return o, kk[:, -w:], vv[:, -w:]


def hybrid_layer(x, h0, conv_buf, k_past, v_past, norm_w, w_in, conv_w, conv_b, dt_bias,
                 a_log, d_skip, ssm_norm_w, w_out_ssm, rel_bias, w_out_att, w_o):
    b, L, _ = x.shape
    z, xbc, dt_raw, q, k, v, g_att, gate_ssm, gate_att = split_projection(x, norm_w, w_in)
    y_ssm, new_conv, new_h = ssm_branch(z, xbc, dt_raw, conv_buf, h0, conv_w, conv_b, dt_bias,
                                        a_log, d_skip, ssm_norm_w, w_out_ssm)
    q = q.reshape(b, L, ATT_HEADS, ATT_HEAD_DIM)
    k = k.reshape(b, L, ATT_HEADS, ATT_HEAD_DIM)
    v = v.reshape(b, L, ATT_HEADS, ATT_HEAD_DIM)
    if k_past is None:
        o, new_k, new_v = chunk_band_attention_prompt(q, k, v, rel_bias)
    else:
        o, new_k, new_v = chunk_band_attention_step(q, k, v, k_past, v_past, rel_bias)
    y_att = jnp.einsum('bld,de->ble', o * jax.nn.silu(g_att), w_out_att)
    merged = jax.nn.sigmoid(gate_ssm) * y_ssm + jax.nn.sigmoid(gate_att) * y_att
    x = x + jnp.einsum('bld,de->ble', merged, w_o)
    return x, new_h, new_conv, new_k, new_v


def setup_inputs(seed: int = 0) -> dict:
    key = jax.random.key(seed)
    ks = jax.random.split(key, 20)
    f32 = jnp.float32
    att_cache = min(ATT_WINDOW, PAST_LEN)

    def nrm(k, shape, scale):
        return scale * jax.random.normal(k, shape, f32)

    dt = jnp.exp(jax.random.uniform(ks[10], (DEPTH, SSM_HEADS), f32,
                                    math.log(1e-3), math.log(1e-1)))
    return {
        'x_prompt': nrm(ks[0], (BATCH, SEQ, D_MODEL), 1.0),
        'x_sample': nrm(ks[1], (DEC_BATCH, DEC_SEQ, D_MODEL), 1.0),
        'state_ssm': nrm(ks[2], (DEPTH, DEC_BATCH, SSM_HEADS, SSM_HEAD_DIM, D_STATE), 0.1),
        'state_conv': nrm(ks[3], (DEPTH, DEC_BATCH, CONV_W - 1, CONV_DIM), 1.0),
        'cache_k': nrm(ks[4], (DEPTH, DEC_BATCH, att_cache, ATT_HEADS, ATT_HEAD_DIM), 1.0),
        'cache_v': nrm(ks[5], (DEPTH, DEC_BATCH, att_cache, ATT_HEADS, ATT_HEAD_DIM), 1.0),
        'norm_w': 1.0 + nrm(ks[6], (DEPTH, D_MODEL), 0.01),
        'w_in': nrm(ks[7], (DEPTH, D_MODEL, D_IN_PROJ), D_MODEL ** -0.5),
        'conv_w': nrm(ks[8], (DEPTH, CONV_W, CONV_DIM), CONV_W ** -0.5),
        'conv_b': nrm(ks[9], (DEPTH, CONV_DIM), 0.01),
        'dt_bias': dt + jnp.log(-jnp.expm1(-dt)),
        'a_log': jnp.log(jax.random.uniform(ks[11], (DEPTH, SSM_HEADS), f32, 1.0, 16.0)),
        'd_skip': 1.0 + nrm(ks[12], (DEPTH, SSM_HEADS), 0.1),
        'ssm_norm_w': 1.0 + nrm(ks[13], (DEPTH, D_INNER), 0.01),
        'w_out_ssm': nrm(ks[14], (DEPTH, D_INNER, D_MODEL), D_INNER ** -0.5),
        'rel_bias': nrm(ks[15], (DEPTH, ATT_HEADS, 2 * REL_CLIP + 1), 0.1),
        'w_out_att': nrm(ks[16], (DEPTH, D_ATT, D_MODEL), D_ATT ** -0.5),
        'w_o': nrm(ks[17], (DEPTH, D_MODEL, D_MODEL), D_MODEL ** -0.5),
        'final_norm_w': 1.0 + nrm(ks[18], (D_MODEL,), 0.01),
    }


def reference(x_prompt, x_sample, state_ssm, state_conv, cache_k, cache_v,
              norm_w, w_in, conv_w, conv_b, dt_bias, a_log, d_skip, ssm_norm_w,
              w_out_ssm, rel_bias, w_out_att, w_o, final_norm_w):
    h_p, h_s = x_prompt, x_sample
    bp = x_prompt.shape[0]
    h0_p = jnp.zeros((bp, SSM_HEADS, SSM_HEAD_DIM, D_STATE), x_prompt.dtype)
    conv0_p = jnp.zeros((bp, CONV_W - 1, CONV_DIM), x_prompt.dtype)
    ssm_p, conv_p, k_p, v_p = [], [], [], []
    ssm_s, conv_s, k_s, v_s = [], [], [], []
    for l in range(DEPTH):
        lw = (norm_w[l], w_in[l], conv_w[l], conv_b[l], dt_bias[l], a_log[l], d_skip[l],
              ssm_norm_w[l], w_out_ssm[l], rel_bias[l], w_out_att[l], w_o[l])
        h_p, a1, a2, a3, a4 = hybrid_layer(h_p, h0_p, conv0_p, None, None, *lw)
        ssm_p.append(a1); conv_p.append(a2); k_p.append(a3); v_p.append(a4)
        h_s, b1, b2, b3, b4 = hybrid_layer(h_s, state_ssm[l], state_conv[l],
                                           cache_k[l], cache_v[l], *lw)
        ssm_s.append(b1); conv_s.append(b2); k_s.append(b3); v_s.append(b4)
    y_prompt = rmsnorm(h_p, final_norm_w)
    y_sample = rmsnorm(h_s, final_norm_w)
    new_ssm_p = jnp.stack(ssm_p)
    new_conv_p = jnp.stack(conv_p)
    new_k_p = jnp.stack(k_p)
    new_v_p = jnp.stack(v_p)
    new_ssm_s = jnp.stack(ssm_s)
    new_conv_s = jnp.stack(conv_s)
    new_k_s = jnp.stack(k_s)
    new_v_s = jnp.stack(v_s)
    return (y_prompt, y_sample, new_ssm_p, new_conv_p, new_k_p, new_v_p,
            new_ssm_s, new_conv_s, new_k_s, new_v_s)
```

```python
import numpy as np
from contextlib import ExitStack
import concourse.bass as bass
import concourse.mybir as mybir
from concourse.bass_utils import run_bass_kernel_spmd

F32 = mybir.dt.float32
BF16 = mybir.dt.bfloat16
U8 = mybir.dt.uint8
AF = mybir.ActivationFunctionType
ALU = mybir.AluOpType

D = 1024
KC = 8
DI = 2048
CD = 3072
NH = 32
AH = 16
DIN = 11296
C_Z, C_X, C_DT, C_Q, C_K, C_V, C_G, C_GS, C_GA = 0, 2048, 5120, 5152, 6176, 7200, 8224, 9248, 10272
EPS = 1e-5
NEG = -30000.0


def _dsz(dt):
    if dt == F32:
        return 4
    if dt == BF16:
        return 2
    if dt == U8:
        return 1
    raise ValueError(dt)


class _Op:
    __slots__ = ('eng', 'fn', 'deps', 'dma', 'key', 'val', 'seq', 'hasdep')


class Sched:
    def __init__(self, nc, es):
        self.nc = nc
        self.es = es
        self.ops = []
        self.engobj = {'pe': nc.tensor, 'act': nc.scalar, 'dve': nc.vector, 'pool': nc.gpsimd, 'sp': nc.sync}
        self.esem = {k: es.enter_context(nc.semaphore('es_' + k)) for k in self.engobj}
        self.dkeys = {}
        self.tracked = {}
        self.acc = {}

    def track(self, name, const=False):
        self.tracked[name] = 'const' if const else True

    def reg(self, ap):
        name = ap.tensor.name
        tr = self.tracked.get(name)
        if tr is None:
            return None
        dims = ap.ap
        ps = dims[0][0]
        npart = dims[0][1]
        off = ap.offset
        if ps > 0:
            p0 = off // ps
            f0 = off % ps
        else:
            p0 = 0
            f0 = off
        lo = f0
        hi = f0
        for (s, c) in dims[1:]:
            if s >= 0:
                hi += s * (c - 1)
            else:
                lo += s * (c - 1)
        sz = _dsz(ap.dtype)
        return (name, p0, p0 + npart, lo * sz, (hi + 1) * sz)

    def add(self, eng, fn, reads=(), writes=(), dma_key=None):
        i = len(self.ops)
        deps = set()
        rr = []
        ww = []
        for a in reads:
            r = a if isinstance(a, tuple) else self.reg(a)
            if r is not None:
                if r[0].startswith('pb'):
                    ww.append((r[0], 0, 128, 0, 2048))
                else:
                    rr.append(r)
        for a in writes:
            r = a if isinstance(a, tuple) else self.reg(a)
            if r is not None:
                if r[0].startswith('pb'):
                    r = (r[0], 0, 128, 0, 2048)
                ww.append(r)
        for (k, p0, p1, lo, hi) in rr:
            for e in self.acc.get(k, ()):
                if e[5] and e[0] < p1 and p0 < e[1] and e[2] < hi and lo < e[3]:
                    deps.add(e[4])
        for (k, p0, p1, lo, hi) in ww:
            lst = self.acc.get(k)
            if lst is None:
                continue
            keep = []
            for e in lst:
                if e[0] < p1 and p0 < e[1] and e[2] < hi and lo < e[3]:
                    deps.add(e[4])
                    if p0 <= e[0] and e[1] <= p1 and lo <= e[2] and e[3] <= hi:
                        continue
                keep.append(e)
            self.acc[k] = keep
        deps.discard(i)
        for (k, p0, p1, lo, hi) in rr:
            if self.tracked.get(k) == 'const':
                continue
            self.acc.setdefault(k, []).append([p0, p1, lo, hi, i, False])
        for (k, p0, p1, lo, hi) in ww:
            self.acc.setdefault(k, []).append([p0, p1, lo, hi, i, True])
        op = _Op()
        op.eng = eng
        op.fn = fn
        op.deps = deps
        op.dma = dma_key is not None
        op.key = dma_key
        op.val = 0
        op.seq = 0
        op.hasdep = False
        if op.dma:
            ent = self.dkeys.get(dma_key)
            if ent is None:
                ent = [self.es.enter_context(self.nc.semaphore('d_%d' % len(self.dkeys))), 0, []]
                self.dkeys[dma_key] = ent
            ent[1] += 16
            op.val = ent[1]
            ent[2].append(i)
        self.ops.append(op)
        return i

    def group_barrier(self, key):
        ent = self.dkeys[key]
        for i in ent[2]:
            self.ops[i].val = ent[1]

    def group_last(self, key, n):
        ent = self.dkeys[key]
        for i in ent[2][-n:]:
            self.ops[i].val = ent[1]

    def emit(self):
        ops = self.ops
        for op in ops:
            for d in op.deps:
                ops[d].hasdep = True
        cnt = {k: 0 for k in self.engobj}
        for op in ops:
            if not op.dma and op.hasdep:
                cnt[op.eng] += 1
                op.seq = cnt[op.eng]
        waited = {k: {} for k in self.engobj}
        nw = 0
        for op in ops:
            eng = self.engobj[op.eng]
            wd = waited[op.eng]
            need = {}
            for d in op.deps:
                dop = ops[d]
                if dop.dma:
                    sem, val = self.dkeys[dop.key][0], dop.val
                else:
                    if dop.eng == 'pe' and op.eng == 'pe' and not op.dma:
                        continue
                    sem, val = self.esem[dop.eng], dop.seq
                k = id(sem)
                if need.get(k, (None, 0))[1] < val:
                    need[k] = (sem, val)
            for k, (sem, val) in need.items():
                if wd.get(k, 0) >= val:
                    continue
                eng.wait_ge(sem, val)
                wd[k] = val
                nw += 1
            inst = op.fn(eng)
            if op.dma:
                inst.then_inc(self.dkeys[op.key][0], 16)
            elif op.hasdep:
                inst.then_inc(self.esem[op.eng], 1)
        sp = self.engobj['sp']
        for key, ent in self.dkeys.items():
            sp.wait_ge(ent[0], ent[1])
        return len(ops), nw

    def mms(self, items):
        reads = []
        writes = []
        for it in items:
            reads += [it['lhsT'], it['rhs']]
            writes.append(it['out'])
            if not it['start']:
                reads.append(it['out'])

        def fn(e, items=items):
            inst = None
            for it in items:
                inst = e.matmul(it['out'], lhsT=it['lhsT'], rhs=it['rhs'], start=it['start'], stop=it['stop'],
                                skip_group_check=True)
            return inst
        return self.add('pe', fn, reads, writes)

    def trs(self, items, ident):
        reads = [ident] + [x[1] for x in items]
        writes = [x[0] for x in items]

        def fn(e, items=items, ident=ident):
            inst = None
            for (o, i) in items:
                inst = e.transpose(o, i, ident)
            return inst
        return self.add('pe', fn, reads, writes)

    def act(self, out, in_, func, bias=None, scale=None, accum_out=None, extra_w=()):
        reads = [in_]
        kw = {}
        if bias is not None:
            kw['bias'] = bias
            if not isinstance(bias, (int, float)):
                reads.append(bias)
        if scale is not None:
            kw['scale'] = scale
            if not isinstance(scale, (int, float)):
                reads.append(scale)
        writes = [out] + list(extra_w)
        if accum_out is not None:
            kw['accum_out'] = accum_out
            writes.append(accum_out)
        return self.add('act', lambda e: e.activation(out=out, in_=in_, func=func, **kw), reads, writes)

    def tt(self, eng, out, in0, in1, op):
        return self.add(eng, lambda e: e.tensor_tensor(out=out, in0=in0, in1=in1, op=op), [in0, in1], [out])

    def ts(self, eng, out, in0, s1, s2, op0, op1=None):
        reads = [in0]
        if not isinstance(s1, (int, float)):
            reads.append(s1)
        if s2 is not None and not isinstance(s2, (int, float)):
            reads.append(s2)
        if op1 is None:
            return self.add(eng, lambda e: e.tensor_scalar(out=out, in0=in0, scalar1=s1, scalar2=None, op0=op0),
                            reads, [out])
        return self.add(eng, lambda e: e.tensor_scalar(out=out, in0=in0, scalar1=s1, scalar2=s2, op0=op0, op1=op1),
                        reads, [out])

    def stt(self, eng, out, in0, scalar, in1, op0, op1):
        reads = [in0, in1]
        if not isinstance(scalar, (int, float)):
            reads.append(scalar)
        return self.add(eng, lambda e: e.scalar_tensor_tensor(out=out, in0=in0, scalar=scalar, in1=in1, op0=op0,
                                                              op1=op1), reads, [out])

    def copy(self, eng, out, in_):
        if eng == 'act':
            return self.add('act', lambda e: e.activation(out=out, in_=in_, func=AF.Identity), [in_], [out])
        return self.add(eng, lambda e: e.tensor_copy(out=out, in_=in_), [in_], [out])

    def memset(self, eng, ap, val):
        return self.add(eng, lambda e: e.memset(ap, val), [], [ap])

    def recip(self, out, in_):
        return self.add('dve', lambda e: e.reciprocal(out=out, in_=in_), [in_], [out])

    def dma(self, q, out, in_, key, extra_r=(), extra_w=(), slow=False):
        if slow:
            fn = lambda e: e.dma_start(out=out, in_=in_, allow_slow_non_contiguous=True)
        else:
            fn = lambda e: e.dma_start(out=out, in_=in_)
        return self.add(q, fn, [in_] + list(extra_r), [out] + list(extra_w), dma_key=key)


def bc(ap, dims, off=0):
    return bass.AP(ap.tensor, ap.offset + off, dims)


def pstr(ap):
    return ap.ap[0][0]


class Arena:
    def __init__(self, nc, es, S, name, nbytes):
        self.t = es.enter_context(nc.sbuf_tensor(name, [128, nbytes], U8))
        self.off = 0
        self.cap = nbytes
        self.name = name
        S.track(name)

    def alloc(self, free_shape, dt):
        n = int(np.prod(free_shape)) * _dsz(dt)
        off = (self.off + 31) // 32 * 32
        assert off + n <= self.cap, (self.name, off, n, self.cap)
        self.off = off + n
        v = self.t[:, off:off + n].bitcast(dt)
        if len(free_shape) == 2:
            v = v.rearrange("p (a b) -> p a b", a=free_shape[0])
        elif len(free_shape) == 3:
            v = v.rearrange("p (a b c) -> p a b c", a=free_shape[0], b=free_shape[1])
        return v


_STOP = (99, 0)
_NCORES = 8
_DBG = ''
_ENV = {}
_DUMP = None


class _StopBuild(Exception):
    pass


def build(nc, SEQ, dbg=None):
    es = ExitStack()
    S = Sched(nc, es)
    try:
        _body(nc, SEQ, es, S, dbg)
    except _StopBuild:
        pass
    if _DUMP is not None:
        _DUMP(S, _ENV)
    nops, nw = S.emit()
    es.close()
    return nops, nw


def _body(nc, SEQ, es, S, dbg=None):
    NTP = SEQ // 256
    KEEP = min(512, SEQ)

    def din(name, shape):
        return nc.dram_tensor(name, shape, F32, kind="ExternalInput").ap()

    def dout(name, shape):
        return nc.dram_tensor(name, shape, F32, kind="ExternalOutput").ap()

    x_p = din("x_p", [SEQ, D])
    x_s = din("x_s", [256, D])
    st_ssm = din("st_ssm", [4, 2048, 128])
    st_conv = din("st_conv", [4, 3, CD])
    ck = din("ck", [4, 512, D])
    cv = din("cv", [4, 512, D])
    norm_w = din("norm_w", [D])
    w_in = din("w_in", [D, DIN])
    conv_w = din("conv_w", [4, CD])
    conv_b = din("conv_b", [CD])
    dt_bias = din("dt_bias", [NH])
    a_log = din("a_log", [NH])
    d_skip = din("d_skip", [NH])
    ssm_norm_w = din("ssm_norm_w", [DI])
    w_out_ssm = din("w_out_ssm", [DI, D])
    rel_bias = din("rel_bias", [AH, 513])
    w_out_att = din("w_out_att", [D, D])
    w_o = din("w_o", [D, D])
    final_norm_w = din("final_norm_w", [D])

    y_p = dout("y_p", [SEQ, D])
    y_s = dout("y_s", [256, D])
    ssm_p = dout("ssm_p", [2048, 128])
    conv_p = dout("conv_p", [3, CD])
    k_p = dout("k_p", [KEEP, D])
    v_p = dout("v_p", [KEEP, D])
    ssm_s = dout("ssm_s", [4, 2048, 128])
    conv_s = dout("conv_s", [4, 3, CD])
    k_s = dout("k_s", [4, 512, D])
    v_s = dout("v_s", [4, 512, D])

    WB = {}
    blist = []

    def addblk(name, src, nrow_chunks, c0, ncols, N):
        WB[name] = dict(idx=len(blist), src=src, kc=nrow_chunks, c0=c0, ncols=ncols, N=N)
        blist.append(name)

    for i in range(2):
        addblk('q%d' % i, w_in, 8, C_Q + 512 * i, 512, DIN)
    for i in range(2):
        addblk('k%d' % i, w_in, 8, C_K + 512 * i, 512, DIN)
    for i in range(2):
        addblk('v%d' % i, w_in, 8, C_V + 512 * i, 512, DIN)
    for i in range(2):
        addblk('g%d' % i, w_in, 8, C_G + 512 * i, 512, DIN)
    for i in range(2):
        addblk('ga%d' % i, w_in, 8, C_GA + 512 * i, 512, DIN)
    for i in range(2):
        addblk('wa%d' % i, w_out_att, 8, 512 * i, 512, D)
    addblk('dt', w_in, 8, C_DT, 32, DIN)
    for i in range(6):
        addblk('x%d' % i, w_in, 8, C_X + 512 * i, 512, DIN)
    for i in range(4):
        addblk('z%d' % i, w_in, 8, C_Z + 512 * i, 512, DIN)
    for i in range(2):
        addblk('gs%d' % i, w_in, 8, C_GS + 512 * i, 512, DIN)
    for i in range(4):
        addblk('ws%d' % i, w_out_ssm, 16, 256 * i, 256, D)
    for i in range(2):
        addblk('wo%d' % i, w_o, 8, 512 * i, 512, D)
    NB = len(blist)
    wscr = nc.dram_tensor("wscr", [NB, 128, 4096], BF16, kind="Internal").ap()
    ext = nc.dram_tensor("ext_bias", [AH, 1024], F32, kind="Internal").ap()

    def wscr_view(name):
        b = WB[name]
        i = b['idx']
        kc, nc_ = b['kc'], b['ncols']
        base = wscr[i]
        return bc(base, [[4096, 128], [nc_, kc], [1, nc_]])

    NGRP = 4
    for n, name in enumerate(blist):
        b = WB[name]
        src = bass.AP(b['src'].tensor, b['c0'], [[b['N'], 128], [128 * b['N'], b['kc']], [1, b['ncols']]])
        grp = min(NGRP - 1, n * NGRP // NB)
        S.dma('pool', wscr_view(name), src, 'cv%d' % grp, extra_w=[('wscr', b['idx'], b['idx'] + 1, 0, 1)])
        b['grp'] = grp
    for g in range(NGRP):
        S.group_barrier('cv%d' % g)

    def cstop(n):
        if _STOP == (n, 0):
            raise _StopBuild()
    cstop(-1)

    CA = Arena(nc, es, S, "cst", 29 * 1024)
    S.tracked["cst"] = True
    A = Arena(nc, es, S, "main", 183040)

    identb = CA.alloc([128], BF16)
    identf = CA.alloc([128], F32)
    Jx = CA.alloc([128], BF16)
    tri = CA.alloc([128], BF16)
    SU = CA.alloc([128], BF16)
    onesb = CA.alloc([128], BF16)
    scr32 = CA.alloc([128], F32)
    cstg = CA.alloc([128], F32)
    cpar = CA.alloc([144], F32)
    normw = cpar[:, 0:8]
    nwssm = cpar[:, 8:24]
    convb = cpar[:, 24:48]
    convw = cpar[:, 48:144].rearrange("p (j c) -> p j c", j=4)
    dtb_bc = CA.alloc([32], F32)
    a_bc = CA.alloc([32], F32)
    dsk_bc = CA.alloc([32], F32)
    fnw_bc = CA.alloc([1024], F32)
    neghalf = CA.alloc([4], F32)
    biasF = CA.alloc([AH, 5, 128], BF16)
    convst = CA.alloc([4, 3, 24], BF16)

    xt = [A.alloc([1024], F32) for _ in range(4)]
    xsn = A.alloc([1024], BF16)
    TH = [A.alloc([512], BF16) for _ in range(3)]
    junk = bc(TH[0], [[pstr(TH[0]), 128], [1, 1024]])
    xnT = A.alloc([8, 256], BF16)
    NSLOT = 3
    Wr = [A.alloc([4096], BF16) for _ in range(NSLOT)]
    KT = A.alloc([8, 6, 128], BF16)
    Vr = A.alloc([6, AH, 66], BF16)
    hT = A.alloc([2048], F32)
    hTb = A.alloc([2048], BF16)
    ost = A.alloc([1024], F32)
    ogT = A.alloc([8, 256], BF16)
    ga_sig = A.alloc([8, 256], BF16)
    gs_sig = A.alloc([8, 256], BF16)
    accT = A.alloc([8, 256], BF16)
    yzT = A.alloc([16, 256], BF16)
    mT = A.alloc([8, 256], BF16)
    smallf = A.alloc([64], F32)
    carry = A.alloc([24, 3], BF16)
    base = A.off
    QT = A.alloc([8, 256], BF16)
    PT = [A.alloc([512], BF16) for _ in range(3)]
    rs = A.alloc([16], F32)
    o_n = A.alloc([1024], BF16)
    og = A.alloc([1024], BF16)
    gs2 = A.alloc([2, 1024], BF16)
    kvst = [A.alloc([512], F32) for _ in range(2)]
    ckst = A.alloc([4, 1024], BF16)
    KTs = A.alloc([8, 2, 64], BF16)
    Vs = A.alloc([2, AH, 66], BF16)
    att_end = A.off
    A.off = base
    zs2 = A.alloc([2, 2048], BF16)
    XW = 3 + 256
    xraw = A.alloc([24, XW], BF16)
    cacc = [A.alloc([256], F32) for _ in range(2)]
    xs_tok = A.alloc([2048], BF16)
    B_tok = A.alloc([512], BF16)
    dtt = A.alloc([2, 32], F32)
    adt = A.alloc([2, 32], F32)
    adt_hi = A.alloc([2, 32], BF16)
    adt_lo = A.alloc([2, 32], BF16)
    dtmp = A.alloc([32], F32)
    ee = A.alloc([64], F32)
    ee2 = A.alloc([32], F32)
    xdtb = [A.alloc([512], BF16) for _ in range(2)]
    xdt2b = [A.alloc([512], BF16) for _ in range(2)]
    xsdb = [A.alloc([512], BF16) for _ in range(2)]
    cbTm = A.alloc([4, 128], BF16)
    Bmh = A.alloc([8, 128], BF16)
    Bml = A.alloc([8, 128], BF16)
    Eb = A.alloc([8, 128], BF16)
    MTb = A.alloc([8, 128], BF16)
    yz = A.alloc([4, 512], BF16)
    ssq = A.alloc([4], F32)
    rstd4 = A.alloc([4], F32)
    xbcst = [A.alloc([512], F32) for _ in range(2)]
    stT = A.alloc([16, 128], F32)
    ssd_end = A.off
    A.off = max(att_end, ssd_end)
    print('SBUF use: cst', CA.off, 'main', A.off, 'att', att_end - base, 'ssd', ssd_end - base)

    _ENV.update(dict(accT=accT, mT=mT, yzT=yzT, ogT=ogT, gs_sig=gs_sig, ga_sig=ga_sig, xnT=xnT, QT=QT, zs2=zs2,
                     yz=yz, ost=ost, gs2=gs2, o_n=o_n, og=og, hT=hT, xraw=xraw, nc=nc, A=A, xt=xt))
    PB = []
    for i in range(8):
        t = es.enter_context(nc.psum_tensor("pb%d" % i, [128, 2048], U8))
        S.track("pb%d" % i)
        PB.append(t)

    def pf(i):
        return PB[i][:, :].bitcast(F32)

    def pbf(i):
        return PB[i][:, :].bitcast(BF16)

    rot = {'mm': [0, 1, 3, 4], 'tp': [2, 0, 1], 'S': [3, 4]}
    rotp = {k: 0 for k in rot}

    def nb(role):
        i = rot[role][rotp[role] % len(rot[role])]
        rotp[role] += 1
        return i
    OB = [5, 6, 7]
    SEGB = [3, 4]
    YB, UB, CBB, SMB = 5, 7, 6, 2

    P = 'pool'
    S.memset(P, scr32, 0.0)
    S.add(P, lambda e: e.affine_select(out=scr32, in_=scr32, pattern=[[-1, 128]], compare_op=ALU.not_equal, fill=1.0,
                                       base=0, channel_multiplier=1), [scr32], [scr32])
    S.copy('dve', identb, scr32)
    S.copy('dve', identf, scr32)
    S.memset(P, scr32, 0.0)
    S.add(P, lambda e: e.affine_select(out=scr32, in_=scr32, pattern=[[1, 128]], compare_op=ALU.not_equal, fill=1.0,
                                       base=-127, channel_multiplier=1), [scr32], [scr32])
    S.copy('dve', Jx, scr32)
    S.memset(P, scr32, 1.0)
    S.copy('dve', onesb, scr32)
    S.add(P, lambda e: e.affine_select(out=scr32, in_=scr32, pattern=[[1, 128]], compare_op=ALU.is_ge, fill=0.0,
                                       base=0, channel_multiplier=-1), [scr32], [scr32])
    S.copy('dve', tri, scr32)
    S.memset(P, scr32, 1.0)
    S.add(P, lambda e: e.affine_select(out=scr32, in_=scr32, pattern=[[-1, 128]], compare_op=ALU.is_ge, fill=0.0,
                                       base=-1, channel_multiplier=1), [scr32], [scr32])
    S.copy('dve', SU, scr32)
    S.memset(P, neghalf, -0.5)
    S.memset(P, hT, 0.0)
    S.memset(P, hTb, 0.0)
    S.memset(P, carry, 0.0)
    S.memset(P, Vr[:, :, :, 64:65], 1.0)

    cstop(-2)
    CK = 'const'
    S.dma('sp', dtb_bc, bass.AP(dt_bias.tensor, 0, [[0, 128], [1, 32]]), CK)
    S.dma('sp', a_bc, bass.AP(a_log.tensor, 0, [[0, 128], [1, 32]]), CK)
    S.dma('sp', dsk_bc, bass.AP(d_skip.tensor, 0, [[0, 128], [1, 32]]), CK)
    S.dma('sp', fnw_bc, bass.AP(final_norm_w.tensor, 0, [[0, 128], [1, 1024]]), CK)
    S.group_barrier(CK)
    cstop(-3)
    S.dma('sp', cstg[0:8, :], bass.AP(norm_w.tensor, 0, [[128, 8], [1, 128]]), 'cstg')
    S.dma('sp', cstg[8:24, :], bass.AP(ssm_norm_w.tensor, 0, [[128, 16], [1, 128]]), 'cstg')
    S.dma('sp', cstg[24:48, :], bass.AP(conv_b.tensor, 0, [[128, 24], [1, 128]]), 'cstg')
    S.group_last('cstg', 3)
    S.trs([(pf(0)[:, 0:48], cstg[0:48, :])], identf[0:48, 0:48])
    S.copy('dve', cpar[:, 0:48], pf(0)[:, 0:48])
    S.dma('sp', cstg[0:96, :], bass.AP(conv_w.tensor, 0, [[128, 96], [1, 128]]), 'cstg')
    S.trs([(pf(0)[:, 0:96], cstg[0:96, :])], identf[0:96, 0:96])
    S.copy('dve', cpar[:, 48:144], pf(0)[:, 0:96])
    for c3 in range(3):
        S.dma('sp', cstg[0:96, :], bass.AP(st_conv.tensor, c3 * 96 * 128, [[128, 96], [1, 128]]), 'cstg')
        S.trs([(pf(0)[:, 0:96], cstg[0:96, :])], identf[0:96, 0:96])
        cf = convst.rearrange("p s j c -> p (s j c)")
        S.copy('dve', cf[:, c3 * 96:(c3 + 1) * 96], pf(0)[:, 0:96])
    cstop(-4)
    EXT = ('ext', 0, 1, 0, 1)
    S.dma('sp', xt[2][0:16, 0:513], rel_bias, 'xt2')
    S.copy('dve', xt[1][0:16, 0:513], xt[2][0:16, 0:513])
    rb512 = xt[2][0:16, 512:513]
    S.copy('dve', xt[1][0:16, 513:1024], bc(rb512, [[rb512.ap[0][0], 16], [0, 511]]))
    S.dma('sp', ext, xt[1][0:16, 0:1024], 'xt1', extra_w=[EXT])
    stgv = bc(xt[2], [[pstr(xt[2]), 128], [128, AH], [1, 128]])
    for t in range(5):
        src = bass.AP(ext.tensor, 641 - 128 * t, [[1, 128], [1024, AH], [1, 128]])
        S.dma('sp', stgv, src, 'xt2', extra_r=[EXT])
        S.copy('dve', biasF[:, :, t, :], stgv)
    S.memset(P, biasF[64:128, :, 0, 64:128], NEG)
    S.memset(P, biasF[0:64, :, 4, 0:64], NEG)
    cstop(-5)
    S.act(a_bc, a_bc, AF.Exp)
    S.ts('dve', a_bc, a_bc, -1.0, None, ALU.mult)
    S.ts('dve', convw, convw, 0.5, None, ALU.mult)
    S.ts('dve', convb, convb, 0.5, None, ALU.mult)

    S.tracked['cst'] = 'const'

    uses = []
    tiles = [('p', i) for i in range(NTP)] + [('s', 0), ('s', 1)]
    per_tile = ['q0', 'q1', 'k0', 'k1', 'v0', 'v1', 'g0', 'g1', 'ga0', 'ga1', 'wa0', 'wa1', 'dt',
                'x0', 'x1', 'x2', 'x3', 'x4', 'x5', 'z0', 'z1', 'z2', 'z3', 'gs0', 'gs1',
                'ws0', 'ws1', 'ws2', 'ws3', 'wo0', 'wo1']
    for _ in tiles:
        uses += per_tile
    wstate = {'issued': 0, 'cur': 0}

    def w_issue(upto):
        while wstate['issued'] < min(upto, len(uses)):
            u = wstate['issued']
            name = uses[u]
            b = WB[name]
            slot = u % NSLOT
            n = b['kc'] * b['ncols']
            dst = bc(Wr[slot], [[pstr(Wr[slot]), 128], [1, n]])
            srcv = bc(wscr[b['idx']], [[4096, 128], [1, n]])
            S.dma('sp', dst, srcv, 'W%d' % slot, extra_r=[('wscr', b['idx'], b['idx'] + 1, 0, 1)])
            wstate['issued'] += 1

    def w_use(name, issue=True):
        u = wstate['cur']
        assert uses[u] == name, (uses[u], name)
        if issue:
            w_issue(u + NSLOT)
        wstate['cur'] += 1
        b = WB[name]
        slot = u % NSLOT
        return bc(Wr[slot], [[pstr(Wr[slot]), 128], [b['ncols'], b['kc']], [1, b['ncols']]])

    def xrows(kind, idx, sub, L):
        if kind == 'p':
            r0 = idx * 256 + sub * 128
            return x_p[r0:r0 + L, :], y_p[r0:r0 + L, :]
        seq = idx * 2 + sub
        return x_s[seq * 64:seq * 64 + 64, :], y_s[seq * 64:seq * 64 + 64, :]

    def load_x(ti):
        kind, idx = tiles[ti]
        L = 128 if kind == 'p' else 64
        for sub in range(2):
            b = (ti % 2) * 2 + sub
            src, _ = xrows(kind, idx, sub, L)
            S.dma('sp', xt[b][0:L, :], src, 'xt%d' % b)

    def rms_rstd(src_sum, n, eps, dst, L, col=1):
        S.ts('dve', dst, src_sum, 1.0 / n, eps, ALU.mult, ALU.add)
        S.tt('pool', dst, dst, neghalf[0:L, 0:col], ALU.pow)

    def fm_block(wv, ncols_tiles, T, evac):
        for et in range(ncols_tiles):
            bk = nb('mm')
            out = pf(bk)[:, 0:T]
            S.mms([dict(out=out, lhsT=wv[:, kc, et * 128:(et + 1) * 128], rhs=xnT[:, kc, 0:T],
                        start=(kc == 0), stop=(kc == 7)) for kc in range(8)])
            evac(et, out)

    def tm_block(wv, ncols, L, subs, evac):
        for sub in subs:
            bk = nb('mm')
            out = pf(bk)[0:L, 0:ncols]
            S.mms([dict(out=out, lhsT=xnT[:, kc, sub * L:(sub + 1) * L], rhs=wv[:, kc, 0:ncols],
                        start=(kc == 0), stop=(kc == 7)) for kc in range(8)])
            evac(sub, out)

    thp = [0]

    def nth():
        thp[0] += 1
        return TH[thp[0] % 3]

    def store_state(dst_ap):
        for q in range(4):
            bk = nb('mm')
            S.trs([(pf(bk)[:, j * 128:(j + 1) * 128], hT[:, (q * 4 + j) * 128:(q * 4 + j + 1) * 128])
                   for j in range(4)], identf)
            S.copy('dve', stT[:, q * 4:(q + 1) * 4, :], pf(bk)[:, 0:512].rearrange("p (a b) -> p a b", a=4))
        S.dma('pool', bass.AP(dst_ap.tensor, dst_ap.offset, [[128, 128], [128 * 128, 16], [1, 128]]), stT, 'stT')

    def load_state(seq):
        src = st_ssm[seq]
        if 'nold' not in _DBG:
            S.dma('pool' if 'poolld' in _DBG else 'sp', stT,
                  bass.AP(src.tensor, src.offset, [[128, 128], [128 * 128, 16], [1, 128]]), 'stT')
        if 'notr' in _DBG:
            return
        for q in range(4):
            bk = nb('mm')
            S.trs([(pf(bk)[:, j * 128:(j + 1) * 128], stT[:, q * 4 + j, :]) for j in range(4)], identf)
            if 'nocp' in _DBG:
                continue
            S.copy('dve', hT[:, q * 512:(q + 1) * 512], pf(bk)[:, 0:512])
            if 'noact' in _DBG:
                continue
            S.copy('act', hTb[:, q * 512:(q + 1) * 512], pf(bk)[:, 0:512])

    def stop(n, ti):
        if _STOP == (n, ti):
            raise _StopBuild()

    load_x(0)
    for ti, (kind, idx) in enumerate(tiles):
        stop(0, ti)
        L = 128 if kind == 'p' else 64
        T = 2 * L
        last_p = (kind == 'p' and idx == NTP - 1)
        def kv_out_row(sub):
            if kind == 'p':
                r0 = idx * 256 + sub * 128
                if r0 >= SEQ - KEEP:
                    return r0 - (SEQ - KEEP)
                return None
            return 448
        if ti + 1 < len(tiles):
            load_x(ti + 1)
        xb = [xt[(ti % 2) * 2 + sub] for sub in range(2)]

        for sub in range(2):
            ssx = smallf[0:L, 0:1]
            S.act(junk[0:L, :], xb[sub][0:L, :], AF.Square, accum_out=ssx)
            rms_rstd(ssx, 1024.0, EPS, smallf[0:L, 1:2], L)
            S.act(xsn[0:L, :], xb[sub][0:L, :], AF.Identity, scale=smallf[0:L, 1:2])
            bk = nb('tp')
            S.trs([(pbf(bk)[:, kc * L:(kc + 1) * L], xsn[0:L, kc * 128:(kc + 1) * 128]) for kc in range(8)],
                  identb[0:L, 0:L])
            S.tt('dve', xnT[:, :, sub * L:(sub + 1) * L], pbf(bk)[:, 0:8 * L].rearrange("p (a b) -> p a b", a=8),
                 bc(normw, [[pstr(normw), 128], [1, 8], [0, L]]), ALU.mult)

        stop(1, ti)
        if kind == 'p':
            g0 = idx * 2
            slot0 = g0 % 6
        for qi in range(2):
            wv = w_use('q%d' % qi)

            def ev(et, out, qi=qi):
                S.act(QT[:, qi * 4 + et, 0:T], out, AF.Identity, scale=0.125)
            fm_block(wv, 4, T, ev)
        stop(1.2, ti)
        for ki in range(2):
            wv = w_use('k%d' % ki)

            def ev(et, out, ki=ki):
                pair = ki * 4 + et
                if kind == 'p':
                    S.copy('act', KT[:, pair, slot0:slot0 + 2, :], out.rearrange("p (s l) -> p s l", s=2))
                else:
                    S.copy('dve', KTs[:, pair, :, :], out.rearrange("p (s l) -> p s l", s=2))
            fm_block(wv, 4, T, ev)
            subs = [s for s in range(2) if kv_out_row(s) is not None]

            def evk(sub, out, ki=ki):
                st = kvst[(ki + sub) % 2]
                S.copy('dve', st[0:L, :], out)
                r = kv_out_row(sub)
                if kind == 'p':
                    S.dma('pool', k_p[r:r + L, ki * 512:(ki + 1) * 512], st[0:L, :], 'kvst%d' % ((ki + sub) % 2))
                else:
                    seq = idx * 2 + sub
                    S.dma('pool', k_s[seq, 448:512, ki * 512:(ki + 1) * 512], st[0:L, :],
                          'kvst%d' % ((ki + sub) % 2))
            if subs:
                tm_block(wv, 512, L, subs, evk)
        stop(1.4, ti)
        if kind == 's':
            S.memset('pool', Vs[:, :, :, 64:65], 1.0)
        for vi in range(2):
            wv = w_use('v%d' % vi)

            def evv(sub, out, vi=vi):
                if kind == 'p':
                    dst = Vr[0:L, slot0 + sub, vi * 8:(vi + 1) * 8, 0:64]
                else:
                    dst = Vs[0:L, sub, vi * 8:(vi + 1) * 8, 0:64]
                S.copy('act', dst, out.rearrange("p (h d) -> p h d", h=8))
                r = kv_out_row(sub)
                if r is not None:
                    st = kvst[(vi + sub) % 2]
                    S.copy('dve', st[0:L, :], out)
                    if kind == 'p':
                        S.dma('pool', v_p[r:r + L, vi * 512:(vi + 1) * 512], st[0:L, :],
                              'kvst%d' % ((vi + sub) % 2))
                    else:
                        seq = idx * 2 + sub
                        S.dma('pool', v_s[seq, 448:512, vi * 512:(vi + 1) * 512], st[0:L, :],
                              'kvst%d' % ((vi + sub) % 2))
            tm_block(wv, 512, L, [0, 1], evv)
        stop(1.6, ti)
        for gi in range(2):
            wv = w_use('g%d' % gi)

            def evg(sub, out, gi=gi):
                th = nth()
                S.act(th[0:L, :], out, AF.Tanh, scale=0.5)
                S.stt('dve', gs2[0:L, sub, gi * 512:(gi + 1) * 512], th[0:L, :], 1.0, out, ALU.add, ALU.mult)
            tm_block(wv, 512, L, [0, 1], evg)
        stop(1.8, ti)
        for gi in range(2):
            wv = w_use('ga%d' % gi)

            def evga(et, out, gi=gi):
                th = nth()
                S.act(th[:, 0:T], out, AF.Tanh, scale=0.5)
                S.act(ga_sig[:, gi * 4 + et, 0:T], th[:, 0:T], AF.Identity, scale=0.25, bias=0.25)
            fm_block(wv, 4, T, evga)

        stop(2, ti)
        for sub in range(2):
            if kind == 'p':
                g = idx * 2 + sub
                tl = [t for t in range(5) if g - 4 + t >= 0]
            else:
                seq = idx * 2 + sub
                tl = [0, 1, 2, 3] if 'not4' in _DBG else ([4] if 'only4' in _DBG else [0, 1, 2, 3, 4])
                kv32 = bc(kvst[0], [[pstr(kvst[0]), 128], [1, 1024]])
                for tt_ in range(4):
                    S.dma('sp', kv32, cv[seq, tt_ * 128:(tt_ + 1) * 128, :], 'kvst0')
                    S.copy('dve', Vr[:, tt_, :, 0:64], kv32.rearrange("p (h d) -> p h d", h=AH))
                    S.dma('sp', ost, ck[seq, tt_ * 128:(tt_ + 1) * 128, :], 'ost')
                    S.copy('act', ckst[:, tt_, :], ost)
                    bk = nb('tp')
                    S.trs([(pbf(bk)[:, pr * 128:(pr + 1) * 128], ckst[:, tt_, pr * 128:(pr + 1) * 128])
                           for pr in range(8)], identb)
                    S.copy('dve', KT[:, :, tt_, :], pbf(bk)[:, 0:1024].rearrange("p (a b) -> p a b", a=8))
                S.dma('pool', k_s[seq, 0:448, :], ck[seq, 64:512, :], 'd2d')
                S.dma('pool', v_s[seq, 0:448, :], cv[seq, 64:512, :], 'd2d')
            stop(2.2, ti)
            for ob in OB:
                S.memset('dve', pf(ob)[0:L, :], 0.0)
            blocks = [(h, t) for h in range(AH) for t in tl]
            per = 512 // L
            groups = [blocks[i:i + per] for i in range(0, len(blocks), per)]
            pend = None

            def emit_pv(pg):
                grp, pt = pg
                if 'nopv' in _DBG:
                    return
                items = []
                for cb, (h, t) in enumerate(grp):
                    Mk = 64 if (kind == 's' and t == 4) else 128
                    if kind == 'p':
                        vt = Vr[0:Mk, (g - 4 + t) % 6, h, 0:65]
                    elif t < 4:
                        vt = Vr[0:Mk, t, h, 0:65]
                    else:
                        vt = Vs[0:Mk, sub, h, 0:65]
                    ob = OB[h // 6]
                    oc = (h % 6) * 65
                    items.append(dict(out=pf(ob)[0:L, oc:oc + 65], lhsT=pt[0:Mk, cb * L:(cb + 1) * L], rhs=vt,
                                      start=False, stop=True))
                S.mms(items)
            for gi_, grp in enumerate(groups):
                sbk = nb('S')
                items = []
                for cb, (h, t) in enumerate(grp):
                    pair, pb_ = h // 2, (h % 2) * 64
                    Mk = 64 if (kind == 's' and t == 4) else 128
                    out = pf(sbk)[0:Mk, cb * L:(cb + 1) * L]
                    if Mk == 128:
                        items.append(dict(out=out, lhsT=Jx, rhs=biasF[:, h, t, 0:L], start=True, stop=False))
                    else:
                        items.append(dict(out=out, lhsT=Jx[:, 0:64], rhs=biasF[:, h, t, 0:L],
                                          start=True, stop=False))
                    if kind == 'p':
                        kt = KT[pb_:pb_ + 64, pair, (g - 4 + t) % 6, 0:Mk]
                    elif t < 4:
                        kt = KT[pb_:pb_ + 64, pair, t, 0:Mk]
                    else:
                        kt = KTs[pb_:pb_ + 64, pair, sub, 0:Mk]
                    items.append(dict(out=out, lhsT=kt, rhs=QT[pb_:pb_ + 64, pair, sub * L:(sub + 1) * L],
                                      start=False, stop=True))
                S.mms(items)
                pt = PT[gi_ % 3]
                ncol = len(grp) * L
                S.act(pt[:, 0:ncol], pf(sbk)[:, 0:ncol], AF.Exp)
                if pend is not None:
                    emit_pv(pend)
                pend = (grp, pt)
            emit_pv(pend)
            stop(2.4, ti)
            for obi, ob in enumerate(OB):
                h0 = obi * 6
                nh_ = min(6, AH - h0)
                ov = pf(ob)[0:L, 0:nh_ * 65].rearrange("p (h d) -> p h d", h=nh_)
                S.recip(rs[0:L, h0:h0 + nh_], ov[:, :, 64])
                S.tt('dve', o_n[0:L, h0 * 64:(h0 + nh_) * 64].rearrange("p (h d) -> p h d", h=nh_), ov[:, :, 0:64],
                     bc(rs[0:L, h0:h0 + nh_], [[pstr(rs), L], [1, nh_], [0, 64]]), ALU.mult)
            S.tt('pool', og[0:L, :], o_n[0:L, :], gs2[0:L, sub, :], ALU.mult)
            bk = nb('tp')
            S.trs([(pbf(bk)[:, kc * L:(kc + 1) * L], og[0:L, kc * 128:(kc + 1) * 128]) for kc in range(8)],
                  identb[0:L, 0:L])
            S.copy('dve', ogT[:, :, sub * L:(sub + 1) * L], pbf(bk)[:, 0:8 * L].rearrange("p (a b) -> p a b", a=8))

        stop(3, ti)
        for wi in range(2):
            wv = w_use('wa%d' % wi)
            for et in range(4):
                bk = nb('mm')
                out = pf(bk)[:, 0:T]
                S.mms([dict(out=out, lhsT=wv[:, kc, et * 128:(et + 1) * 128], rhs=ogT[:, kc, 0:T],
                            start=(kc == 0), stop=(kc == 7)) for kc in range(8)])
                S.tt('dve', accT[:, wi * 4 + et, 0:T], out, ga_sig[:, wi * 4 + et, 0:T], ALU.mult)

        stop(4, ti)
        wv = w_use('dt')

        def evdt(sub, out):
            S.tt('dve', dtmp[0:L, :], out, dtb_bc[0:L, :], ALU.add)
            S.act(dtmp[0:L, :], dtmp[0:L, :], AF.Exp)
            S.act(dtt[0:L, sub, :], dtmp[0:L, :], AF.Ln, bias=1.0)
            S.tt('dve', adt[0:L, sub, :], dtt[0:L, sub, :], a_bc[0:L, :], ALU.mult)
            S.copy('dve', adt_hi[0:L, sub, :], adt[0:L, sub, :])
            S.tt('dve', adt_lo[0:L, sub, :], adt[0:L, sub, :], adt_hi[0:L, sub, :], ALU.subtract)
        tm_block(wv, 32, L, [0, 1], evdt)

        if kind == 'p':
            segs = [(0, T)]
            SW = T + 3
            S.copy('pool', xraw[:, :, 0:3], carry)
        else:
            SW = 67
            segs = [(0, 64), (67, 64)]
            for s in range(2):
                S.copy('pool', xraw[:, :, s * 67:s * 67 + 3], convst[:, idx * 2 + s].rearrange("p j c -> p c j"))
        for xi in range(6):
            wv = w_use('x%d' % xi)

            def evx(et, out, xi=xi):
                ct = xi * 4 + et
                if kind == 'p':
                    S.copy('act', xraw[:, ct, 3:3 + T], out)
                else:
                    S.copy('dve', xraw[:, ct, 0:134].rearrange("p (s c) -> p s c", c=67)[:, :, 3:67],
                           out.rearrange("p (s c) -> p s c", c=64))
            fm_block(wv, 4, T, evx)
            if last_p or kind == 's':
                osubs = [1] if kind == 'p' else [0, 1]

                def evc(sub, out, xi=xi):
                    st = xbcst[(xi + sub) % 2]
                    S.copy('dve', st[L - 32:L, :], out[L - 32:L, :])
                    dst = conv_p if kind == 'p' else conv_s[idx * 2 + sub]
                    S.dma('pool', dst[:, xi * 512:(xi + 1) * 512], st[L - 3:L, :], 'xbcst%d' % ((xi + sub) % 2))
                tm_block(wv, 512, L, osubs, evc)
            def conv_block(xj):
                accs = [cacc[0], cacc[1], xbcst[0][:, 0:256], xbcst[1][:, 0:256]]
                for (c0, nt) in segs:
                    cts = [xj * 4 + e for e in range(4)]
                    for e, ct in enumerate(cts):
                        S.ts('dve', accs[e][:, 0:nt], xraw[:, ct, c0:c0 + nt], convw[:, 0, ct:ct + 1],
                             convb[:, ct:ct + 1], ALU.mult, ALU.add)
                    for j in range(1, 4):
                        for e, ct in enumerate(cts):
                            S.stt('dve', accs[e][:, 0:nt], xraw[:, ct, c0 + j:c0 + j + nt],
                                  convw[:, j, ct:ct + 1], accs[e][:, 0:nt], ALU.mult, ALU.add)
                    if kind == 'p' and c0 == 0:
                        S.copy('pool', carry[:, xj * 4:(xj + 1) * 4, :], xraw[:, xj * 4:(xj + 1) * 4, T:T + 3])
                    ths = []
                    for e, ct in enumerate(cts):
                        th = nth()
                        ths.append(th)
                        S.act(th[:, 0:nt], accs[e][:, 0:nt], AF.Tanh)
                        if e >= 1:
                            S.stt('dve', xraw[:, cts[e - 1], c0:c0 + nt], ths[e - 1][:, 0:nt], 1.0,
                                  accs[e - 1][:, 0:nt], ALU.add, ALU.mult)
                    S.stt('dve', xraw[:, cts[3], c0:c0 + nt], ths[3][:, 0:nt], 1.0, accs[3][:, 0:nt],
                          ALU.add, ALU.mult)
            if xi > 0:
                conv_block(xi - 1)
        conv_block(5)

        stop(5, ti)
        for zi in range(4):
            wv = w_use('z%d' % zi)

            def evz(sub, out, zi=zi):
                th = nth()
                S.act(th[0:L, :], out, AF.Tanh, scale=0.5)
                S.stt('dve', zs2[0:L, sub, zi * 512:(zi + 1) * 512], th[0:L, :], 1.0, out, ALU.add, ALU.mult)
            tm_block(wv, 512, L, [0, 1], evz)
        for gi in range(2):
            wv = w_use('gs%d' % gi)

            def evgs(et, out, gi=gi):
                th = nth()
                S.act(th[:, 0:T], out, AF.Tanh, scale=0.5)
                S.act(gs_sig[:, gi * 4 + et, 0:T], th[:, 0:T], AF.Identity, scale=0.5, bias=0.5)
            fm_block(wv, 4, T, evgs)

        stop(6, ti)
        def cview(t, n):
            return bc(t, [[pstr(t), L], [L, n], [1, L]])
        Bmh_v, Bml_v, Eb_v, MT_v, cb_v = cview(Bmh, 8), cview(Bml, 8), cview(Eb, 8), cview(MTb, 8), cview(cbTm, 4)
        for sub in range(2):
            if kind == 'p':
                cs = sub * 128
            else:
                cs = sub * 67
                load_state(idx * 2 + sub)
            stop(6.2, ti)
            for hb in range(2):
                bk = nb('tp')
                S.trs([(pbf(bk)[0:L, j * 128:(j + 1) * 128], xraw[:, hb * 8 + j, cs:cs + L]) for j in range(8)],
                      identb)
                S.copy('act' if hb else 'dve', xs_tok[0:L, hb * 1024:(hb + 1) * 1024], pbf(bk)[0:L, 0:1024])
            bk = nb('tp')
            S.trs([(pbf(bk)[0:L, j * 128:(j + 1) * 128], xraw[:, 16 + j, cs:cs + L]) for j in range(4)], identb)
            S.copy('dve', B_tok[0:L, :], pbf(bk)[0:L, 0:512])
            stop(6.4, ti)
            sm = pf(SMB)
            ah, al = adt_hi[0:L, sub, :], adt_lo[0:L, sub, :]
            S.mms([dict(out=sm[0:L, 0:32], lhsT=tri[0:L, 0:L], rhs=ah, start=True, stop=False),
                   dict(out=sm[0:L, 0:32], lhsT=tri[0:L, 0:L], rhs=al, start=False, stop=True),
                   dict(out=sm[0:L, 32:64], lhsT=SU[0:L, 0:L], rhs=ah, start=True, stop=False),
                   dict(out=sm[0:L, 32:64], lhsT=SU[0:L, 0:L], rhs=al, start=False, stop=True),
                   dict(out=sm[:, 64:96], lhsT=onesb[0:L, :], rhs=ah, start=True, stop=False),
                   dict(out=sm[:, 64:96], lhsT=onesb[0:L, :], rhs=al, start=False, stop=True)])
            S.act(ee[0:L, :], sm[0:L, 0:64], AF.Exp)
            S.act(ee2, sm[:, 64:96], AF.Exp)
            cbb = pf(CBB)
            S.mms([dict(out=cbb[0:L, gq * L:(gq + 1) * L], lhsT=xraw[:, 16 + gq, cs:cs + L],
                        rhs=xraw[:, 20 + gq, cs:cs + L], start=True, stop=True) for gq in range(4)])
            S.tt('dve', cb_v, cbb[0:L, 0:4 * L].rearrange("p (g l) -> p g l", g=4),
                 bc(tri, [[pstr(tri), L], [0, 4], [1, L]]), ALU.mult)
            stop(6.6, ti)
            def b8(ap2):
                return bc(ap2, [[ap2.ap[0][0], L], [1, 8], [0, 64]])
            x3 = lambda a: a[0:L, :].rearrange("p (r d) -> p r d", r=8)
            nhalf = 2 if L == 128 else 1
            hh = 8 // nhalf

            def s0_pool(gq):
                xdt, xdt2, xsd = xdtb[gq % 2], xdt2b[gq % 2], xsdb[gq % 2]
                xsg = xs_tok[0:L, gq * 512:(gq + 1) * 512].rearrange("p (r d) -> p r d", r=8)
                S.tt('pool', x3(xdt), xsg, b8(dtt[0:L, sub, gq * 8:(gq + 1) * 8]), ALU.mult)
                S.tt('pool', x3(xdt2), x3(xdt), b8(ee[0:L, 32 + gq * 8:32 + (gq + 1) * 8]), ALU.mult)
                S.tt('pool', x3(xsd), xsg, b8(dsk_bc[0:L, gq * 8:(gq + 1) * 8]), ALU.mult)

            def s1a(gq):
                for (src, dst) in ((adt_hi, Bmh_v), (adt_lo, Bml_v)):
                    a8 = src[0:L, sub, gq * 8:(gq + 1) * 8]
                    S.tt('dve', dst, bc(a8, [[a8.ap[0][0], L], [1, 8], [0, L]]),
                         bc(tri, [[pstr(tri), L], [0, 8], [1, L]]), ALU.mult)

            def s1b(gq):
                items = []
                for hf in range(nhalf):
                    out = pf(SEGB[hf])[0:L, 0:hh * L]
                    items.append(dict(out=out, lhsT=SU[0:L, 0:L], rhs=Bmh_v[:, hf * hh:(hf + 1) * hh, :],
                                      start=True, stop=False))
                    items.append(dict(out=out, lhsT=SU[0:L, 0:L], rhs=Bml_v[:, hf * hh:(hf + 1) * hh, :],
                                      start=False, stop=True))
                S.mms(items)
                for hf in range(nhalf):
                    S.act(Eb_v[:, hf * hh:(hf + 1) * hh, :],
                          pf(SEGB[hf])[0:L, 0:hh * L].rearrange("p (r l) -> p r l", r=hh), AF.Exp)

            def s2(gq):
                xdt, xdt2, xsd = xdtb[gq % 2], xdt2b[gq % 2], xsdb[gq % 2]
                cg = cb_v[:, gq, :]
                S.tt('dve', MT_v, Eb_v, bc(cg, [[cg.ap[0][0], L], [0, 8], [1, L]]), ALU.mult)
                yb = pf(YB)
                S.mms([dict(out=yb[0:L, 0:512], lhsT=xraw[:, 20 + gq, cs:cs + L], rhs=hTb[:, gq * 512:(gq + 1) * 512],
                            start=True, stop=False)])
                S.tt('dve', yb[0:L, 0:512].rearrange("p (r d) -> p r d", r=8),
                     yb[0:L, 0:512].rearrange("p (r d) -> p r d", r=8), b8(ee[0:L, gq * 8:(gq + 1) * 8]), ALU.mult)
                items = [dict(out=yb[0:L, 0:512], lhsT=identb[0:L, 0:L], rhs=xsd[0:L, :], start=False, stop=False)]
                for r in range(8):
                    items.append(dict(out=yb[0:L, r * 64:(r + 1) * 64], lhsT=MT_v[:, r, :],
                                      rhs=xdt[0:L, r * 64:(r + 1) * 64], start=False, stop=(r == 7)))
                S.mms(items)
                ub = pf(UB)
                S.mms([dict(out=ub[:, 0:512], lhsT=B_tok[0:L, gq * 128:(gq + 1) * 128], rhs=xdt2[0:L, :],
                            start=True, stop=True)])

            def s3(gq):
                yb = pf(YB)
                ub = pf(UB)
                S.tt('dve', yz[0:L, gq, :], yb[0:L, 0:512], zs2[0:L, sub, gq * 512:(gq + 1) * 512], ALU.mult)
                S.act(junk[0:L, 0:512], yz[0:L, gq, :], AF.Square, accum_out=ssq[0:L, gq:gq + 1])
                hg = hT[:, gq * 512:(gq + 1) * 512]
                e8 = ee2[:, gq * 8:(gq + 1) * 8]
                S.tt('pool', hg.rearrange("p (r d) -> p r d", r=8), hg.rearrange("p (r d) -> p r d", r=8),
                     bc(e8, [[e8.ap[0][0], 128], [1, 8], [0, 64]]), ALU.mult)
                S.tt('dve', hg, hg, ub[:, 0:512], ALU.add)
                S.copy('act', hTb[:, gq * 512:(gq + 1) * 512], hg)

            s0_pool(0)
            s1a(0)
            s1b(0)
            for gq in range(4):
                if gq < 3:
                    s0_pool(gq + 1)
                s2(gq)
                if gq < 3:
                    s1a(gq + 1)
                    s1b(gq + 1)
                s3(gq)
            stop(6.8, ti)
            S.ts('dve', rstd4[0:L, :], ssq[0:L, :], 1.0 / 512.0, 4.0 * EPS, ALU.mult, ALU.add)
            S.tt('pool', rstd4[0:L, :], rstd4[0:L, :], neghalf[0:L, :], ALU.pow)
            S.tt('pool', yz[0:L, :, :], yz[0:L, :, :], bc(rstd4, [[pstr(rstd4), L], [1, 4], [0, 512]]), ALU.mult)
            yzf = yz[0:L, :, :].rearrange("p a b -> p (a b)")
            for hb in range(2):
                bk = nb('tp')
                S.trs([(pbf(bk)[:, j * L:(j + 1) * L], yzf[:, (hb * 8 + j) * 128:(hb * 8 + j + 1) * 128])
                       for j in range(8)], identb[0:L, 0:L])
                nwv = nwssm[:, hb * 8:(hb + 1) * 8]
                S.tt('dve', yzT[:, hb * 8:(hb + 1) * 8, sub * L:(sub + 1) * L],
                     pbf(bk)[:, 0:8 * L].rearrange("p (a b) -> p a b", a=8),
                     bc(nwv, [[nwv.ap[0][0], 128], [1, 8], [0, L]]), ALU.mult)
            stop(6.9, ti)
            if kind == 's':
                store_state(ssm_s[idx * 2 + sub])
            elif last_p and sub == 1:
                store_state(ssm_p)

        stop(7, ti)
        for wi in range(4):
            wv = w_use('ws%d' % wi)
            for et in range(2):
                e8 = wi * 2 + et
                bk = nb('mm')
                out = pf(bk)[:, 0:T]
                S.mms([dict(out=out, lhsT=wv[:, kc, et * 128:(et + 1) * 128], rhs=yzT[:, kc, 0:T],
                            start=(kc == 0), stop=(kc == 15)) for kc in range(16)])
                th = nth()
                S.tt('dve', th[:, 0:T], out, gs_sig[:, e8, 0:T], ALU.mult)
                S.tt('pool', mT[:, e8, 0:T], th[:, 0:T], accT[:, e8, 0:T], ALU.add)

        stop(8, ti)
        wvs = [w_use('wo0'), w_use('wo1', issue=False)]
        for sub in range(2):
            for hf in range(2):
                bk = nb('mm')
                out = pf(bk)[0:L, 0:512]
                S.mms([dict(out=out, lhsT=mT[:, kc, sub * L:(sub + 1) * L], rhs=wvs[hf][:, kc, :],
                            start=(kc == 0), stop=(kc == 7)) for kc in range(8)])
                S.tt('dve', ost[0:L, hf * 512:(hf + 1) * 512], out, xb[sub][0:L, hf * 512:(hf + 1) * 512], ALU.add)
            ssx = smallf[0:L, 2:3]
            S.act(junk[0:L, :], ost[0:L, :], AF.Square, accum_out=ssx)
            rms_rstd(ssx, 1024.0, EPS, smallf[0:L, 3:4], L)
            S.stt('dve', ost[0:L, :], ost[0:L, :], smallf[0:L, 3:4], fnw_bc[0:L, :], ALU.mult, ALU.mult)
            _, dsty = xrows(kind, idx, sub, L)
            S.dma('pool', dsty, ost[0:L, :], 'ost')

    if dbg is not None:
        dbg(S, locals())


_CACHE = {}


def _get_nc(SEQ):
    if SEQ not in _CACHE:
        nc = bass.Bass("TRN2", target_bir_lowering=False)
        build(nc, SEQ)
        _CACHE[SEQ] = nc
    return _CACHE[SEQ]


def kernel(x_prompt, x_sample, state_ssm, state_conv, cache_k, cache_v, norm_w, w_in, conv_w, conv_b, dt_bias,
           a_log, d_skip, ssm_norm_w, w_out_ssm, rel_bias, w_out_att, w_o, final_norm_w):
    f = lambda a: np.ascontiguousarray(np.asarray(a, dtype=np.float32))
    B, SEQ, _ = x_prompt.shape
    assert B == 8 and SEQ % 256 == 0
    KEEP = min(512, SEQ)
    nc = _get_nc(SEQ)
    shared = dict(norm_w=f(norm_w[0]), w_in=f(w_in[0]), conv_w=f(conv_w[0]), conv_b=f(conv_b[0]),
                  dt_bias=f(dt_bias[0]), a_log=f(a_log[0]), d_skip=f(d_skip[0]), ssm_norm_w=f(ssm_norm_w[0]),
                  w_out_ssm=f(w_out_ssm[0]), rel_bias=f(rel_bias[0]), w_out_att=f(w_out_att[0]), w_o=f(w_o[0]),
                  final_norm_w=f(final_norm_w))
    in_maps = []
    for c in range(_NCORES):
        m = dict(shared)
        m['x_p'] = f(x_prompt[c])
        m['x_s'] = f(x_sample[4 * c:4 * c + 4]).reshape(256, D)
        m['st_ssm'] = f(state_ssm[0, 4 * c:4 * c + 4]).reshape(4, 2048, 128)
        m['st_conv'] = f(state_conv[0, 4 * c:4 * c + 4])
        m['ck'] = f(cache_k[0, 4 * c:4 * c + 4]).reshape(4, 512, D)
        m['cv'] = f(cache_v[0, 4 * c:4 * c + 4]).reshape(4, 512, D)
        in_maps.append(m)
    res = run_bass_kernel_spmd(nc, in_maps, core_ids=list(range(_NCORES)))
    R = list(res.results) + [res.results[0]] * (8 - _NCORES)
    y_prompt = np.stack([R[c]['y_p'] for c in range(8)]).astype(np.float32)
    y_sample = np.concatenate([R[c]['y_s'].reshape(4, 64, D) for c in range(8)]).astype(np.float32)
    ssm_p = np.stack([R[c]['ssm_p'].reshape(NH, 64, 128) for c in range(8)])[None].astype(np.float32)
    conv_p = np.stack([R[c]['conv_p'] for c in range(8)])[None].astype(np.float32)
    k_p = np.stack([R[c]['k_p'].reshape(KEEP, AH, 64) for c in range(8)])[None].astype(np.float32)
    v_p = np.stack([R[c]['v_p'].reshape(KEEP, AH, 64) for c in range(8)])[None].astype(np.float32)
    ssm_s = np.concatenate([R[c]['ssm_s'].reshape(4, NH, 64, 128) for c in range(8)])[None].astype(np.float32)
    conv_s = np.concatenate([R[c]['conv_s'] for c in range(8)])[None].astype(np.float32)
    k_s = np.concatenate([R[c]['k_s'].reshape(4, 512, AH, 64) for c in range(8)])[None].astype(np.float32)
    v_s = np.concatenate([R[c]['v_s'].reshape(4, 512, AH, 64) for c in range(8)])[None].astype(np.float32)
    return (y_prompt, y_sample, ssm_p, conv_p, k_p, v_p, ssm_s, conv_s, k_s, v_s)
```

```python
import numpy as np
from contextlib import ExitStack
import concourse.bass as bass
import concourse.mybir as mybir
from concourse.bass_utils import run_bass_kernel_spmd

F32 = mybir.dt.float32
BF16 = mybir.dt.bfloat16
U8 = mybir.dt.uint8
AF = mybir.ActivationFunctionType
ALU = mybir.AluOpType

D = 1024
KC = 8
DI = 2048
CD = 3072
NH = 32
AH = 16
DIN = 11296
C_Z, C_X, C_DT, C_Q, C_K, C_V, C_G, C_GS, C_GA = 0, 2048, 5120, 5152, 6176, 7200, 8224, 9248, 10272
EPS = 1e-5
NEG = -30000.0


def _dsz(dt):
    if dt == F32:
        return 4
    if dt == BF16:
        return 2
    if dt == U8:
        return 1
    raise ValueError(dt)


class _Op:
    __slots__ = ('eng', 'fn', 'deps', 'dma', 'key', 'val', 'seq', 'hasdep')


class Sched:
    def __init__(self, nc, es):
        self.nc = nc
        self.es = es
        self.ops = []
        self.engobj = {'pe': nc.tensor, 'act': nc.scalar, 'dve': nc.vector, 'pool': nc.gpsimd, 'sp': nc.sync}
        self.esem = {k: es.enter_context(nc.semaphore('es_' + k)) for k in self.engobj}
        self.dkeys = {}
        self.tracked = {}
        self.acc = {}

    def track(self, name, const=False):
        self.tracked[name] = 'const' if const else True

    def reg(self, ap):
        name = ap.tensor.name
        tr = self.tracked.get(name)
        if tr is None:
            return None
        dims = ap.ap
        ps = dims[0][0]
        npart = dims[0][1]
        off = ap.offset
        if ps > 0:
            p0 = off // ps
            f0 = off % ps
        else:
            p0 = 0
            f0 = off
        lo = f0
        hi = f0
        for (s, c) in dims[1:]:
            if s >= 0:
                hi += s * (c - 1)
            else:
                lo += s * (c - 1)
        sz = _dsz(ap.dtype)
        return (name, p0, p0 + npart, lo * sz, (hi + 1) * sz)

    def add(self, eng, fn, reads=(), writes=(), dma_key=None):
        i = len(self.ops)
        deps = set()
        rr = []
        ww = []
        for a in reads:
            r = a if isinstance(a, tuple) else self.reg(a)
            if r is not None:
                if r[0].startswith('pb'):
                    ww.append((r[0], 0, 128, 0, 2048))
                else:
                    rr.append(r)
        for a in writes:
            r = a if isinstance(a, tuple) else self.reg(a)
            if r is not None:
                if r[0].startswith('pb'):
                    r = (r[0], 0, 128, 0, 2048)
                ww.append(r)
        for (k, p0, p1, lo, hi) in rr:
            for e in self.acc.get(k, ()):
                if e[5] and e[0] < p1 and p0 < e[1] and e[2] < hi and lo < e[3]:
                    deps.add(e[4])
        for (k, p0, p1, lo, hi) in ww:
            lst = self.acc.get(k)
            if lst is None:
                continue
            keep = []
            for e in lst:
                if e[0] < p1 and p0 < e[1] and e[2] < hi and lo < e[3]:
                    deps.add(e[4])
                    if p0 <= e[0] and e[1] <= p1 and lo <= e[2] and e[3] <= hi:
                        continue
                keep.append(e)
            self.acc[k] = keep
        deps.discard(i)
        for (k, p0, p1, lo, hi) in rr:
            if self.tracked.get(k) == 'const':
                continue
            self.acc.setdefault(k, []).append([p0, p1, lo, hi, i, False])
        for (k, p0, p1, lo, hi) in ww:
            self.acc.setdefault(k, []).append([p0, p1, lo, hi, i, True])
        op = _Op()
        op.eng = eng
        op.fn = fn
        op.deps = deps
        op.dma = dma_key is not None
        op.key = dma_key
        op.val = 0
        op.seq = 0
        op.hasdep = False
        if op.dma:
            ent = self.dkeys.get(dma_key)
            if ent is None:
                ent = [self.es.enter_context(self.nc.semaphore('d_%d' % len(self.dkeys))), 0, []]
                self.dkeys[dma_key] = ent
            ent[1] += 16
            op.val = ent[1]
            ent[2].append(i)
        self.ops.append(op)
        return i

    def group_barrier(self, key):
        ent = self.dkeys[key]
        for i in ent[2]:
            self.ops[i].val = ent[1]

    def group_last(self, key, n):
        ent = self.dkeys[key]
        for i in ent[2][-n:]:
            self.ops[i].val = ent[1]

    def emit(self):
        ops = self.ops
        for op in ops:
            for d in op.deps:
                ops[d].hasdep = True
        cnt = {k: 0 for k in self.engobj}
        for op in ops:
            if not op.dma and op.hasdep:
                cnt[op.eng] += 1
                op.seq = cnt[op.eng]
        waited = {k: {} for k in self.engobj}
        nw = 0
        for op in ops:
            eng = self.engobj[op.eng]
            wd = waited[op.eng]
            need = {}
            for d in op.deps:
                dop = ops[d]
                if dop.dma:
                    sem, val = self.dkeys[dop.key][0], dop.val
                else:
                    if dop.eng == 'pe' and op.eng == 'pe' and not op.dma:
                        continue
                    sem, val = self.esem[dop.eng], dop.seq
                k = id(sem)
                if need.get(k, (None, 0))[1] < val:
                    need[k] = (sem, val)
            for k, (sem, val) in need.items():
                if wd.get(k, 0) >= val:
                    continue
                eng.wait_ge(sem, val)
                wd[k] = val
                nw += 1
            inst = op.fn(eng)
            if op.dma:
                inst.then_inc(self.dkeys[op.key][0], 16)
            elif op.hasdep:
                inst.then_inc(self.esem[op.eng], 1)
        sp = self.engobj['sp']
        for key, ent in self.dkeys.items():
            sp.wait_ge(ent[0], ent[1])
        return len(ops), nw

    def mms(self, items):
        reads = []
        writes = []
        for it in items:
            reads += [it['lhsT'], it['rhs']]
            writes.append(it['out'])
            if not it['start']:
                reads.append(it['out'])

        def fn(e, items=items):
            inst = None
            for it in items:
                inst = e.matmul(it['out'], lhsT=it['lhsT'], rhs=it['rhs'], start=it['start'], stop=it['stop'],
                                skip_group_check=True)
            return inst
        return self.add('pe', fn, reads, writes)

    def trs(self, items, ident):
        reads = [ident] + [x[1] for x in items]
        writes = [x[0] for x in items]

        def fn(e, items=items, ident=ident):
            inst = None
            for (o, i) in items:
                inst = e.transpose(o, i, ident)
            return inst
        return self.add('pe', fn, reads, writes)

    def act(self, out, in_, func, bias=None, scale=None, accum_out=None, extra_w=()):
        reads = [in_]
        kw = {}
        if bias is not None:
            kw['bias'] = bias
            if not isinstance(bias, (int, float)):
                reads.append(bias)
        if scale is not None:
            kw['scale'] = scale
            if not isinstance(scale, (int, float)):
                reads.append(scale)
        writes = [out] + list(extra_w)
        if accum_out is not None:
            kw['accum_out'] = accum_out
            writes.append(accum_out)
        return self.add('act', lambda e: e.activation(out=out, in_=in_, func=func, **kw), reads, writes)

    def tt(self, eng, out, in0, in1, op):
        return self.add(eng, lambda e: e.tensor_tensor(out=out, in0=in0, in1=in1, op=op), [in0, in1], [out])

    def ts(self, eng, out, in0, s1, s2, op0, op1=None):
        reads = [in0]
        if not isinstance(s1, (int, float)):
            reads.append(s1)
        if s2 is not None and not isinstance(s2, (int, float)):
            reads.append(s2)
        if op1 is None:
            return self.add(eng, lambda e: e.tensor_scalar(out=out, in0=in0, scalar1=s1, scalar2=None, op0=op0),
                            reads, [out])
        return self.add(eng, lambda e: e.tensor_scalar(out=out, in0=in0, scalar1=s1, scalar2=s2, op0=op0, op1=op1),
                        reads, [out])

    def stt(self, eng, out, in0, scalar, in1, op0, op1):
        reads = [in0, in1]
        if not isinstance(scalar, (int, float)):
            reads.append(scalar)
        return self.add(eng, lambda e: e.scalar_tensor_tensor(out=out, in0=in0, scalar=scalar, in1=in1, op0=op0,
                                                              op1=op1), reads, [out])

    def copy(self, eng, out, in_):
        if eng == 'act':
            return self.add('act', lambda e: e.activation(out=out, in_=in_, func=AF.Identity), [in_], [out])
        return self.add(eng, lambda e: e.tensor_copy(out=out, in_=in_), [in_], [out])

    def memset(self, eng, ap, val):
        return self.add(eng, lambda e: e.memset(ap, val), [], [ap])

    def recip(self, out, in_):
        return self.add('dve', lambda e: e.reciprocal(out=out, in_=in_), [in_], [out])

    def dma(self, q, out, in_, key, extra_r=(), extra_w=(), slow=False):
        if slow:
            fn = lambda e: e.dma_start(out=out, in_=in_, allow_slow_non_contiguous=True)
        else:
            fn = lambda e: e.dma_start(out=out, in_=in_)
        return self.add(q, fn, [in_] + list(extra_r), [out] + list(extra_w), dma_key=key)


def bc(ap, dims, off=0):
    return bass.AP(ap.tensor, ap.offset + off, dims)


def pstr(ap):
    return ap.ap[0][0]


class Arena:
    def __init__(self, nc, es, S, name, nbytes):
        self.t = es.enter_context(nc.sbuf_tensor(name, [128, nbytes], U8))
        self.off = 0
        self.cap = nbytes
        self.name = name
        S.track(name)

    def alloc(self, free_shape, dt):
        n = int(np.prod(free_shape)) * _dsz(dt)
        off = (self.off + 31) // 32 * 32
        assert off + n <= self.cap, (self.name, off, n, self.cap)
        self.off = off + n
        v = self.t[:, off:off + n].bitcast(dt)
        if len(free_shape) == 2:
            v = v.rearrange("p (a b) -> p a b", a=free_shape[0])
        elif len(free_shape) == 3:
            v = v.rearrange("p (a b c) -> p a b c", a=free_shape[0], b=free_shape[1])
        return v


_STOP = (99, 0)
_NCORES = 8
_DBG = ''
_ENV = {}
_DUMP = None


class _StopBuild(Exception):
    pass


def build(nc, SEQ, dbg=None):
    es = ExitStack()
    S = Sched(nc, es)
    try:
        _body(nc, SEQ, es, S, dbg)
    except _StopBuild:
        pass
    if _DUMP is not None:
        _DUMP(S, _ENV)
    nops, nw = S.emit()
    es.close()
    return nops, nw


def _body(nc, SEQ, es, S, dbg=None):
    NTP = SEQ // 256
    KEEP = min(512, SEQ)

    def din(name, shape):
        return nc.dram_tensor(name, shape, F32, kind="ExternalInput").ap()

    def dout(name, shape):
        return nc.dram_tensor(name, shape, F32, kind="ExternalOutput").ap()

    x_p = din("x_p", [SEQ, D])
    x_s = din("x_s", [256, D])
    st_ssm = din("st_ssm", [4, 2048, 128])
    st_conv = din("st_conv", [4, 3, CD])
    ck = din("ck", [4, 512, D])
    cv = din("cv", [4, 512, D])
    norm_w = din("norm_w", [D])
    w_in = din("w_in", [D, DIN])
    conv_w = din("conv_w", [4, CD])
    conv_b = din("conv_b", [CD])
    dt_bias = din("dt_bias", [NH])
    a_log = din("a_log", [NH])
    d_skip = din("d_skip", [NH])
    ssm_norm_w = din("ssm_norm_w", [DI])
    w_out_ssm = din("w_out_ssm", [DI, D])
    rel_bias = din("rel_bias", [AH, 513])
    w_out_att = din("w_out_att", [D, D])
    w_o = din("w_o", [D, D])
    final_norm_w = din("final_norm_w", [D])

    y_p = dout("y_p", [SEQ, D])
    y_s = dout("y_s", [256, D])
    ssm_p = dout("ssm_p", [2048, 128])
    conv_p = dout("conv_p", [3, CD])
    k_p = dout("k_p", [KEEP, D])
    v_p = dout("v_p", [KEEP, D])
    ssm_s = dout("ssm_s", [4, 2048, 128])
    conv_s = dout("conv_s", [4, 3, CD])
    k_s = dout("k_s", [4, 512, D])
    v_s = dout("v_s", [4, 512, D])

    WB = {}
    blist = []

    def addblk(name, src, nrow_chunks, c0, ncols, N):
        WB[name] = dict(idx=len(blist), src=src, kc=nrow_chunks, c0=c0, ncols=ncols, N=N)
        blist.append(name)

    for i in range(2):
        addblk('q%d' % i, w_in, 8, C_Q + 512 * i, 512, DIN)
    for i in range(2):
        addblk('k%d' % i, w_in, 8, C_K + 512 * i, 512, DIN)
    for i in range(2):
        addblk('v%d' % i, w_in, 8, C_V + 512 * i, 512, DIN)
    for i in range(2):
        addblk('g%d' % i, w_in, 8, C_G + 512 * i, 512, DIN)
    for i in range(2):
        addblk('ga%d' % i, w_in, 8, C_GA + 512 * i, 512, DIN)
    for i in range(2):
        addblk('wa%d' % i, w_out_att, 8, 512 * i, 512, D)
    addblk('dt', w_in, 8, C_DT, 32, DIN)
    for i in range(6):
        addblk('x%d' % i, w_in, 8, C_X + 512 * i, 512, DIN)
    for i in range(4):
        addblk('z%d' % i, w_in, 8, C_Z + 512 * i, 512, DIN)
    for i in range(2):
        addblk('gs%d' % i, w_in, 8, C_GS + 512 * i, 512, DIN)
    for i in range(4):
        addblk('ws%d' % i, w_out_ssm, 16, 256 * i, 256, D)
    for i in range(2):
        addblk('wo%d' % i, w_o, 8, 512 * i, 512, D)
    NB = len(blist)
    wscr = nc.dram_tensor("wscr", [NB, 128, 4096], BF16, kind="Internal").ap()
    ext = nc.dram_tensor("ext_bias", [AH, 1024], F32, kind="Internal").ap()

    def wscr_view(name):
        b = WB[name]
        i = b['idx']
        kc, nc_ = b['kc'], b['ncols']
        base = wscr[i]
        return bc(base, [[4096, 128], [nc_, kc], [1, nc_]])

    NGRP = 4
    for n, name in enumerate(blist):
        b = WB[name]
        src = bass.AP(b['src'].tensor, b['c0'], [[b['N'], 128], [128 * b['N'], b['kc']], [1, b['ncols']]])
        grp = min(NGRP - 1, n * NGRP // NB)
        S.dma('pool', wscr_view(name), src, 'cv%d' % grp, extra_w=[('wscr', b['idx'], b['idx'] + 1, 0, 1)])
        b['grp'] = grp
    for g in range(NGRP):
        S.group_barrier('cv%d' % g)

    def cstop(n):
        if _STOP == (n, 0):
            raise _StopBuild()
    cstop(-1)

    CA = Arena(nc, es, S, "cst", 29 * 1024)
    S.tracked["cst"] = True
    A = Arena(nc, es, S, "main", 183040)

    identb = CA.alloc([128], BF16)
    identf = CA.alloc([128], F32)
    Jx = CA.alloc([128], BF16)
    tri = CA.alloc([128], BF16)
    SU = CA.alloc([128], BF16)
    onesb = CA.alloc([128], BF16)
    scr32 = CA.alloc([128], F32)
    cstg = CA.alloc([128], F32)
    cpar = CA.alloc([144], F32)
    normw = cpar[:, 0:8]
    nwssm = cpar[:, 8:24]
    convb = cpar[:, 24:48]
    convw = cpar[:, 48:144].rearrange("p (j c) -> p j c", j=4)
    dtb_bc = CA.alloc([32], F32)
    a_bc = CA.alloc([32], F32)
    dsk_bc = CA.alloc([32], F32)
    fnw_bc = CA.alloc([1024], F32)
    neghalf = CA.alloc([4], F32)
    biasF = CA.alloc([AH, 5, 128], BF16)
    convst = CA.alloc([4, 3, 24], BF16)

    xt = [A.alloc([1024], F32) for _ in range(4)]
    xsn = A.alloc([1024], BF16)
    TH = [A.alloc([512], BF16) for _ in range(3)]
    junk = bc(TH[0], [[pstr(TH[0]), 128], [1, 1024]])
    xnT = A.alloc([8, 256], BF16)
    NSLOT = 3
    Wr = [A.alloc([4096], BF16) for _ in range(NSLOT)]
    KT = A.alloc([8, 6, 128], BF16)
    Vr = A.alloc([6, AH, 66], BF16)
    hT = A.alloc([2048], F32)
    hTb = A.alloc([2048], BF16)
    ost = A.alloc([1024], F32)
    ogT = A.alloc([8, 256], BF16)
    ga_sig = A.alloc([8, 256], BF16)
    gs_sig = A.alloc([8, 256], BF16)
    accT = A.alloc([8, 256], BF16)
    yzT = A.alloc([16, 256], BF16)
    mT = A.alloc([8, 256], BF16)
    smallf = A.alloc([64], F32)
    carry = A.alloc([24, 3], BF16)
    base = A.off
    QT = A.alloc([8, 256], BF16)
    PT = [A.alloc([512], BF16) for _ in range(3)]
    rs = A.alloc([16], F32)
    o_n = A.alloc([1024], BF16)
    og = A.alloc([1024], BF16)
    gs2 = A.alloc([2, 1024], BF16)
    kvst = [A.alloc([512], F32) for _ in range(2)]
    ckst = A.alloc([4, 1024], BF16)
    KTs = A.alloc([8, 2, 64], BF16)
    Vs = A.alloc([2, AH, 66], BF16)
    att_end = A.off
    A.off = base
    zs2 = A.alloc([2, 2048], BF16)
    xs_tok = A.alloc([2048], BF16)
    B_tok = A.alloc([512], BF16)
    xdtb = [A.alloc([512], BF16) for _ in range(2)]
    xdt2b = [A.alloc([512], BF16) for _ in range(2)]
    xsdb = [A.alloc([512], BF16) for _ in range(2)]
    cbTm = A.alloc([4, 128], BF16)
    Bmh = A.alloc([8, 128], BF16)
    Bml = A.alloc([8, 128], BF16)
    Eb = A.alloc([8, 128], BF16)
    MTb = A.alloc([8, 128], BF16)
    yz = A.alloc([4, 512], BF16)
    ssq = A.alloc([4], F32)
    rstd4 = A.alloc([4], F32)
    A.off = max(A.off, att_end)
    XW = 3 + 256
    xraw = A.alloc([24, XW], BF16)
    cacc = [A.alloc([256], F32) for _ in range(2)]
    dtt = A.alloc([2, 32], F32)
    adt = A.alloc([2, 32], F32)
    adt_hi = A.alloc([2, 32], BF16)
    adt_lo = A.alloc([2, 32], BF16)
    dtmp = A.alloc([32], F32)
    ee = A.alloc([64], F32)
    ee2 = A.alloc([32], F32)
    xbcst = [A.alloc([512], F32) for _ in range(2)]
    stT = A.alloc([16, 128], F32)
    ssd_end = A.off
    A.off = max(att_end, ssd_end)
    print('SBUF use: cst', CA.off, 'main', A.off, 'att', att_end - base, 'ssd', ssd_end - base)

    _ENV.update(dict(accT=accT, mT=mT, yzT=yzT, ogT=ogT, gs_sig=gs_sig, ga_sig=ga_sig, xnT=xnT, QT=QT, zs2=zs2,
                     yz=yz, ost=ost, gs2=gs2, o_n=o_n, og=og, hT=hT, xraw=xraw, nc=nc, A=A, xt=xt))
    PB = []
    for i in range(8):
        t = es.enter_context(nc.psum_tensor("pb%d" % i, [128, 2048], U8))
        S.track("pb%d" % i)
        PB.append(t)

    def pf(i):
        return PB[i][:, :].bitcast(F32)

    def pbf(i):
        return PB[i][:, :].bitcast(BF16)

    rot = {'mm': [0, 1, 3, 4], 'tp': [2, 0, 1], 'S': [3, 4]}
    rotp = {k: 0 for k in rot}

    def nb(role):
        i = rot[role][rotp[role] % len(rot[role])]
        rotp[role] += 1
        return i
    OB = [5, 6, 7]
    SEGB = [3, 4]
    YB, UB, CBB, SMB = 5, 7, 6, 2

    P = 'pool'
    S.memset(P, scr32, 0.0)
    S.add(P, lambda e: e.affine_select(out=scr32, in_=scr32, pattern=[[-1, 128]], compare_op=ALU.not_equal, fill=1.0,
                                       base=0, channel_multiplier=1), [scr32], [scr32])
    S.copy('dve', identb, scr32)
    S.copy('dve', identf, scr32)
    S.memset(P, scr32, 0.0)
    S.add(P, lambda e: e.affine_select(out=scr32, in_=scr32, pattern=[[1, 128]], compare_op=ALU.not_equal, fill=1.0,
                                       base=-127, channel_multiplier=1), [scr32], [scr32])
    S.copy('dve', Jx, scr32)
    S.memset(P, scr32, 1.0)
    S.copy('dve', onesb, scr32)
    S.add(P, lambda e: e.affine_select(out=scr32, in_=scr32, pattern=[[1, 128]], compare_op=ALU.is_ge, fill=0.0,
                                       base=0, channel_multiplier=-1), [scr32], [scr32])
    S.copy('dve', tri, scr32)
    S.memset(P, scr32, 1.0)
    S.add(P, lambda e: e.affine_select(out=scr32, in_=scr32, pattern=[[-1, 128]], compare_op=ALU.is_ge, fill=0.0,
                                       base=-1, channel_multiplier=1), [scr32], [scr32])
    S.copy('dve', SU, scr32)
    S.memset(P, neghalf, -0.5)
    S.memset(P, hT, 0.0)
    S.memset(P, hTb, 0.0)
    S.memset(P, carry, 0.0)
    S.memset(P, Vr[:, :, :, 64:65], 1.0)

    cstop(-2)
    CK = 'const'
    S.dma('sp', dtb_bc, bass.AP(dt_bias.tensor, 0, [[0, 128], [1, 32]]), CK)
    S.dma('sp', a_bc, bass.AP(a_log.tensor, 0, [[0, 128], [1, 32]]), CK)
    S.dma('sp', dsk_bc, bass.AP(d_skip.tensor, 0, [[0, 128], [1, 32]]), CK)
    S.dma('sp', fnw_bc, bass.AP(final_norm_w.tensor, 0, [[0, 128], [1, 1024]]), CK)
    S.group_barrier(CK)
    cstop(-3)
    S.dma('sp', cstg[0:8, :], bass.AP(norm_w.tensor, 0, [[128, 8], [1, 128]]), 'cstg')
    S.dma('sp', cstg[8:24, :], bass.AP(ssm_norm_w.tensor, 0, [[128, 16], [1, 128]]), 'cstg')
    S.dma('sp', cstg[24:48, :], bass.AP(conv_b.tensor, 0, [[128, 24], [1, 128]]), 'cstg')
    S.group_last('cstg', 3)
    S.trs([(pf(0)[:, 0:48], cstg[0:48, :])], identf[0:48, 0:48])
    S.copy('dve', cpar[:, 0:48], pf(0)[:, 0:48])
    S.dma('sp', cstg[0:96, :], bass.AP(conv_w.tensor, 0, [[128, 96], [1, 128]]), 'cstg')
    S.trs([(pf(0)[:, 0:96], cstg[0:96, :])], identf[0:96, 0:96])
    S.copy('dve', cpar[:, 48:144], pf(0)[:, 0:96])
    for c3 in range(3):
        S.dma('sp', cstg[0:96, :], bass.AP(st_conv.tensor, c3 * 96 * 128, [[128, 96], [1, 128]]), 'cstg')
        S.trs([(pf(0)[:, 0:96], cstg[0:96, :])], identf[0:96, 0:96])
        cf = convst.rearrange("p s j c -> p (s j c)")
        S.copy('dve', cf[:, c3 * 96:(c3 + 1) * 96], pf(0)[:, 0:96])
    cstop(-4)
    EXT = ('ext', 0, 1, 0, 1)
    S.dma('sp', xt[2][0:16, 0:513], rel_bias, 'xt2')
    S.copy('dve', xt[1][0:16, 0:513], xt[2][0:16, 0:513])
    rb512 = xt[2][0:16, 512:513]
    S.copy('dve', xt[1][0:16, 513:1024], bc(rb512, [[rb512.ap[0][0], 16], [0, 511]]))
    S.dma('sp', ext, xt[1][0:16, 0:1024], 'xt1', extra_w=[EXT])
    stgv = bc(xt[2], [[pstr(xt[2]), 128], [128, AH], [1, 128]])
    for t in range(5):
        src = bass.AP(ext.tensor, 641 - 128 * t, [[1, 128], [1024, AH], [1, 128]])
        S.dma('sp', stgv, src, 'xt2', extra_r=[EXT])
        S.copy('dve', biasF[:, :, t, :], stgv)
    S.memset(P, biasF[64:128, :, 0, 64:128], NEG)
    S.memset(P, biasF[0:64, :, 4, 0:64], NEG)
    cstop(-5)
    S.act(a_bc, a_bc, AF.Exp)
    S.ts('dve', a_bc, a_bc, -1.0, None, ALU.mult)
    S.ts('dve', convw, convw, 0.5, None, ALU.mult)
    S.ts('dve', convb, convb, 0.5, None, ALU.mult)

    S.tracked['cst'] = 'const'

    uses = []
    tiles = [('p', i) for i in range(NTP)] + [('s', 0), ('s', 1)]
    per_tile = ['q0', 'q1', 'k0', 'k1', 'v0', 'v1', 'g0', 'g1', 'ga0', 'ga1', 'dt',
                'x0', 'x1', 'x2', 'x3', 'x4', 'x5', 'wa0', 'wa1', 'z0', 'z1', 'z2', 'z3', 'gs0', 'gs1',
                'ws0', 'ws1', 'ws2', 'ws3', 'wo0', 'wo1']
    for _ in tiles:
        uses += per_tile
    wstate = {'issued': 0, 'cur': 0}

    def w_issue(upto):
        while wstate['issued'] < min(upto, len(uses)):
            u = wstate['issued']
            name = uses[u]
            b = WB[name]
            slot = u % NSLOT
            n = b['kc'] * b['ncols']
            dst = bc(Wr[slot], [[pstr(Wr[slot]), 128], [1, n]])
            srcv = bc(wscr[b['idx']], [[4096, 128], [1, n]])
            S.dma('sp', dst, srcv, 'W%d' % slot, extra_r=[('wscr', b['idx'], b['idx'] + 1, 0, 1)])
            wstate['issued'] += 1

    def w_use(name, issue=True):
        u = wstate['cur']
        assert uses[u] == name, (uses[u], name)
        if issue:
            w_issue(u + NSLOT)
        wstate['cur'] += 1
        b = WB[name]
        slot = u % NSLOT
        return bc(Wr[slot], [[pstr(Wr[slot]), 128], [b['ncols'], b['kc']], [1, b['ncols']]])

    def xrows(kind, idx, sub, L):
        if kind == 'p':
            r0 = idx * 256 + sub * 128
            return x_p[r0:r0 + L, :], y_p[r0:r0 + L, :]
        seq = idx * 2 + sub
        return x_s[seq * 64:seq * 64 + 64, :], y_s[seq * 64:seq * 64 + 64, :]

    def load_x(ti):
        kind, idx = tiles[ti]
        L = 128 if kind == 'p' else 64
        for sub in range(2):
            b = (ti % 2) * 2 + sub
            src, _ = xrows(kind, idx, sub, L)
            S.dma('sp', xt[b][0:L, :], src, 'xt%d' % b)

    def rms_rstd(src_sum, n, eps, dst, L, col=1):
        S.ts('dve', dst, src_sum, 1.0 / n, eps, ALU.mult, ALU.add)
        S.tt('pool', dst, dst, neghalf[0:L, 0:col], ALU.pow)

    def fm_block(wv, ncols_tiles, T, evac):
        for et in range(ncols_tiles):
            bk = nb('mm')
            out = pf(bk)[:, 0:T]
            S.mms([dict(out=out, lhsT=wv[:, kc, et * 128:(et + 1) * 128], rhs=xnT[:, kc, 0:T],
                        start=(kc == 0), stop=(kc == 7)) for kc in range(8)])
            evac(et, out)

    def tm_block(wv, ncols, L, subs, evac):
        for sub in subs:
            bk = nb('mm')
            out = pf(bk)[0:L, 0:ncols]
            S.mms([dict(out=out, lhsT=xnT[:, kc, sub * L:(sub + 1) * L], rhs=wv[:, kc, 0:ncols],
                        start=(kc == 0), stop=(kc == 7)) for kc in range(8)])
            evac(sub, out)

    thp = [0]

    def nth():
        thp[0] += 1
        return TH[thp[0] % 3]

    def store_state(dst_ap):
        for q in range(4):
            bk = nb('mm')
            S.trs([(pf(bk)[:, j * 128:(j + 1) * 128], hT[:, (q * 4 + j) * 128:(q * 4 + j + 1) * 128])
                   for j in range(4)], identf)
            S.copy('dve', stT[:, q * 4:(q + 1) * 4, :], pf(bk)[:, 0:512].rearrange("p (a b) -> p a b", a=4))
        S.dma('pool', bass.AP(dst_ap.tensor, dst_ap.offset, [[128, 128], [128 * 128, 16], [1, 128]]), stT, 'stT')

    def load_state(seq):
        src = st_ssm[seq]
        if 'nold' not in _DBG:
            S.dma('pool' if 'poolld' in _DBG else 'sp', stT,
                  bass.AP(src.tensor, src.offset, [[128, 128], [128 * 128, 16], [1, 128]]), 'stT')
        if 'notr' in _DBG:
            return
        for q in range(4):
            bk = nb('mm')
            S.trs([(pf(bk)[:, j * 128:(j + 1) * 128], stT[:, q * 4 + j, :]) for j in range(4)], identf)
            if 'nocp' in _DBG:
                continue
            S.copy('dve', hT[:, q * 512:(q + 1) * 512], pf(bk)[:, 0:512])
            if 'noact' in _DBG:
                continue
            S.copy('act', hTb[:, q * 512:(q + 1) * 512], pf(bk)[:, 0:512])

    def norm_tile(tj):
        kind_, idx_ = tiles[tj]
        L = 128 if kind_ == 'p' else 64
        xb = [xt[(tj % 2) * 2 + sub] for sub in range(2)]
        for sub in range(2):
            ssx = smallf[0:L, 0:1]
            S.act(junk[0:L, :], xb[sub][0:L, :], AF.Square, accum_out=ssx)
            rms_rstd(ssx, 1024.0, EPS, smallf[0:L, 1:2], L)
            S.act(xsn[0:L, :], xb[sub][0:L, :], AF.Identity, scale=smallf[0:L, 1:2])
            bk = nb('tp')
            S.trs([(pbf(bk)[:, kc * L:(kc + 1) * L], xsn[0:L, kc * 128:(kc + 1) * 128]) for kc in range(8)],
                  identb[0:L, 0:L])
            S.tt('dve', xnT[:, :, sub * L:(sub + 1) * L], pbf(bk)[:, 0:8 * L].rearrange("p (a b) -> p a b", a=8),
                 bc(normw, [[pstr(normw), 128], [1, 8], [0, L]]), ALU.mult)


    def stop(n, ti):
        if _STOP == (n, ti):
            raise _StopBuild()

    load_x(0)
    norm_tile(0)
    for ti, (kind, idx) in enumerate(tiles):
        stop(0, ti)
        L = 128 if kind == 'p' else 64
        T = 2 * L
        last_p = (kind == 'p' and idx == NTP - 1)
        def kv_out_row(sub):
            if kind == 'p':
                r0 = idx * 256 + sub * 128
                if r0 >= SEQ - KEEP:
                    return r0 - (SEQ - KEEP)
                return None
            return 448
        if ti + 1 < len(tiles):
            load_x(ti + 1)
        xb = [xt[(ti % 2) * 2 + sub] for sub in range(2)]

        stop(1, ti)
        if kind == 'p':
            g0 = idx * 2
            slot0 = g0 % 6
        for qi in range(2):
            wv = w_use('q%d' % qi)

            def ev(et, out, qi=qi):
                S.act(QT[:, qi * 4 + et, 0:T], out, AF.Identity, scale=0.125)
            fm_block(wv, 4, T, ev)
        stop(1.2, ti)
        for ki in range(2):
            wv = w_use('k%d' % ki)

            def ev(et, out, ki=ki):
                pair = ki * 4 + et
                if kind == 'p':
                    S.copy('act', KT[:, pair, slot0:slot0 + 2, :], out.rearrange("p (s l) -> p s l", s=2))
                else:
                    S.copy('dve', KTs[:, pair, :, :], out.rearrange("p (s l) -> p s l", s=2))
            fm_block(wv, 4, T, ev)
            subs = [s for s in range(2) if kv_out_row(s) is not None]

            def evk(sub, out, ki=ki):
                st = kvst[(ki + sub) % 2]
                S.copy('dve', st[0:L, :], out)
                r = kv_out_row(sub)
                if kind == 'p':
                    S.dma('pool', k_p[r:r + L, ki * 512:(ki + 1) * 512], st[0:L, :], 'kvst%d' % ((ki + sub) % 2))
                else:
                    seq = idx * 2 + sub
                    S.dma('pool', k_s[seq, 448:512, ki * 512:(ki + 1) * 512], st[0:L, :],
                          'kvst%d' % ((ki + sub) % 2))
            if subs:
                tm_block(wv, 512, L, subs, evk)
        stop(1.4, ti)
        if kind == 's':
            S.memset('pool', Vs[:, :, :, 64:65], 1.0)
        for vi in range(2):
            wv = w_use('v%d' % vi)

            def evv(sub, out, vi=vi):
                if kind == 'p':
                    dst = Vr[0:L, slot0 + sub, vi * 8:(vi + 1) * 8, 0:64]
                else:
                    dst = Vs[0:L, sub, vi * 8:(vi + 1) * 8, 0:64]
                S.copy('act', dst, out.rearrange("p (h d) -> p h d", h=8))
                r = kv_out_row(sub)
                if r is not None:
                    st = kvst[(vi + sub) % 2]
                    S.copy('dve', st[0:L, :], out)
                    if kind == 'p':
                        S.dma('pool', v_p[r:r + L, vi * 512:(vi + 1) * 512], st[0:L, :],
                              'kvst%d' % ((vi + sub) % 2))
                    else:
                        seq = idx * 2 + sub
                        S.dma('pool', v_s[seq, 448:512, vi * 512:(vi + 1) * 512], st[0:L, :],
                              'kvst%d' % ((vi + sub) % 2))
            tm_block(wv, 512, L, [0, 1], evv)
        stop(1.6, ti)
        for gi in range(2):
            wv = w_use('g%d' % gi)

            def evg(sub, out, gi=gi):
                th = nth()
                S.act(th[0:L, :], out, AF.Tanh, scale=0.5)
                S.stt('dve', gs2[0:L, sub, gi * 512:(gi + 1) * 512], th[0:L, :], 1.0, out, ALU.add, ALU.mult)
            tm_block(wv, 512, L, [0, 1], evg)
        stop(1.8, ti)
        for gi in range(2):
            wv = w_use('ga%d' % gi)

            def evga(et, out, gi=gi):
                th = nth()
                S.act(th[:, 0:T], out, AF.Tanh, scale=0.5)
                S.act(ga_sig[:, gi * 4 + et, 0:T], th[:, 0:T], AF.Identity, scale=0.25, bias=0.25)
            fm_block(wv, 4, T, evga)

        stop(1.9, ti)
        wv = w_use('dt')

        def evdt(sub, out):
            S.tt('dve', dtmp[0:L, :], out, dtb_bc[0:L, :], ALU.add)
            S.act(dtmp[0:L, :], dtmp[0:L, :], AF.Exp)
            S.act(dtt[0:L, sub, :], dtmp[0:L, :], AF.Ln, bias=1.0)
            S.tt('dve', adt[0:L, sub, :], dtt[0:L, sub, :], a_bc[0:L, :], ALU.mult)
            S.copy('dve', adt_hi[0:L, sub, :], adt[0:L, sub, :])
            S.tt('dve', adt_lo[0:L, sub, :], adt[0:L, sub, :], adt_hi[0:L, sub, :], ALU.subtract)
        tm_block(wv, 32, L, [0, 1], evdt)

        if kind == 'p':
            segs = [(0, T)]
            SW = T + 3
            S.copy('pool', xraw[:, :, 0:3], carry)
        else:
            SW = 67
            segs = [(0, 64), (67, 64)]
            for s in range(2):
                S.copy('pool', xraw[:, :, s * 67:s * 67 + 3], convst[:, idx * 2 + s].rearrange("p j c -> p c j"))
        for xi in range(6):
            wv = w_use('x%d' % xi)

            def evx(et, out, xi=xi):
                ct = xi * 4 + et
                if kind == 'p':
                    S.copy('act', xraw[:, ct, 3:3 + T], out)
                else:
                    S.copy('dve', xraw[:, ct, 0:134].rearrange("p (s c) -> p s c", c=67)[:, :, 3:67],
                           out.rearrange("p (s c) -> p s c", c=64))
            fm_block(wv, 4, T, evx)
            if last_p or kind == 's':
                osubs = [1] if kind == 'p' else [0, 1]

                def evc(sub, out, xi=xi):
                    st = xbcst[(xi + sub) % 2]
                    S.copy('dve', st[L - 32:L, :], out[L - 32:L, :])
                    dst = conv_p if kind == 'p' else conv_s[idx * 2 + sub]
                    S.dma('pool', dst[:, xi * 512:(xi + 1) * 512], st[L - 3:L, :], 'xbcst%d' % ((xi + sub) % 2))
                tm_block(wv, 512, L, osubs, evc)
        def conv_block(xj):
            accs = [cacc[0], cacc[1], xbcst[0][:, 0:256], xbcst[1][:, 0:256]]
            for (c0, nt) in segs:
                cts = [xj * 4 + e for e in range(4)]
                for e, ct in enumerate(cts):
                    S.ts('dve', accs[e][:, 0:nt], xraw[:, ct, c0:c0 + nt], convw[:, 0, ct:ct + 1],
                         convb[:, ct:ct + 1], ALU.mult, ALU.add)
                for j in range(1, 4):
                    for e, ct in enumerate(cts):
                        S.stt('dve', accs[e][:, 0:nt], xraw[:, ct, c0 + j:c0 + j + nt],
                              convw[:, j, ct:ct + 1], accs[e][:, 0:nt], ALU.mult, ALU.add)
                if kind == 'p' and c0 == 0:
                    S.copy('pool', carry[:, xj * 4:(xj + 1) * 4, :], xraw[:, xj * 4:(xj + 1) * 4, T:T + 3])
                ths = []
                for e, ct in enumerate(cts):
                    th = nth()
                    ths.append(th)
                    S.act(th[:, 0:nt], accs[e][:, 0:nt], AF.Tanh)
                    if e >= 1:
                        S.stt('dve', xraw[:, cts[e - 1], c0:c0 + nt], ths[e - 1][:, 0:nt], 1.0,
                              accs[e - 1][:, 0:nt], ALU.add, ALU.mult)
                S.stt('dve', xraw[:, cts[3], c0:c0 + nt], ths[3][:, 0:nt], 1.0, accs[3][:, 0:nt],
                      ALU.add, ALU.mult)


        conv_todo = [0, 1, 2, 3, 4, 5]
        stop(2, ti)
        for sub in range(2):
            if kind == 'p':
                g = idx * 2 + sub
                tl = [t for t in range(5) if g - 4 + t >= 0]
            else:
                seq = idx * 2 + sub
                tl = [0, 1, 2, 3] if 'not4' in _DBG else ([4] if 'only4' in _DBG else [0, 1, 2, 3, 4])
                kv32 = bc(kvst[0], [[pstr(kvst[0]), 128], [1, 1024]])
                for tt_ in range(4):
                    S.dma('sp', kv32, cv[seq, tt_ * 128:(tt_ + 1) * 128, :], 'kvst0')
                    S.copy('dve', Vr[:, tt_, :, 0:64], kv32.rearrange("p (h d) -> p h d", h=AH))
                    S.dma('sp', ost, ck[seq, tt_ * 128:(tt_ + 1) * 128, :], 'ost')
                    S.copy('act', ckst[:, tt_, :], ost)
                    bk = nb('tp')
                    S.trs([(pbf(bk)[:, pr * 128:(pr + 1) * 128], ckst[:, tt_, pr * 128:(pr + 1) * 128])
                           for pr in range(8)], identb)
                    S.copy('dve', KT[:, :, tt_, :], pbf(bk)[:, 0:1024].rearrange("p (a b) -> p a b", a=8))
                S.dma('pool', k_s[seq, 0:448, :], ck[seq, 64:512, :], 'd2d')
                S.dma('pool', v_s[seq, 0:448, :], cv[seq, 64:512, :], 'd2d')
            stop(2.2, ti)
            for ob in OB:
                S.memset('dve', pf(ob)[0:L, :], 0.0)
            blocks = [(h, t) for h in range(AH) for t in tl]
            per = 512 // L
            groups = [blocks[i:i + per] for i in range(0, len(blocks), per)]
            pend = None

            def emit_pv(pg):
                grp, pt = pg
                if 'nopv' in _DBG:
                    return
                items = []
                for cb, (h, t) in enumerate(grp):
                    Mk = 64 if (kind == 's' and t == 4) else 128
                    if kind == 'p':
                        vt = Vr[0:Mk, (g - 4 + t) % 6, h, 0:65]
                    elif t < 4:
                        vt = Vr[0:Mk, t, h, 0:65]
                    else:
                        vt = Vs[0:Mk, sub, h, 0:65]
                    ob = OB[h // 6]
                    oc = (h % 6) * 65
                    items.append(dict(out=pf(ob)[0:L, oc:oc + 65], lhsT=pt[0:Mk, cb * L:(cb + 1) * L], rhs=vt,
                                      start=False, stop=True))
                S.mms(items)
            for gi_, grp in enumerate(groups):
                sbk = nb('S')
                items = []
                for cb, (h, t) in enumerate(grp):
                    pair, pb_ = h // 2, (h % 2) * 64
                    Mk = 64 if (kind == 's' and t == 4) else 128
                    out = pf(sbk)[0:Mk, cb * L:(cb + 1) * L]
                    if Mk == 128:
                        items.append(dict(out=out, lhsT=Jx, rhs=biasF[:, h, t, 0:L], start=True, stop=False))
                    else:
                        items.append(dict(out=out, lhsT=Jx[:, 0:64], rhs=biasF[:, h, t, 0:L],
                                          start=True, stop=False))
                    if kind == 'p':
                        kt = KT[pb_:pb_ + 64, pair, (g - 4 + t) % 6, 0:Mk]
                    elif t < 4:
                        kt = KT[pb_:pb_ + 64, pair, t, 0:Mk]
                    else:
                        kt = KTs[pb_:pb_ + 64, pair, sub, 0:Mk]
                    items.append(dict(out=out, lhsT=kt, rhs=QT[pb_:pb_ + 64, pair, sub * L:(sub + 1) * L],
                                      start=False, stop=True))
                S.mms(items)
                pt = PT[gi_ % 3]
                ncol = len(grp) * L
                S.act(pt[:, 0:ncol], pf(sbk)[:, 0:ncol], AF.Exp)
                if pend is not None:
                    emit_pv(pend)
                pend = (grp, pt)
                if conv_todo and (gi_ % 6 == 5 or len(groups) < 12):
                    conv_block(conv_todo.pop(0))
            emit_pv(pend)
            stop(2.4, ti)
            for obi, ob in enumerate(OB):
                h0 = obi * 6
                nh_ = min(6, AH - h0)
                ov = pf(ob)[0:L, 0:nh_ * 65].rearrange("p (h d) -> p h d", h=nh_)
                S.recip(rs[0:L, h0:h0 + nh_], ov[:, :, 64])
                S.tt('dve', o_n[0:L, h0 * 64:(h0 + nh_) * 64].rearrange("p (h d) -> p h d", h=nh_), ov[:, :, 0:64],
                     bc(rs[0:L, h0:h0 + nh_], [[pstr(rs), L], [1, nh_], [0, 64]]), ALU.mult)
            S.tt('pool', og[0:L, :], o_n[0:L, :], gs2[0:L, sub, :], ALU.mult)
            bk = nb('tp')
            S.trs([(pbf(bk)[:, kc * L:(kc + 1) * L], og[0:L, kc * 128:(kc + 1) * 128]) for kc in range(8)],
                  identb[0:L, 0:L])
            S.copy('dve', ogT[:, :, sub * L:(sub + 1) * L], pbf(bk)[:, 0:8 * L].rearrange("p (a b) -> p a b", a=8))

        while conv_todo:
            conv_block(conv_todo.pop(0))
        stop(3, ti)
        stop(4, ti)
        for wi in range(2):
            wv = w_use('wa%d' % wi)
            for et in range(4):
                bk = nb('mm')
                out = pf(bk)[:, 0:T]
                S.mms([dict(out=out, lhsT=wv[:, kc, et * 128:(et + 1) * 128], rhs=ogT[:, kc, 0:T],
                            start=(kc == 0), stop=(kc == 7)) for kc in range(8)])
                S.tt('dve', accT[:, wi * 4 + et, 0:T], out, ga_sig[:, wi * 4 + et, 0:T], ALU.mult)

        stop(5, ti)
        for zi in range(4):
            wv = w_use('z%d' % zi)

            def evz(sub, out, zi=zi):
                th = nth()
                S.act(th[0:L, :], out, AF.Tanh, scale=0.5)
                S.stt('dve', zs2[0:L, sub, zi * 512:(zi + 1) * 512], th[0:L, :], 1.0, out, ALU.add, ALU.mult)
            tm_block(wv, 512, L, [0, 1], evz)
        for gi in range(2):
            wv = w_use('gs%d' % gi)

            def evgs(et, out, gi=gi):
                th = nth()
                S.act(th[:, 0:T], out, AF.Tanh, scale=0.5)
                S.act(gs_sig[:, gi * 4 + et, 0:T], th[:, 0:T], AF.Identity, scale=0.5, bias=0.5)
            fm_block(wv, 4, T, evgs)

        if ti + 1 < len(tiles):
            norm_tile(ti + 1)
        stop(6, ti)
        def cview(t, n):
            return bc(t, [[pstr(t), L], [L, n], [1, L]])
        Bmh_v, Bml_v, Eb_v, MT_v, cb_v = cview(Bmh, 8), cview(Bml, 8), cview(Eb, 8), cview(MTb, 8), cview(cbTm, 4)
        for sub in range(2):
            if kind == 'p':
                cs = sub * 128
            else:
                cs = sub * 67
                load_state(idx * 2 + sub)
            stop(6.2, ti)
            for hb in range(2):
                bk = nb('tp')
                S.trs([(pbf(bk)[0:L, j * 128:(j + 1) * 128], xraw[:, hb * 8 + j, cs:cs + L]) for j in range(8)],
                      identb)
                S.copy('act' if hb else 'dve', xs_tok[0:L, hb * 1024:(hb + 1) * 1024], pbf(bk)[0:L, 0:1024])
            bk = nb('tp')
            S.trs([(pbf(bk)[0:L, j * 128:(j + 1) * 128], xraw[:, 16 + j, cs:cs + L]) for j in range(4)], identb)
            S.copy('dve', B_tok[0:L, :], pbf(bk)[0:L, 0:512])
            stop(6.4, ti)
            sm = pf(SMB)
            ah, al = adt_hi[0:L, sub, :], adt_lo[0:L, sub, :]
            S.mms([dict(out=sm[0:L, 0:32], lhsT=tri[0:L, 0:L], rhs=ah, start=True, stop=False),
                   dict(out=sm[0:L, 0:32], lhsT=tri[0:L, 0:L], rhs=al, start=False, stop=True),
                   dict(out=sm[0:L, 32:64], lhsT=SU[0:L, 0:L], rhs=ah, start=True, stop=False),
                   dict(out=sm[0:L, 32:64], lhsT=SU[0:L, 0:L], rhs=al, start=False, stop=True),
                   dict(out=sm[:, 64:96], lhsT=onesb[0:L, :], rhs=ah, start=True, stop=False),
                   dict(out=sm[:, 64:96], lhsT=onesb[0:L, :], rhs=al, start=False, stop=True)])
            S.act(ee[0:L, :], sm[0:L, 0:64], AF.Exp)
            S.act(ee2, sm[:, 64:96], AF.Exp)
            cbb = pf(CBB)
            S.mms([dict(out=cbb[0:L, gq * L:(gq + 1) * L], lhsT=xraw[:, 16 + gq, cs:cs + L],
                        rhs=xraw[:, 20 + gq, cs:cs + L], start=True, stop=True) for gq in range(4)])
            S.tt('dve', cb_v, cbb[0:L, 0:4 * L].rearrange("p (g l) -> p g l", g=4),
                 bc(tri, [[pstr(tri), L], [0, 4], [1, L]]), ALU.mult)
            stop(6.6, ti)
            def b8(ap2):
                return bc(ap2, [[ap2.ap[0][0], L], [1, 8], [0, 64]])
            x3 = lambda a: a[0:L, :].rearrange("p (r d) -> p r d", r=8)
            nhalf = 2 if L == 128 else 1
            hh = 8 // nhalf

            def s0_pool(gq):
                xdt, xdt2, xsd = xdtb[gq % 2], xdt2b[gq % 2], xsdb[gq % 2]
                xsg = xs_tok[0:L, gq * 512:(gq + 1) * 512].rearrange("p (r d) -> p r d", r=8)
                S.tt('pool', x3(xdt), xsg, b8(dtt[0:L, sub, gq * 8:(gq + 1) * 8]), ALU.mult)
                S.tt('pool', x3(xdt2), x3(xdt), b8(ee[0:L, 32 + gq * 8:32 + (gq + 1) * 8]), ALU.mult)
                S.tt('pool', x3(xsd), xsg, b8(dsk_bc[0:L, gq * 8:(gq + 1) * 8]), ALU.mult)

            def s1a(gq):
                for (src, dst) in ((adt_hi, Bmh_v), (adt_lo, Bml_v)):
                    a8 = src[0:L, sub, gq * 8:(gq + 1) * 8]
                    S.tt('dve', dst, bc(a8, [[a8.ap[0][0], L], [1, 8], [0, L]]),
                         bc(tri, [[pstr(tri), L], [0, 8], [1, L]]), ALU.mult)

            def s1b(gq):
                items = []
                for hf in range(nhalf):
                    out = pf(SEGB[hf])[0:L, 0:hh * L]
                    items.append(dict(out=out, lhsT=SU[0:L, 0:L], rhs=Bmh_v[:, hf * hh:(hf + 1) * hh, :],
                                      start=True, stop=False))
                    items.append(dict(out=out, lhsT=SU[0:L, 0:L], rhs=Bml_v[:, hf * hh:(hf + 1) * hh, :],
                                      start=False, stop=True))
                S.mms(items)
                for hf in range(nhalf):
                    S.act(Eb_v[:, hf * hh:(hf + 1) * hh, :],
                          pf(SEGB[hf])[0:L, 0:hh * L].rearrange("p (r l) -> p r l", r=hh), AF.Exp)

            def s2(gq):
                xdt, xdt2, xsd = xdtb[gq % 2], xdt2b[gq % 2], xsdb[gq % 2]
                cg = cb_v[:, gq, :]
                S.tt('dve', MT_v, Eb_v, bc(cg, [[cg.ap[0][0], L], [0, 8], [1, L]]), ALU.mult)
                yb = pf(YB)
                S.mms([dict(out=yb[0:L, 0:512], lhsT=xraw[:, 20 + gq, cs:cs + L], rhs=hTb[:, gq * 512:(gq + 1) * 512],
                            start=True, stop=False)])
                S.tt('dve', yb[0:L, 0:512].rearrange("p (r d) -> p r d", r=8),
                     yb[0:L, 0:512].rearrange("p (r d) -> p r d", r=8), b8(ee[0:L, gq * 8:(gq + 1) * 8]), ALU.mult)
                items = [dict(out=yb[0:L, 0:512], lhsT=identb[0:L, 0:L], rhs=xsd[0:L, :], start=False, stop=False)]
                for r in range(8):
                    items.append(dict(out=yb[0:L, r * 64:(r + 1) * 64], lhsT=MT_v[:, r, :],
                                      rhs=xdt[0:L, r * 64:(r + 1) * 64], start=False, stop=(r == 7)))
                S.mms(items)
                ub = pf(UB)
                S.mms([dict(out=ub[:, 0:512], lhsT=B_tok[0:L, gq * 128:(gq + 1) * 128], rhs=xdt2[0:L, :],
                            start=True, stop=True)])

            def s3(gq):
                yb = pf(YB)
                ub = pf(UB)
                S.tt('dve', yz[0:L, gq, :], yb[0:L, 0:512], zs2[0:L, sub, gq * 512:(gq + 1) * 512], ALU.mult)
                S.act(junk[0:L, 0:512], yz[0:L, gq, :], AF.Square, accum_out=ssq[0:L, gq:gq + 1])
                hg = hT[:, gq * 512:(gq + 1) * 512]
                e8 = ee2[:, gq * 8:(gq + 1) * 8]
                S.tt('pool', hg.rearrange("p (r d) -> p r d", r=8), hg.rearrange("p (r d) -> p r d", r=8),
                     bc(e8, [[e8.ap[0][0], 128], [1, 8], [0, 64]]), ALU.mult)
                S.tt('dve', hg, hg, ub[:, 0:512], ALU.add)
                S.copy('act', hTb[:, gq * 512:(gq + 1) * 512], hg)

            s0_pool(0)
            s1a(0)
            s1b(0)
            for gq in range(4):
                if gq < 3:
                    s0_pool(gq + 1)
                s2(gq)
                if gq < 3:
                    s1a(gq + 1)
                    s1b(gq + 1)
                s3(gq)
            stop(6.8, ti)
            S.ts('dve', rstd4[0:L, :], ssq[0:L, :], 1.0 / 512.0, 4.0 * EPS, ALU.mult, ALU.add)
            S.tt('pool', rstd4[0:L, :], rstd4[0:L, :], neghalf[0:L, :], ALU.pow)
            S.tt('pool', yz[0:L, :, :], yz[0:L, :, :], bc(rstd4, [[pstr(rstd4), L], [1, 4], [0, 512]]), ALU.mult)
            yzf = yz[0:L, :, :].rearrange("p a b -> p (a b)")
            for hb in range(2):
                bk = nb('tp')
                S.trs([(pbf(bk)[:, j * L:(j + 1) * L], yzf[:, (hb * 8 + j) * 128:(hb * 8 + j + 1) * 128])
                       for j in range(8)], identb[0:L, 0:L])
                nwv = nwssm[:, hb * 8:(hb + 1) * 8]
                S.tt('dve', yzT[:, hb * 8:(hb + 1) * 8, sub * L:(sub + 1) * L],
                     pbf(bk)[:, 0:8 * L].rearrange("p (a b) -> p a b", a=8),
                     bc(nwv, [[nwv.ap[0][0], 128], [1, 8], [0, L]]), ALU.mult)
            stop(6.9, ti)
            if kind == 's':
                store_state(ssm_s[idx * 2 + sub])
            elif last_p and sub == 1:
                store_state(ssm_p)

        stop(7, ti)
        for wi in range(4):
            wv = w_use('ws%d' % wi)
            for et in range(2):
                e8 = wi * 2 + et
                bk = nb('mm')
                out = pf(bk)[:, 0:T]
                S.mms([dict(out=out, lhsT=wv[:, kc, et * 128:(et + 1) * 128], rhs=yzT[:, kc, 0:T],
                            start=(kc == 0), stop=(kc == 15)) for kc in range(16)])
                th = nth()
                S.tt('dve', th[:, 0:T], out, gs_sig[:, e8, 0:T], ALU.mult)
                S.tt('pool', mT[:, e8, 0:T], th[:, 0:T], accT[:, e8, 0:T], ALU.add)

        stop(8, ti)
        wvs = [w_use('wo0'), w_use('wo1', issue=False)]
        for sub in range(2):
            for hf in range(2):
                bk = nb('mm')
                out = pf(bk)[0:L, 0:512]
                S.mms([dict(out=out, lhsT=mT[:, kc, sub * L:(sub + 1) * L], rhs=wvs[hf][:, kc, :],
                            start=(kc == 0), stop=(kc == 7)) for kc in range(8)])
                S.tt('dve', ost[0:L, hf * 512:(hf + 1) * 512], out, xb[sub][0:L, hf * 512:(hf + 1) * 512], ALU.add)
            ssx = smallf[0:L, 2:3]
            S.act(junk[0:L, :], ost[0:L, :], AF.Square, accum_out=ssx)
            rms_rstd(ssx, 1024.0, EPS, smallf[0:L, 3:4], L)
            S.stt('dve', ost[0:L, :], ost[0:L, :], smallf[0:L, 3:4], fnw_bc[0:L, :], ALU.mult, ALU.mult)
            _, dsty = xrows(kind, idx, sub, L)
            S.dma('pool', dsty, ost[0:L, :], 'ost')

    if dbg is not None:
        dbg(S, locals())


_CACHE = {}


def _get_nc(SEQ):
    if SEQ not in _CACHE:
        nc = bass.Bass("TRN2", target_bir_lowering=False)
        build(nc, SEQ)
        _CACHE[SEQ] = nc
    return _CACHE[SEQ]


def kernel(x_prompt, x_sample, state_ssm, state_conv, cache_k, cache_v, norm_w, w_in, conv_w, conv_b, dt_bias,
           a_log, d_skip, ssm_norm_w, w_out_ssm, rel_bias, w_out_att, w_o, final_norm_w):
    f = lambda a: np.ascontiguousarray(np.asarray(a, dtype=np.float32))
    B, SEQ, _ = x_prompt.shape
    assert B == 8 and SEQ % 256 == 0
    KEEP = min(512, SEQ)
    nc = _get_nc(SEQ)
    shared = dict(norm_w=f(norm_w[0]), w_in=f(w_in[0]), conv_w=f(conv_w[0]), conv_b=f(conv_b[0]),
                  dt_bias=f(dt_bias[0]), a_log=f(a_log[0]), d_skip=f(d_skip[0]), ssm_norm_w=f(ssm_norm_w[0]),
                  w_out_ssm=f(w_out_ssm[0]), rel_bias=f(rel_bias[0]), w_out_att=f(w_out_att[0]), w_o=f(w_o[0]),
                  final_norm_w=f(final_norm_w))
    in_maps = []
    for c in range(_NCORES):
        m = dict(shared)
        m['x_p'] = f(x_prompt[c])
        m['x_s'] = f(x_sample[4 * c:4 * c + 4]).reshape(256, D)
        m['st_ssm'] = f(state_ssm[0, 4 * c:4 * c + 4]).reshape(4, 2048, 128)
        m['st_conv'] = f(state_conv[0, 4 * c:4 * c + 4])
        m['ck'] = f(cache_k[0, 4 * c:4 * c + 4]).reshape(4, 512, D)
        m['cv'] = f(cache_v[0, 4 * c:4 * c + 4]).reshape(4, 512, D)
        in_maps.append(m)
    res = run_bass_kernel_spmd(nc, in_maps, core_ids=list(range(_NCORES)))
    R = list(res.results) + [res.results[0]] * (8 - _NCORES)
    y_prompt = np.stack([R[c]['y_p'] for c in range(8)]).astype(np.float32)
    y_sample = np.concatenate([R[c]['y_s'].reshape(4, 64, D) for c in range(8)]).astype(np.float32)
    ssm_p = np.stack([R[c]['ssm_p'].reshape(NH, 64, 128) for c in range(8)])[None].astype(np.float32)
    conv_p = np.stack([R[c]['conv_p'] for c in range(8)])[None].astype(np.float32)
    k_p = np.stack([R[c]['k_p'].reshape(KEEP, AH, 64) for c in range(8)])[None].astype(np.float32)
    v_p = np.stack([R[c]['v_p'].reshape(KEEP, AH, 64) for c in range(8)])[None].astype(np.float32)
    ssm_s = np.concatenate([R[c]['ssm_s'].reshape(4, NH, 64, 128) for c in range(8)])[None].astype(np.float32)
    conv_s = np.concatenate([R[c]['conv_s'] for c in range(8)])[None].astype(np.float32)
    k_s = np.concatenate([R[c]['k_s'].reshape(4, 512, AH, 64) for c in range(8)])[None].astype(np.float32)
    v_s = np.concatenate([R[c]['v_s'].reshape(4, 512, AH, 64) for c in range(8)])[None].astype(np.float32)
    return (y_prompt, y_sample, ssm_p, conv_p, k_p, v_p, ssm_s, conv_s, k_s, v_s)
```

```python
import numpy as np
from contextlib import ExitStack
import concourse.bass as bass
import concourse.mybir as mybir
from concourse.bass_utils import run_bass_kernel_spmd

F32 = mybir.dt.float32
BF16 = mybir.dt.bfloat16
U8 = mybir.dt.uint8
AF = mybir.ActivationFunctionType
ALU = mybir.AluOpType

D = 1024
KC = 8
DI = 2048
CD = 3072
NH = 32
AH = 16
DIN = 11296
C_Z, C_X, C_DT, C_Q, C_K, C_V, C_G, C_GS, C_GA = 0, 2048, 5120, 5152, 6176, 7200, 8224, 9248, 10272
EPS = 1e-5
NEG = -30000.0


def _dsz(dt):
    if dt == F32:
        return 4
    if dt == BF16:
        return 2
    if dt == U8:
        return 1
    raise ValueError(dt)


class _Op:
    __slots__ = ('eng', 'fn', 'deps', 'dma', 'key', 'val', 'seq', 'hasdep')


class Sched:
    def __init__(self, nc, es):
        self.nc = nc
        self.es = es
        self.ops = []
        self.engobj = {'pe': nc.tensor, 'act': nc.scalar, 'dve': nc.vector, 'pool': nc.gpsimd, 'sp': nc.sync}
        self.esem = {k: es.enter_context(nc.semaphore('es_' + k)) for k in self.engobj}
        self.dkeys = {}
        self.tracked = {}
        self.acc = {}

    def track(self, name, const=False):
        self.tracked[name] = 'const' if const else True

    def reg(self, ap):
        name = ap.tensor.name
        tr = self.tracked.get(name)
        if tr is None:
            return None
        dims = ap.ap
        ps = dims[0][0]
        npart = dims[0][1]
        off = ap.offset
        if ps > 0:
            p0 = off // ps
            f0 = off % ps
        else:
            p0 = 0
            f0 = off
        lo = f0
        hi = f0
        for (s, c) in dims[1:]:
            if s >= 0:
                hi += s * (c - 1)
            else:
                lo += s * (c - 1)
        sz = _dsz(ap.dtype)
        return (name, p0, p0 + npart, lo * sz, (hi + 1) * sz)

    def add(self, eng, fn, reads=(), writes=(), dma_key=None):
        i = len(self.ops)
        deps = set()
        rr = []
        ww = []
        for a in reads:
            r = a if isinstance(a, tuple) else self.reg(a)
            if r is not None:
                if r[0].startswith('pb'):
                    ww.append((r[0], 0, 128, 0, 2048))
                else:
                    rr.append(r)
        for a in writes:
            r = a if isinstance(a, tuple) else self.reg(a)
            if r is not None:
                if r[0].startswith('pb'):
                    r = (r[0], 0, 128, 0, 2048)
                ww.append(r)
        for (k, p0, p1, lo, hi) in rr:
            for e in self.acc.get(k, ()):
                if e[5] and e[0] < p1 and p0 < e[1] and e[2] < hi and lo < e[3]:
                    deps.add(e[4])
        for (k, p0, p1, lo, hi) in ww:
            lst = self.acc.get(k)
            if lst is None:
                continue
            keep = []
            for e in lst:
                if e[0] < p1 and p0 < e[1] and e[2] < hi and lo < e[3]:
                    deps.add(e[4])
                    if p0 <= e[0] and e[1] <= p1 and lo <= e[2] and e[3] <= hi:
                        continue
                keep.append(e)
            self.acc[k] = keep
        deps.discard(i)
        for (k, p0, p1, lo, hi) in rr:
            if self.tracked.get(k) == 'const':
                continue
            self.acc.setdefault(k, []).append([p0, p1, lo, hi, i, False])
        for (k, p0, p1, lo, hi) in ww:
            self.acc.setdefault(k, []).append([p0, p1, lo, hi, i, True])
        op = _Op()
        op.eng = eng
        op.fn = fn
        op.deps = deps
        op.dma = dma_key is not None
        op.key = dma_key
        op.val = 0
        op.seq = 0
        op.hasdep = False
        if op.dma:
            ent = self.dkeys.get(dma_key)
            if ent is None:
                ent = [self.es.enter_context(self.nc.semaphore('d_%d' % len(self.dkeys))), 0, []]
                self.dkeys[dma_key] = ent
            ent[1] += 16
            op.val = ent[1]
            ent[2].append(i)
        self.ops.append(op)
        return i

    def group_barrier(self, key):
        ent = self.dkeys[key]
        for i in ent[2]:
            self.ops[i].val = ent[1]

    def group_last(self, key, n):
        ent = self.dkeys[key]
        for i in ent[2][-n:]:
            self.ops[i].val = ent[1]

    def emit(self):
        ops = self.ops
        for op in ops:
            for d in op.deps:
                ops[d].hasdep = True
        cnt = {k: 0 for k in self.engobj}
        for op in ops:
            if not op.dma and op.hasdep:
                cnt[op.eng] += 1
                op.seq = cnt[op.eng]
        waited = {k: {} for k in self.engobj}
        nw = 0
        for op in ops:
            eng = self.engobj[op.eng]
            wd = waited[op.eng]
            need = {}
            for d in op.deps:
                dop = ops[d]
                if dop.dma:
                    sem, val = self.dkeys[dop.key][0], dop.val
                else:
                    if dop.eng == 'pe' and op.eng == 'pe' and not op.dma:
                        continue
                    sem, val = self.esem[dop.eng], dop.seq
                k = id(sem)
                if need.get(k, (None, 0))[1] < val:
                    need[k] = (sem, val)
            for k, (sem, val) in need.items():
                if wd.get(k, 0) >= val:
                    continue
                eng.wait_ge(sem, val)
                wd[k] = val
                nw += 1
            inst = op.fn(eng)
            if op.dma:
                inst.then_inc(self.dkeys[op.key][0], 16)
            elif op.hasdep:
                inst.then_inc(self.esem[op.eng], 1)
        sp = self.engobj['sp']
        for key, ent in self.dkeys.items():
            sp.wait_ge(ent[0], ent[1])
        return len(ops), nw

    def mms(self, items):
        reads = []
        writes = []
        for it in items:
            reads += [it['lhsT'], it['rhs']]
            writes.append(it['out'])
            if not it['start']:
                reads.append(it['out'])

        def fn(e, items=items):
            inst = None
            for it in items:
                inst = e.matmul(it['out'], lhsT=it['lhsT'], rhs=it['rhs'], start=it['start'], stop=it['stop'],
                                skip_group_check=True)
            return inst
        return self.add('pe', fn, reads, writes)

    def trs(self, items, ident):
        reads = [ident] + [x[1] for x in items]
        writes = [x[0] for x in items]

        def fn(e, items=items, ident=ident):
            inst = None
            for (o, i) in items:
                inst = e.transpose(o, i, ident)
            return inst
        return self.add('pe', fn, reads, writes)

    def act(self, out, in_, func, bias=None, scale=None, accum_out=None, extra_w=()):
        reads = [in_]
        kw = {}
        if bias is not None:
            kw['bias'] = bias
            if not isinstance(bias, (int, float)):
                reads.append(bias)
        if scale is not None:
            kw['scale'] = scale
            if not isinstance(scale, (int, float)):
                reads.append(scale)
        writes = [out] + list(extra_w)
        if accum_out is not None:
            kw['accum_out'] = accum_out
            writes.append(accum_out)
        return self.add('act', lambda e: e.activation(out=out, in_=in_, func=func, **kw), reads, writes)

    def tt(self, eng, out, in0, in1, op):
        return self.add(eng, lambda e: e.tensor_tensor(out=out, in0=in0, in1=in1, op=op), [in0, in1], [out])

    def ts(self, eng, out, in0, s1, s2, op0, op1=None):
        reads = [in0]
        if not isinstance(s1, (int, float)):
            reads.append(s1)
        if s2 is not None and not isinstance(s2, (int, float)):
            reads.append(s2)
        if op1 is None:
            return self.add(eng, lambda e: e.tensor_scalar(out=out, in0=in0, scalar1=s1, scalar2=None, op0=op0),
                            reads, [out])
        return self.add(eng, lambda e: e.tensor_scalar(out=out, in0=in0, scalar1=s1, scalar2=s2, op0=op0, op1=op1),
                        reads, [out])

    def stt(self, eng, out, in0, scalar, in1, op0, op1):
        reads = [in0, in1]
        if not isinstance(scalar, (int, float)):
            reads.append(scalar)
        return self.add(eng, lambda e: e.scalar_tensor_tensor(out=out, in0=in0, scalar=scalar, in1=in1, op0=op0,
                                                              op1=op1), reads, [out])

    def copy(self, eng, out, in_):
        if eng == 'act':
            return self.add('act', lambda e: e.activation(out=out, in_=in_, func=AF.Identity), [in_], [out])
        return self.add(eng, lambda e: e.tensor_copy(out=out, in_=in_), [in_], [out])

    def memset(self, eng, ap, val):
        return self.add(eng, lambda e: e.memset(ap, val), [], [ap])

    def recip(self, out, in_):
        return self.add('dve', lambda e: e.reciprocal(out=out, in_=in_), [in_], [out])

    def dma(self, q, out, in_, key, extra_r=(), extra_w=(), slow=False):
        if slow:
            fn = lambda e: e.dma_start(out=out, in_=in_, allow_slow_non_contiguous=True)
        else:
            fn = lambda e: e.dma_start(out=out, in_=in_)
        return self.add(q, fn, [in_] + list(extra_r), [out] + list(extra_w), dma_key=key)


def bc(ap, dims, off=0):
    return bass.AP(ap.tensor, ap.offset + off, dims)


def pstr(ap):
    return ap.ap[0][0]


class Arena:
    def __init__(self, nc, es, S, name, nbytes):
        self.t = es.enter_context(nc.sbuf_tensor(name, [128, nbytes], U8))
        self.off = 0
        self.cap = nbytes
        self.name = name
        S.track(name)

    def alloc(self, free_shape, dt):
        n = int(np.prod(free_shape)) * _dsz(dt)
        off = (self.off + 31) // 32 * 32
        assert off + n <= self.cap, (self.name, off, n, self.cap)
        self.off = off + n
        v = self.t[:, off:off + n].bitcast(dt)
        if len(free_shape) == 2:
            v = v.rearrange("p (a b) -> p a b", a=free_shape[0])
        elif len(free_shape) == 3:
            v = v.rearrange("p (a b c) -> p a b c", a=free_shape[0], b=free_shape[1])
        return v


_STOP = (99, 0)
_NCORES = 8
_DBG = ''
_ENV = {}
_DUMP = None


class _StopBuild(Exception):
    pass


def build(nc, SEQ, dbg=None):
    es = ExitStack()
    S = Sched(nc, es)
    try:
        _body(nc, SEQ, es, S, dbg)
    except _StopBuild:
        pass
    if _DUMP is not None:
        _DUMP(S, _ENV)
    nops, nw = S.emit()
    es.close()
    return nops, nw


def _body(nc, SEQ, es, S, dbg=None):
    NTP = SEQ // 256
    KEEP = min(512, SEQ)

    def din(name, shape):
        return nc.dram_tensor(name, shape, F32, kind="ExternalInput").ap()

    def dout(name, shape):
        return nc.dram_tensor(name, shape, F32, kind="ExternalOutput").ap()

    x_p = din("x_p", [SEQ, D])
    x_s = din("x_s", [256, D])
    st_ssm = din("st_ssm", [4, 2048, 128])
    st_conv = din("st_conv", [4, 3, CD])
    ck = din("ck", [4, 512, D])
    cv = din("cv", [4, 512, D])
    norm_w = din("norm_w", [D])
    w_in = din("w_in", [D, DIN])
    conv_w = din("conv_w", [4, CD])
    conv_b = din("conv_b", [CD])
    dt_bias = din("dt_bias", [NH])
    a_log = din("a_log", [NH])
    d_skip = din("d_skip", [NH])
    ssm_norm_w = din("ssm_norm_w", [DI])
    w_out_ssm = din("w_out_ssm", [DI, D])
    rel_bias = din("rel_bias", [AH, 513])
    w_out_att = din("w_out_att", [D, D])
    w_o = din("w_o", [D, D])
    final_norm_w = din("final_norm_w", [D])

    y_p = dout("y_p", [SEQ, D])
    y_s = dout("y_s", [256, D])
    ssm_p = dout("ssm_p", [2048, 128])
    conv_p = dout("conv_p", [3, CD])
    k_p = dout("k_p", [KEEP, D])
    v_p = dout("v_p", [KEEP, D])
    ssm_s = dout("ssm_s", [4, 2048, 128])
    conv_s = dout("conv_s", [4, 3, CD])
    k_s = dout("k_s", [4, 512, D])
    v_s = dout("v_s", [4, 512, D])

    WB = {}
    blist = []

    def addblk(name, src, nrow_chunks, c0, ncols, N):
        WB[name] = dict(idx=len(blist), src=src, kc=nrow_chunks, c0=c0, ncols=ncols, N=N)
        blist.append(name)

    for i in range(2):
        addblk('q%d' % i, w_in, 8, C_Q + 512 * i, 512, DIN)
    for i in range(2):
        addblk('k%d' % i, w_in, 8, C_K + 512 * i, 512, DIN)
    for i in range(2):
        addblk('v%d' % i, w_in, 8, C_V + 512 * i, 512, DIN)
    for i in range(2):
        addblk('g%d' % i, w_in, 8, C_G + 512 * i, 512, DIN)
    for i in range(2):
        addblk('ga%d' % i, w_in, 8, C_GA + 512 * i, 512, DIN)
    for i in range(2):
        addblk('wa%d' % i, w_out_att, 8, 512 * i, 512, D)
    addblk('dt', w_in, 8, C_DT, 32, DIN)
    for i in range(6):
        addblk('x%d' % i, w_in, 8, C_X + 512 * i, 512, DIN)
    for i in range(4):
        addblk('z%d' % i, w_in, 8, C_Z + 512 * i, 512, DIN)
    for i in range(2):
        addblk('gs%d' % i, w_in, 8, C_GS + 512 * i, 512, DIN)
    for i in range(4):
        addblk('ws%d' % i, w_out_ssm, 16, 256 * i, 256, D)
    for i in range(2):
        addblk('wo%d' % i, w_o, 8, 512 * i, 512, D)
    NB = len(blist)
    wscr = nc.dram_tensor("wscr", [NB, 128, 4096], BF16, kind="Internal").ap()
    ext = nc.dram_tensor("ext_bias", [AH, 1024], F32, kind="Internal").ap()

    def wscr_view(name):
        b = WB[name]
        i = b['idx']
        kc, nc_ = b['kc'], b['ncols']
        base = wscr[i]
        return bc(base, [[4096, 128], [nc_, kc], [1, nc_]])

    def cstop(n):
        if _STOP == (n, 0):
            raise _StopBuild()

    convlist = []

    def issue_conversion(name, nth_conv):
        b = WB[name]
        src = bass.AP(b['src'].tensor, b['c0'], [[b['N'], 128], [128 * b['N'], b['kc']], [1, b['ncols']]])
        extra_r = []
        if nth_conv >= 8:
            prev = WB[convlist[nth_conv - 8]]
            extra_r = [('wscr', prev['idx'], prev['idx'] + 1, 0, 1)]
        S.dma('pool', wscr_view(name), src, 'cv%d' % (nth_conv % 8), extra_r=extra_r,
              extra_w=[('wscr', b['idx'], b['idx'] + 1, 0, 1)])
        convlist.append(name)

    CA = Arena(nc, es, S, "cst", 29 * 1024)
    S.tracked["cst"] = True
    A = Arena(nc, es, S, "main", 183040)

    identb = CA.alloc([128], BF16)
    identf = CA.alloc([128], F32)
    Jx = CA.alloc([128], BF16)
    tri = CA.alloc([128], BF16)
    SU = CA.alloc([128], BF16)
    onesb = CA.alloc([128], BF16)
    scr32 = CA.alloc([128], F32)
    cstg = CA.alloc([128], F32)
    cpar = CA.alloc([144], F32)
    normw = cpar[:, 0:8]
    nwssm = cpar[:, 8:24]
    convb = cpar[:, 24:48]
    convw = cpar[:, 48:144].rearrange("p (j c) -> p j c", j=4)
    dtb_bc = CA.alloc([32], F32)
    a_bc = CA.alloc([32], F32)
    dsk_bc = CA.alloc([32], F32)
    fnw_bc = CA.alloc([1024], F32)
    neghalf = CA.alloc([4], F32)
    biasF = CA.alloc([AH, 5, 128], BF16)
    convst = CA.alloc([4, 3, 24], BF16)

    xt = [A.alloc([1024], F32) for _ in range(4)]
    xsn = A.alloc([1024], BF16)
    TH = [A.alloc([512], BF16) for _ in range(3)]
    junk = bc(TH[0], [[pstr(TH[0]), 128], [1, 1024]])
    xnT = A.alloc([8, 256], BF16)
    NSLOT = 3
    Wr = [A.alloc([4096], BF16) for _ in range(NSLOT)]
    KT = A.alloc([8, 6, 128], BF16)
    Vr = A.alloc([6, AH, 66], BF16)
    hT = A.alloc([2048], F32)
    hTb = A.alloc([2048], BF16)
    ost = A.alloc([1024], F32)
    ogT = A.alloc([8, 256], BF16)
    ga_sig = A.alloc([8, 256], BF16)
    gs_sig = A.alloc([8, 256], BF16)
    accT = A.alloc([8, 256], BF16)
    yzT = A.alloc([16, 256], BF16)
    mT = A.alloc([8, 256], BF16)
    smallf = A.alloc([64], F32)
    carry = A.alloc([24, 3], BF16)
    base = A.off
    QT = A.alloc([8, 256], BF16)
    PT = [A.alloc([512], BF16) for _ in range(3)]
    rs = A.alloc([16], F32)
    o_n = A.alloc([1024], BF16)
    og = A.alloc([1024], BF16)
    gs2 = A.alloc([2, 1024], BF16)
    kvst = [A.alloc([512], F32) for _ in range(2)]
    ckst = A.alloc([4, 1024], BF16)
    KTs = A.alloc([8, 2, 64], BF16)
    Vs = A.alloc([2, AH, 66], BF16)
    att_end = A.off
    A.off = base
    zs2 = A.alloc([2, 2048], BF16)
    xs_tok = A.alloc([2048], BF16)
    B_tok = A.alloc([512], BF16)
    xdtb = [A.alloc([512], BF16) for _ in range(2)]
    xdt2b = [A.alloc([512], BF16) for _ in range(2)]
    xsdb = [A.alloc([512], BF16) for _ in range(2)]
    cbTm = A.alloc([4, 128], BF16)
    Bmh = A.alloc([8, 128], BF16)
    Bml = A.alloc([8, 128], BF16)
    Eb = A.alloc([8, 128], BF16)
    MTb = A.alloc([8, 128], BF16)
    yz = A.alloc([4, 512], BF16)
    ssq = A.alloc([4], F32)
    rstd4 = A.alloc([4], F32)
    A.off = max(A.off, att_end)
    XW = 3 + 256
    xraw = A.alloc([24, XW], BF16)
    cacc = [A.alloc([256], F32) for _ in range(2)]
    dtt = A.alloc([2, 32], F32)
    adt = A.alloc([2, 32], F32)
    adt_hi = A.alloc([2, 32], BF16)
    adt_lo = A.alloc([2, 32], BF16)
    dtmp = A.alloc([32], F32)
    ee = A.alloc([64], F32)
    ee2 = A.alloc([32], F32)
    xbcst = [A.alloc([512], F32) for _ in range(2)]
    stT = A.alloc([16, 128], F32)
    ssd_end = A.off
    A.off = max(att_end, ssd_end)
    print('SBUF use: cst', CA.off, 'main', A.off, 'att', att_end - base, 'ssd', ssd_end - base)

    _ENV.update(dict(accT=accT, mT=mT, yzT=yzT, ogT=ogT, gs_sig=gs_sig, ga_sig=ga_sig, xnT=xnT, QT=QT, zs2=zs2,
                     yz=yz, ost=ost, gs2=gs2, o_n=o_n, og=og, hT=hT, xraw=xraw, nc=nc, A=A, xt=xt))
    PB = []
    for i in range(8):
        t = es.enter_context(nc.psum_tensor("pb%d" % i, [128, 2048], U8))
        S.track("pb%d" % i)
        PB.append(t)

    def pf(i):
        return PB[i][:, :].bitcast(F32)

    def pbf(i):
        return PB[i][:, :].bitcast(BF16)

    rot = {'mm': [0, 1, 3, 4], 'tp': [2, 0, 1], 'S': [3, 4]}
    rotp = {k: 0 for k in rot}

    def nb(role):
        i = rot[role][rotp[role] % len(rot[role])]
        rotp[role] += 1
        return i
    OB = [5, 6, 7]
    SEGB = [3, 4]
    YB, UB, CBB, SMB = 5, 7, 6, 2

    P = 'pool'
    S.memset(P, scr32, 0.0)
    S.add(P, lambda e: e.affine_select(out=scr32, in_=scr32, pattern=[[-1, 128]], compare_op=ALU.not_equal, fill=1.0,
                                       base=0, channel_multiplier=1), [scr32], [scr32])
    S.copy('dve', identb, scr32)
    S.copy('dve', identf, scr32)
    S.memset(P, scr32, 0.0)
    S.add(P, lambda e: e.affine_select(out=scr32, in_=scr32, pattern=[[1, 128]], compare_op=ALU.not_equal, fill=1.0,
                                       base=-127, channel_multiplier=1), [scr32], [scr32])
    S.copy('dve', Jx, scr32)
    S.memset(P, scr32, 1.0)
    S.copy('dve', onesb, scr32)
    S.add(P, lambda e: e.affine_select(out=scr32, in_=scr32, pattern=[[1, 128]], compare_op=ALU.is_ge, fill=0.0,
                                       base=0, channel_multiplier=-1), [scr32], [scr32])
    S.copy('dve', tri, scr32)
    S.memset(P, scr32, 1.0)
    S.add(P, lambda e: e.affine_select(out=scr32, in_=scr32, pattern=[[-1, 128]], compare_op=ALU.is_ge, fill=0.0,
                                       base=-1, channel_multiplier=1), [scr32], [scr32])
    S.copy('dve', SU, scr32)
    S.memset(P, neghalf, -0.5)
    S.memset(P, hT, 0.0)
    S.memset(P, hTb, 0.0)
    S.memset(P, carry, 0.0)
    S.memset(P, Vr[:, :, :, 64:65], 1.0)

    cstop(-2)
    CK = 'const'
    S.dma('sp', dtb_bc, bass.AP(dt_bias.tensor, 0, [[0, 128], [1, 32]]), CK)
    S.dma('sp', a_bc, bass.AP(a_log.tensor, 0, [[0, 128], [1, 32]]), CK)
    S.dma('sp', dsk_bc, bass.AP(d_skip.tensor, 0, [[0, 128], [1, 32]]), CK)
    S.dma('sp', fnw_bc, bass.AP(final_norm_w.tensor, 0, [[0, 128], [1, 1024]]), CK)
    S.group_barrier(CK)
    cstop(-3)
    S.dma('sp', cstg[0:8, :], bass.AP(norm_w.tensor, 0, [[128, 8], [1, 128]]), 'cstg')
    S.dma('sp', cstg[8:24, :], bass.AP(ssm_norm_w.tensor, 0, [[128, 16], [1, 128]]), 'cstg')
    S.dma('sp', cstg[24:48, :], bass.AP(conv_b.tensor, 0, [[128, 24], [1, 128]]), 'cstg')
    S.group_last('cstg', 3)
    S.trs([(pf(0)[:, 0:48], cstg[0:48, :])], identf[0:48, 0:48])
    S.copy('dve', cpar[:, 0:48], pf(0)[:, 0:48])
    S.dma('sp', cstg[0:96, :], bass.AP(conv_w.tensor, 0, [[128, 96], [1, 128]]), 'cstg')
    S.trs([(pf(0)[:, 0:96], cstg[0:96, :])], identf[0:96, 0:96])
    S.copy('dve', cpar[:, 48:144], pf(0)[:, 0:96])
    for c3 in range(3):
        S.dma('sp', cstg[0:96, :], bass.AP(st_conv.tensor, c3 * 96 * 128, [[128, 96], [1, 128]]), 'cstg')
        S.trs([(pf(0)[:, 0:96], cstg[0:96, :])], identf[0:96, 0:96])
        cf = convst.rearrange("p s j c -> p (s j c)")
        S.copy('dve', cf[:, c3 * 96:(c3 + 1) * 96], pf(0)[:, 0:96])
    cstop(-4)
    EXT = ('ext', 0, 1, 0, 1)
    S.dma('sp', xt[2][0:16, 0:513], rel_bias, 'xt2')
    S.copy('dve', xt[1][0:16, 0:513], xt[2][0:16, 0:513])
    rb512 = xt[2][0:16, 512:513]
    S.copy('dve', xt[1][0:16, 513:1024], bc(rb512, [[rb512.ap[0][0], 16], [0, 511]]))
    S.dma('sp', ext, xt[1][0:16, 0:1024], 'xt1', extra_w=[EXT])
    stgv = bc(xt[2], [[pstr(xt[2]), 128], [128, AH], [1, 128]])
    for t in range(5):
        src = bass.AP(ext.tensor, 641 - 128 * t, [[1, 128], [1024, AH], [1, 128]])
        S.dma('sp', stgv, src, 'xt2', extra_r=[EXT])
        S.copy('dve', biasF[:, :, t, :], stgv)
    S.memset(P, biasF[64:128, :, 0, 64:128], NEG)
    S.memset(P, biasF[0:64, :, 4, 0:64], NEG)
    cstop(-5)
    S.act(a_bc, a_bc, AF.Exp)
    S.ts('dve', a_bc, a_bc, -1.0, None, ALU.mult)
    S.ts('dve', convw, convw, 0.5, None, ALU.mult)
    S.ts('dve', convb, convb, 0.5, None, ALU.mult)

    S.tracked['cst'] = 'const'

    uses = []
    tiles = [('p', i) for i in range(NTP)] + [('s', 0), ('s', 1)]
    per_tile = ['q0', 'q1', 'k0', 'k1', 'v0', 'v1', 'g0', 'g1', 'ga0', 'ga1', 'dt',
                'x0', 'x1', 'x2', 'x3', 'x4', 'x5', 'wa0', 'wa1', 'z0', 'z1', 'z2', 'z3', 'gs0', 'gs1',
                'ws0', 'ws1', 'ws2', 'ws3', 'wo0', 'wo1']
    for _ in tiles:
        uses += per_tile
    wstate = {'issued': 0, 'cur': 0, 'conv': set()}

    def w_issue(upto):
        while wstate['issued'] < min(upto, len(uses)):
            u = wstate['issued']
            name = uses[u]
            b = WB[name]
            slot = u % NSLOT
            if name not in wstate['conv']:
                issue_conversion(name, len(wstate['conv']))
                wstate['conv'].add(name)
            n = b['kc'] * b['ncols']
            dst = bc(Wr[slot], [[pstr(Wr[slot]), 128], [1, n]])
            srcv = bc(wscr[b['idx']], [[4096, 128], [1, n]])
            S.dma('sp', dst, srcv, 'W%d' % slot, extra_r=[('wscr', b['idx'], b['idx'] + 1, 0, 1)])
            wstate['issued'] += 1

    def w_use(name, issue=True):
        u = wstate['cur']
        assert uses[u] == name, (uses[u], name)
        if issue:
            w_issue(u + NSLOT)
        wstate['cur'] += 1
        b = WB[name]
        slot = u % NSLOT
        return bc(Wr[slot], [[pstr(Wr[slot]), 128], [b['ncols'], b['kc']], [1, b['ncols']]])

    def xrows(kind, idx, sub, L):
        if kind == 'p':
            r0 = idx * 256 + sub * 128
            return x_p[r0:r0 + L, :], y_p[r0:r0 + L, :]
        seq = idx * 2 + sub
        return x_s[seq * 64:seq * 64 + 64, :], y_s[seq * 64:seq * 64 + 64, :]

    def load_x(ti):
        kind, idx = tiles[ti]
        L = 128 if kind == 'p' else 64
        for sub in range(2):
            b = (ti % 2) * 2 + sub
            src, _ = xrows(kind, idx, sub, L)
            S.dma('sp', xt[b][0:L, :], src, 'xt%d' % b)

    def rms_rstd(src_sum, n, eps, dst, L, col=1):
        S.ts('dve', dst, src_sum, 1.0 / n, eps, ALU.mult, ALU.add)
        S.tt('pool', dst, dst, neghalf[0:L, 0:col], ALU.pow)

    def fm_block(wv, ncols_tiles, T, evac):
        for et in range(ncols_tiles):
            bk = nb('mm')
            out = pf(bk)[:, 0:T]
            S.mms([dict(out=out, lhsT=wv[:, kc, et * 128:(et + 1) * 128], rhs=xnT[:, kc, 0:T],
                        start=(kc == 0), stop=(kc == 7)) for kc in range(8)])
            evac(et, out)

    def tm_block(wv, ncols, L, subs, evac):
        for sub in subs:
            bk = nb('mm')
            out = pf(bk)[0:L, 0:ncols]
            S.mms([dict(out=out, lhsT=xnT[:, kc, sub * L:(sub + 1) * L], rhs=wv[:, kc, 0:ncols],
                        start=(kc == 0), stop=(kc == 7)) for kc in range(8)])
            evac(sub, out)

    thp = [0]

    def nth():
        thp[0] += 1
        return TH[thp[0] % 3]

    def store_state(dst_ap):
        for q in range(4):
            bk = nb('mm')
            S.trs([(pf(bk)[:, j * 128:(j + 1) * 128], hT[:, (q * 4 + j) * 128:(q * 4 + j + 1) * 128])
                   for j in range(4)], identf)
            S.copy('dve', stT[:, q * 4:(q + 1) * 4, :], pf(bk)[:, 0:512].rearrange("p (a b) -> p a b", a=4))
        S.dma('pool', bass.AP(dst_ap.tensor, dst_ap.offset, [[128, 128], [128 * 128, 16], [1, 128]]), stT, 'stT')

    def load_state(seq):
        src = st_ssm[seq]
        if 'nold' not in _DBG:
            S.dma('pool' if 'poolld' in _DBG else 'sp', stT,
                  bass.AP(src.tensor, src.offset, [[128, 128], [128 * 128, 16], [1, 128]]), 'stT')
        if 'notr' in _DBG:
            return
        for q in range(4):
            bk = nb('mm')
            S.trs([(pf(bk)[:, j * 128:(j + 1) * 128], stT[:, q * 4 + j, :]) for j in range(4)], identf)
            if 'nocp' in _DBG:
                continue
            S.copy('dve', hT[:, q * 512:(q + 1) * 512], pf(bk)[:, 0:512])
            if 'noact' in _DBG:
                continue
            S.copy('act', hTb[:, q * 512:(q + 1) * 512], pf(bk)[:, 0:512])

    def norm_tile(tj):
        kind_, idx_ = tiles[tj]
        L = 128 if kind_ == 'p' else 64
        xb = [xt[(tj % 2) * 2 + sub] for sub in range(2)]
        for sub in range(2):
            ssx = smallf[0:L, 0:1]
            S.act(junk[0:L, :], xb[sub][0:L, :], AF.Square, accum_out=ssx)
            rms_rstd(ssx, 1024.0, EPS, smallf[0:L, 1:2], L)
            S.act(xsn[0:L, :], xb[sub][0:L, :], AF.Identity, scale=smallf[0:L, 1:2])
            bk = nb('tp')
            S.trs([(pbf(bk)[:, kc * L:(kc + 1) * L], xsn[0:L, kc * 128:(kc + 1) * 128]) for kc in range(8)],
                  identb[0:L, 0:L])
            S.tt('dve', xnT[:, :, sub * L:(sub + 1) * L], pbf(bk)[:, 0:8 * L].rearrange("p (a b) -> p a b", a=8),
                 bc(normw, [[pstr(normw), 128], [1, 8], [0, L]]), ALU.mult)


    def stop(n, ti):
        if _STOP == (n, ti):
            raise _StopBuild()

    load_x(0)
    norm_tile(0)
    for ti, (kind, idx) in enumerate(tiles):
        stop(0, ti)
        L = 128 if kind == 'p' else 64
        T = 2 * L
        last_p = (kind == 'p' and idx == NTP - 1)
        def kv_out_row(sub):
            if kind == 'p':
                r0 = idx * 256 + sub * 128
                if r0 >= SEQ - KEEP:
                    return r0 - (SEQ - KEEP)
                return None
            return 448
        if ti + 1 < len(tiles):
            load_x(ti + 1)
        xb = [xt[(ti % 2) * 2 + sub] for sub in range(2)]

        stop(1, ti)
        if kind == 'p':
            g0 = idx * 2
            slot0 = g0 % 6
        for qi in range(2):
            wv = w_use('q%d' % qi)

            def ev(et, out, qi=qi):
                S.act(QT[:, qi * 4 + et, 0:T], out, AF.Identity, scale=0.125)
            fm_block(wv, 4, T, ev)
        stop(1.2, ti)
        for ki in range(2):
            wv = w_use('k%d' % ki)

            def ev(et, out, ki=ki):
                pair = ki * 4 + et
                if kind == 'p':
                    S.copy('act', KT[:, pair, slot0:slot0 + 2, :], out.rearrange("p (s l) -> p s l", s=2))
                else:
                    S.copy('dve', KTs[:, pair, :, :], out.rearrange("p (s l) -> p s l", s=2))
            fm_block(wv, 4, T, ev)
            subs = [s for s in range(2) if kv_out_row(s) is not None]

            def evk(sub, out, ki=ki):
                st = kvst[(ki + sub) % 2]
                S.copy('dve', st[0:L, :], out)
                r = kv_out_row(sub)
                if kind == 'p':
                    S.dma('pool', k_p[r:r + L, ki * 512:(ki + 1) * 512], st[0:L, :], 'kvst%d' % ((ki + sub) % 2))
                else:
                    seq = idx * 2 + sub
                    S.dma('pool', k_s[seq, 448:512, ki * 512:(ki + 1) * 512], st[0:L, :],
                          'kvst%d' % ((ki + sub) % 2))
            if subs:
                tm_block(wv, 512, L, subs, evk)
        stop(1.4, ti)
        if kind == 's':
            S.memset('pool', Vs[:, :, :, 64:65], 1.0)
        for vi in range(2):
            wv = w_use('v%d' % vi)

            def evv(sub, out, vi=vi):
                if kind == 'p':
                    dst = Vr[0:L, slot0 + sub, vi * 8:(vi + 1) * 8, 0:64]
                else:
                    dst = Vs[0:L, sub, vi * 8:(vi + 1) * 8, 0:64]
                S.copy('act', dst, out.rearrange("p (h d) -> p h d", h=8))
                r = kv_out_row(sub)
                if r is not None:
                    st = kvst[(vi + sub) % 2]
                    S.copy('dve', st[0:L, :], out)
                    if kind == 'p':
                        S.dma('pool', v_p[r:r + L, vi * 512:(vi + 1) * 512], st[0:L, :],
                              'kvst%d' % ((vi + sub) % 2))
                    else:
                        seq = idx * 2 + sub
                        S.dma('pool', v_s[seq, 448:512, vi * 512:(vi + 1) * 512], st[0:L, :],
                              'kvst%d' % ((vi + sub) % 2))
            tm_block(wv, 512, L, [0, 1], evv)
        stop(1.6, ti)
        for gi in range(2):
            wv = w_use('g%d' % gi)

            def evg(sub, out, gi=gi):
                th = nth()
                S.act(th[0:L, :], out, AF.Tanh, scale=0.5)
                S.stt('dve', gs2[0:L, sub, gi * 512:(gi + 1) * 512], th[0:L, :], 1.0, out, ALU.add, ALU.mult)
            tm_block(wv, 512, L, [0, 1], evg)
        stop(1.8, ti)
        for gi in range(2):
            wv = w_use('ga%d' % gi)

            def evga(et, out, gi=gi):
                th = nth()
                S.act(th[:, 0:T], out, AF.Tanh, scale=0.5)
                S.act(ga_sig[:, gi * 4 + et, 0:T], th[:, 0:T], AF.Identity, scale=0.25, bias=0.25)
            fm_block(wv, 4, T, evga)

        stop(1.9, ti)
        wv = w_use('dt')

        def evdt(sub, out):
            S.tt('dve', dtmp[0:L, :], out, dtb_bc[0:L, :], ALU.add)
            S.act(dtmp[0:L, :], dtmp[0:L, :], AF.Exp)
            S.act(dtt[0:L, sub, :], dtmp[0:L, :], AF.Ln, bias=1.0)
            S.tt('dve', adt[0:L, sub, :], dtt[0:L, sub, :], a_bc[0:L, :], ALU.mult)
            S.copy('dve', adt_hi[0:L, sub, :], adt[0:L, sub, :])
            S.tt('dve', adt_lo[0:L, sub, :], adt[0:L, sub, :], adt_hi[0:L, sub, :], ALU.subtract)
        tm_block(wv, 32, L, [0, 1], evdt)

        if kind == 'p':
            segs = [(0, T)]
            SW = T + 3
            S.copy('pool', xraw[:, :, 0:3], carry)
        else:
            SW = 67
            segs = [(0, 64), (67, 64)]
            for s in range(2):
                S.copy('pool', xraw[:, :, s * 67:s * 67 + 3], convst[:, idx * 2 + s].rearrange("p j c -> p c j"))
        for xi in range(6):
            wv = w_use('x%d' % xi)

            def evx(et, out, xi=xi):
                ct = xi * 4 + et
                if kind == 'p':
                    S.copy('act', xraw[:, ct, 3:3 + T], out)
                else:
                    S.copy('dve', xraw[:, ct, 0:134].rearrange("p (s c) -> p s c", c=67)[:, :, 3:67],
                           out.rearrange("p (s c) -> p s c", c=64))
            fm_block(wv, 4, T, evx)
            if last_p or kind == 's':
                osubs = [1] if kind == 'p' else [0, 1]

                def evc(sub, out, xi=xi):
                    st = xbcst[(xi + sub) % 2]
                    S.copy('dve', st[L - 32:L, :], out[L - 32:L, :])
                    dst = conv_p if kind == 'p' else conv_s[idx * 2 + sub]
                    S.dma('pool', dst[:, xi * 512:(xi + 1) * 512], st[L - 3:L, :], 'xbcst%d' % ((xi + sub) % 2))
                tm_block(wv, 512, L, osubs, evc)
        def conv_block(xj):
            accs = [cacc[0], cacc[1], xbcst[0][:, 0:256], xbcst[1][:, 0:256]]
            for (c0, nt) in segs:
                cts = [xj * 4 + e for e in range(4)]
                for e, ct in enumerate(cts):
                    S.ts('dve', accs[e][:, 0:nt], xraw[:, ct, c0:c0 + nt], convw[:, 0, ct:ct + 1],
                         convb[:, ct:ct + 1], ALU.mult, ALU.add)
                for j in range(1, 4):
                    for e, ct in enumerate(cts):
                        S.stt('dve', accs[e][:, 0:nt], xraw[:, ct, c0 + j:c0 + j + nt],
                              convw[:, j, ct:ct + 1], accs[e][:, 0:nt], ALU.mult, ALU.add)
                if kind == 'p' and c0 == 0:
                    S.copy('pool', carry[:, xj * 4:(xj + 1) * 4, :], xraw[:, xj * 4:(xj + 1) * 4, T:T + 3])
                ths = []
                for e, ct in enumerate(cts):
                    th = nth()
                    ths.append(th)
                    S.act(th[:, 0:nt], accs[e][:, 0:nt], AF.Tanh)
                    if e >= 1:
                        S.stt('dve', xraw[:, cts[e - 1], c0:c0 + nt], ths[e - 1][:, 0:nt], 1.0,
                              accs[e - 1][:, 0:nt], ALU.add, ALU.mult)
                S.stt('dve', xraw[:, cts[3], c0:c0 + nt], ths[3][:, 0:nt], 1.0, accs[3][:, 0:nt],
                      ALU.add, ALU.mult)


        conv_todo = [0, 1, 2, 3, 4, 5]
        stop(2, ti)
        for sub in range(2):
            if kind == 'p':
                g = idx * 2 + sub
                tl = [t for t in range(5) if g - 4 + t >= 0]
            else:
                seq = idx * 2 + sub
                tl = [0, 1, 2, 3] if 'not4' in _DBG else ([4] if 'only4' in _DBG else [0, 1, 2, 3, 4])
                kv32 = bc(kvst[0], [[pstr(kvst[0]), 128], [1, 1024]])
                for tt_ in range(4):
                    S.dma('sp', kv32, cv[seq, tt_ * 128:(tt_ + 1) * 128, :], 'kvst0')
                    S.copy('dve', Vr[:, tt_, :, 0:64], kv32.rearrange("p (h d) -> p h d", h=AH))
                    S.dma('sp', ost, ck[seq, tt_ * 128:(tt_ + 1) * 128, :], 'ost')
                    S.copy('act', ckst[:, tt_, :], ost)
                    bk = nb('tp')
                    S.trs([(pbf(bk)[:, pr * 128:(pr + 1) * 128], ckst[:, tt_, pr * 128:(pr + 1) * 128])
                           for pr in range(8)], identb)
                    S.copy('dve', KT[:, :, tt_, :], pbf(bk)[:, 0:1024].rearrange("p (a b) -> p a b", a=8))
                S.dma('pool', k_s[seq, 0:448, :], ck[seq, 64:512, :], 'd2d')
                S.dma('pool', v_s[seq, 0:448, :], cv[seq, 64:512, :], 'd2d')
            stop(2.2, ti)
            for ob in OB:
                S.memset('dve', pf(ob)[0:L, :], 0.0)
            blocks = [(h, t) for h in range(AH) for t in tl]
            per = 512 // L
            groups = [blocks[i:i + per] for i in range(0, len(blocks), per)]
            pend = None

            def emit_pv(pg):
                grp, pt = pg
                if 'nopv' in _DBG:
                    return
                items = []
                for cb, (h, t) in enumerate(grp):
                    Mk = 64 if (kind == 's' and t == 4) else 128
                    if kind == 'p':
                        vt = Vr[0:Mk, (g - 4 + t) % 6, h, 0:65]
                    elif t < 4:
                        vt = Vr[0:Mk, t, h, 0:65]
                    else:
                        vt = Vs[0:Mk, sub, h, 0:65]
                    ob = OB[h // 6]
                    oc = (h % 6) * 65
                    items.append(dict(out=pf(ob)[0:L, oc:oc + 65], lhsT=pt[0:Mk, cb * L:(cb + 1) * L], rhs=vt,
                                      start=False, stop=True))
                S.mms(items)
            for gi_, grp in enumerate(groups):
                sbk = nb('S')
                items = []
                for cb, (h, t) in enumerate(grp):
                    pair, pb_ = h // 2, (h % 2) * 64
                    Mk = 64 if (kind == 's' and t == 4) else 128
                    out = pf(sbk)[0:Mk, cb * L:(cb + 1) * L]
                    if Mk == 128:
                        items.append(dict(out=out, lhsT=Jx, rhs=biasF[:, h, t, 0:L], start=True, stop=False))
                    else:
                        items.append(dict(out=out, lhsT=Jx[:, 0:64], rhs=biasF[:, h, t, 0:L],
                                          start=True, stop=False))
                    if kind == 'p':
                        kt = KT[pb_:pb_ + 64, pair, (g - 4 + t) % 6, 0:Mk]
                    elif t < 4:
                        kt = KT[pb_:pb_ + 64, pair, t, 0:Mk]
                    else:
                        kt = KTs[pb_:pb_ + 64, pair, sub, 0:Mk]
                    items.append(dict(out=out, lhsT=kt, rhs=QT[pb_:pb_ + 64, pair, sub * L:(sub + 1) * L],
                                      start=False, stop=True))
                S.mms(items)
                pt = PT[gi_ % 3]
                ncol = len(grp) * L
                S.act(pt[:, 0:ncol], pf(sbk)[:, 0:ncol], AF.Exp)
                if pend is not None:
                    emit_pv(pend)
                pend = (grp, pt)
                if conv_todo and (gi_ % 6 == 5 or len(groups) < 12):
                    conv_block(conv_todo.pop(0))
            emit_pv(pend)
            stop(2.4, ti)
            for obi, ob in enumerate(OB):
                h0 = obi * 6
                nh_ = min(6, AH - h0)
                ov = pf(ob)[0:L, 0:nh_ * 65].rearrange("p (h d) -> p h d", h=nh_)
                S.recip(rs[0:L, h0:h0 + nh_], ov[:, :, 64])
                S.tt('dve', o_n[0:L, h0 * 64:(h0 + nh_) * 64].rearrange("p (h d) -> p h d", h=nh_), ov[:, :, 0:64],
                     bc(rs[0:L, h0:h0 + nh_], [[pstr(rs), L], [1, nh_], [0, 64]]), ALU.mult)
            S.tt('pool', og[0:L, :], o_n[0:L, :], gs2[0:L, sub, :], ALU.mult)
            bk = nb('tp')
            S.trs([(pbf(bk)[:, kc * L:(kc + 1) * L], og[0:L, kc * 128:(kc + 1) * 128]) for kc in range(8)],
                  identb[0:L, 0:L])
            S.copy('dve', ogT[:, :, sub * L:(sub + 1) * L], pbf(bk)[:, 0:8 * L].rearrange("p (a b) -> p a b", a=8))

        while conv_todo:
            conv_block(conv_todo.pop(0))
        stop(3, ti)
        stop(4, ti)
        for wi in range(2):
            wv = w_use('wa%d' % wi)
            for et in range(4):
                bk = nb('mm')
                out = pf(bk)[:, 0:T]
                S.mms([dict(out=out, lhsT=wv[:, kc, et * 128:(et + 1) * 128], rhs=ogT[:, kc, 0:T],
                            start=(kc == 0), stop=(kc == 7)) for kc in range(8)])
                S.tt('dve', accT[:, wi * 4 + et, 0:T], out, ga_sig[:, wi * 4 + et, 0:T], ALU.mult)

        stop(5, ti)
        for zi in range(4):
            wv = w_use('z%d' % zi)

            def evz(sub, out, zi=zi):
                th = nth()
                S.act(th[0:L, :], out, AF.Tanh, scale=0.5)
                S.stt('dve', zs2[0:L, sub, zi * 512:(zi + 1) * 512], th[0:L, :], 1.0, out, ALU.add, ALU.mult)
            tm_block(wv, 512, L, [0, 1], evz)
        for gi in range(2):
            wv = w_use('gs%d' % gi)

            def evgs(et, out, gi=gi):
                th = nth()
                S.act(th[:, 0:T], out, AF.Tanh, scale=0.5)
                S.act(gs_sig[:, gi * 4 + et, 0:T], th[:, 0:T], AF.Identity, scale=0.5, bias=0.5)
            fm_block(wv, 4, T, evgs)

        if ti + 1 < len(tiles):
            norm_tile(ti + 1)
        stop(6, ti)
        def cview(t, n):
            return bc(t, [[pstr(t), L], [L, n], [1, L]])
        Bmh_v, Bml_v, Eb_v, MT_v, cb_v = cview(Bmh, 8), cview(Bml, 8), cview(Eb, 8), cview(MTb, 8), cview(cbTm, 4)
        for sub in range(2):
            if kind == 'p':
                cs = sub * 128
            else:
                cs = sub * 67
                load_state(idx * 2 + sub)
            stop(6.2, ti)
            for hb in range(2):
                bk = nb('tp')
                S.trs([(pbf(bk)[0:L, j * 128:(j + 1) * 128], xraw[:, hb * 8 + j, cs:cs + L]) for j in range(8)],
                      identb)
                S.copy('act' if hb else 'dve', xs_tok[0:L, hb * 1024:(hb + 1) * 1024], pbf(bk)[0:L, 0:1024])
            bk = nb('tp')
            S.trs([(pbf(bk)[0:L, j * 128:(j + 1) * 128], xraw[:, 16 + j, cs:cs + L]) for j in range(4)], identb)
            S.copy('dve', B_tok[0:L, :], pbf(bk)[0:L, 0:512])
            stop(6.4, ti)
            sm = pf(SMB)
            ah, al = adt_hi[0:L, sub, :], adt_lo[0:L, sub, :]
            S.mms([dict(out=sm[0:L, 0:32], lhsT=tri[0:L, 0:L], rhs=ah, start=True, stop=False),
                   dict(out=sm[0:L, 0:32], lhsT=tri[0:L, 0:L], rhs=al, start=False, stop=True),
                   dict(out=sm[0:L, 32:64], lhsT=SU[0:L, 0:L], rhs=ah, start=True, stop=False),
                   dict(out=sm[0:L, 32:64], lhsT=SU[0:L, 0:L], rhs=al, start=False, stop=True),
                   dict(out=sm[:, 64:96], lhsT=onesb[0:L, :], rhs=ah, start=True, stop=False),
                   dict(out=sm[:, 64:96], lhsT=onesb[0:L, :], rhs=al, start=False, stop=True)])
            S.act(ee[0:L, :], sm[0:L, 0:64], AF.Exp)
            S.act(ee2, sm[:, 64:96], AF.Exp)
            cbb = pf(CBB)
            S.mms([dict(out=cbb[0:L, gq * L:(gq + 1) * L], lhsT=xraw[:, 16 + gq, cs:cs + L],
                        rhs=xraw[:, 20 + gq, cs:cs + L], start=True, stop=True) for gq in range(4)])
            S.tt('dve', cb_v, cbb[0:L, 0:4 * L].rearrange("p (g l) -> p g l", g=4),
                 bc(tri, [[pstr(tri), L], [0, 4], [1, L]]), ALU.mult)
            stop(6.6, ti)
            def b8(ap2):
                return bc(ap2, [[ap2.ap[0][0], L], [1, 8], [0, 64]])
            x3 = lambda a: a[0:L, :].rearrange("p (r d) -> p r d", r=8)
            nhalf = 2 if L == 128 else 1
            hh = 8 // nhalf

            def s0_pool(gq):
                xdt, xdt2, xsd = xdtb[gq % 2], xdt2b[gq % 2], xsdb[gq % 2]
                xsg = xs_tok[0:L, gq * 512:(gq + 1) * 512].rearrange("p (r d) -> p r d", r=8)
                S.tt('pool', x3(xdt), xsg, b8(dtt[0:L, sub, gq * 8:(gq + 1) * 8]), ALU.mult)
                S.tt('pool', x3(xdt2), x3(xdt), b8(ee[0:L, 32 + gq * 8:32 + (gq + 1) * 8]), ALU.mult)
                S.tt('pool', x3(xsd), xsg, b8(dsk_bc[0:L, gq * 8:(gq + 1) * 8]), ALU.mult)

            def s1a(gq):
                for (src, dst) in ((adt_hi, Bmh_v), (adt_lo, Bml_v)):
                    a8 = src[0:L, sub, gq * 8:(gq + 1) * 8]
                    S.tt('dve', dst, bc(a8, [[a8.ap[0][0], L], [1, 8], [0, L]]),
                         bc(tri, [[pstr(tri), L], [0, 8], [1, L]]), ALU.mult)

            def s1b(gq):
                items = []
                for hf in range(nhalf):
                    out = pf(SEGB[hf])[0:L, 0:hh * L]
                    items.append(dict(out=out, lhsT=SU[0:L, 0:L], rhs=Bmh_v[:, hf * hh:(hf + 1) * hh, :],
                                      start=True, stop=False))
                    items.append(dict(out=out, lhsT=SU[0:L, 0:L], rhs=Bml_v[:, hf * hh:(hf + 1) * hh, :],
                                      start=False, stop=True))
                S.mms(items)
                for hf in range(nhalf):
                    S.act(Eb_v[:, hf * hh:(hf + 1) * hh, :],
                          pf(SEGB[hf])[0:L, 0:hh * L].rearrange("p (r l) -> p r l", r=hh), AF.Exp)

            def s2(gq):
                xdt, xdt2, xsd = xdtb[gq % 2], xdt2b[gq % 2], xsdb[gq % 2]
                cg = cb_v[:, gq, :]
                S.tt('dve', MT_v, Eb_v, bc(cg, [[cg.ap[0][0], L], [0, 8], [1, L]]), ALU.mult)
                yb = pf(YB)
                S.mms([dict(out=yb[0:L, 0:512], lhsT=xraw[:, 20 + gq, cs:cs + L], rhs=hTb[:, gq * 512:(gq + 1) * 512],
                            start=True, stop=False)])
                S.tt('dve', yb[0:L, 0:512].rearrange("p (r d) -> p r d", r=8),
                     yb[0:L, 0:512].rearrange("p (r d) -> p r d", r=8), b8(ee[0:L, gq * 8:(gq + 1) * 8]), ALU.mult)
                items = [dict(out=yb[0:L, 0:512], lhsT=identb[0:L, 0:L], rhs=xsd[0:L, :], start=False, stop=False)]
                for r in range(8):
                    items.append(dict(out=yb[0:L, r * 64:(r + 1) * 64], lhsT=MT_v[:, r, :],
                                      rhs=xdt[0:L, r * 64:(r + 1) * 64], start=False, stop=(r == 7)))
                S.mms(items)
                ub = pf(UB)
                S.mms([dict(out=ub[:, 0:512], lhsT=B_tok[0:L, gq * 128:(gq + 1) * 128], rhs=xdt2[0:L, :],
                            start=True, stop=True)])

            def s3(gq):
                yb = pf(YB)
                ub = pf(UB)
                S.tt('dve', yz[0:L, gq, :], yb[0:L, 0:512], zs2[0:L, sub, gq * 512:(gq + 1) * 512], ALU.mult)
                S.act(junk[0:L, 0:512], yz[0:L, gq, :], AF.Square, accum_out=ssq[0:L, gq:gq + 1])
                hg = hT[:, gq * 512:(gq + 1) * 512]
                e8 = ee2[:, gq * 8:(gq + 1) * 8]
                S.tt('pool', hg.rearrange("p (r d) -> p r d", r=8), hg.rearrange("p (r d) -> p r d", r=8),
                     bc(e8, [[e8.ap[0][0], 128], [1, 8], [0, 64]]), ALU.mult)
                S.tt('dve', hg, hg, ub[:, 0:512], ALU.add)
                S.copy('act', hTb[:, gq * 512:(gq + 1) * 512], hg)

            s0_pool(0)
            s1a(0)
            s1b(0)
            for gq in range(4):
                if gq < 3:
                    s0_pool(gq + 1)
                s2(gq)
                if gq < 3:
                    s1a(gq + 1)
                    s1b(gq + 1)
                s3(gq)
            stop(6.8, ti)
            S.ts('dve', rstd4[0:L, :], ssq[0:L, :], 1.0 / 512.0, 4.0 * EPS, ALU.mult, ALU.add)
            S.tt('pool', rstd4[0:L, :], rstd4[0:L, :], neghalf[0:L, :], ALU.pow)
            S.tt('pool', yz[0:L, :, :], yz[0:L, :, :], bc(rstd4, [[pstr(rstd4), L], [1, 4], [0, 512]]), ALU.mult)
            yzf = yz[0:L, :, :].rearrange("p a b -> p (a b)")
            for hb in range(2):
                bk = nb('tp')
                S.trs([(pbf(bk)[:, j * L:(j + 1) * L], yzf[:, (hb * 8 + j) * 128:(hb * 8 + j + 1) * 128])
                       for j in range(8)], identb[0:L, 0:L])
                nwv = nwssm[:, hb * 8:(hb + 1) * 8]
                S.tt('dve', yzT[:, hb * 8:(hb + 1) * 8, sub * L:(sub + 1) * L],
                     pbf(bk)[:, 0:8 * L].rearrange("p (a b) -> p a b", a=8),
                     bc(nwv, [[nwv.ap[0][0], 128], [1, 8], [0, L]]), ALU.mult)
            stop(6.9, ti)
            if kind == 's':
                store_state(ssm_s[idx * 2 + sub])
            elif last_p and sub == 1:
                store_state(ssm_p)

        stop(7, ti)
        for wi in range(4):
            wv = w_use('ws%d' % wi)
            for et in range(2):
                e8 = wi * 2 + et
                bk = nb('mm')
                out = pf(bk)[:, 0:T]
                S.mms([dict(out=out, lhsT=wv[:, kc, et * 128:(et + 1) * 128], rhs=yzT[:, kc, 0:T],
                            start=(kc == 0), stop=(kc == 15)) for kc in range(16)])
                th = nth()
                S.tt('dve', th[:, 0:T], out, gs_sig[:, e8, 0:T], ALU.mult)
                S.tt('pool', mT[:, e8, 0:T], th[:, 0:T], accT[:, e8, 0:T], ALU.add)

        stop(8, ti)
        wvs = [w_use('wo0'), w_use('wo1', issue=False)]
        for sub in range(2):
            for hf in range(2):
                bk = nb('mm')
                out = pf(bk)[0:L, 0:512]
                S.mms([dict(out=out, lhsT=mT[:, kc, sub * L:(sub + 1) * L], rhs=wvs[hf][:, kc, :],
                            start=(kc == 0), stop=(kc == 7)) for kc in range(8)])
                S.tt('dve', ost[0:L, hf * 512:(hf + 1) * 512], out, xb[sub][0:L, hf * 512:(hf + 1) * 512], ALU.add)
            ssx = smallf[0:L, 2:3]
            S.act(junk[0:L, :], ost[0:L, :], AF.Square, accum_out=ssx)
            rms_rstd(ssx, 1024.0, EPS, smallf[0:L, 3:4], L)
            S.stt('dve', ost[0:L, :], ost[0:L, :], smallf[0:L, 3:4], fnw_bc[0:L, :], ALU.mult, ALU.mult)
            _, dsty = xrows(kind, idx, sub, L)
            S.dma('pool', dsty, ost[0:L, :], 'ost')

    if dbg is not None:
        dbg(S, locals())


_CACHE = {}


def _get_nc(SEQ):
    if SEQ not in _CACHE:
        nc = bass.Bass("TRN2", target_bir_lowering=False)
        build(nc, SEQ)
        _CACHE[SEQ] = nc
    return _CACHE[SEQ]


def kernel(x_prompt, x_sample, state_ssm, state_conv, cache_k, cache_v, norm_w, w_in, conv_w, conv_b, dt_bias,
           a_log, d_skip, ssm_norm_w, w_out_ssm, rel_bias, w_out_att, w_o, final_norm_w):
    f = lambda a: np.ascontiguousarray(np.asarray(a, dtype=np.float32))
    B, SEQ, _ = x_prompt.shape
    assert B == 8 and SEQ % 256 == 0
    KEEP = min(512, SEQ)
    nc = _get_nc(SEQ)
    shared = dict(norm_w=f(norm_w[0]), w_in=f(w_in[0]), conv_w=f(conv_w[0]), conv_b=f(conv_b[0]),
                  dt_bias=f(dt_bias[0]), a_log=f(a_log[0]), d_skip=f(d_skip[0]), ssm_norm_w=f(ssm_norm_w[0]),
                  w_out_ssm=f(w_out_ssm[0]), rel_bias=f(rel_bias[0]), w_out_att=f(w_out_att[0]), w_o=f(w_o[0]),
                  final_norm_w=f(final_norm_w))
    in_maps = []
    for c in range(_NCORES):
        m = dict(shared)
        m['x_p'] = f(x_prompt[c])
        m['x_s'] = f(x_sample[4 * c:4 * c + 4]).reshape(256, D)
        m['st_ssm'] = f(state_ssm[0, 4 * c:4 * c + 4]).reshape(4, 2048, 128)
        m['st_conv'] = f(state_conv[0, 4 * c:4 * c + 4])
        m['ck'] = f(cache_k[0, 4 * c:4 * c + 4]).reshape(4, 512, D)
        m['cv'] = f(cache_v[0, 4 * c:4 * c + 4]).reshape(4, 512, D)
        in_maps.append(m)
    res = run_bass_kernel_spmd(nc, in_maps, core_ids=list(range(_NCORES)))
    R = list(res.results) + [res.results[0]] * (8 - _NCORES)
    y_prompt = np.stack([R[c]['y_p'] for c in range(8)]).astype(np.float32)
    y_sample = np.concatenate([R[c]['y_s'].reshape(4, 64, D) for c in range(8)]).astype(np.float32)
    ssm_p = np.stack([R[c]['ssm_p'].reshape(NH, 64, 128) for c in range(8)])[None].astype(np.float32)
    conv_p = np.stack([R[c]['conv_p'] for c in range(8)])[None].astype(np.float32)
    k_p = np.stack([R[c]['k_p'].reshape(KEEP, AH, 64) for c in range(8)])[None].astype(np.float32)
    v_p = np.stack([R[c]['v_p'].reshape(KEEP, AH, 64) for c in range(8)])[None].astype(np.float32)
    ssm_s = np.concatenate([R[c]['ssm_s'].reshape(4, NH, 64, 128) for c in range(8)])[None].astype(np.float32)
    conv_s = np.concatenate([R[c]['conv_s'] for c in range(8)])[None].astype(np.float32)
    k_s = np.concatenate([R[c]['k_s'].reshape(4, 512, AH, 64) for c in range(8)])[None].astype(np.float32)
    v_s = np.concatenate([R[c]['v_s'].reshape(4, 512, AH, 64) for c in range(8)])[None].astype(np.float32)
    return (y_prompt, y_sample, ssm_p, conv_p, k_p, v_p, ssm_s, conv_s, k_s, v_s)
```

```python
import numpy as np
from contextlib import ExitStack
import concourse.bass as bass
import concourse.mybir as mybir
from concourse.bass_utils import run_bass_kernel_spmd

F32 = mybir.dt.float32
BF16 = mybir.dt.bfloat16
U8 = mybir.dt.uint8
AF = mybir.ActivationFunctionType
ALU = mybir.AluOpType

D = 1024
KC = 8
DI = 2048
CD = 3072
NH = 32
AH = 16
DIN = 11296
C_Z, C_X, C_DT, C_Q, C_K, C_V, C_G, C_GS, C_GA = 0, 2048, 5120, 5152, 6176, 7200, 8224, 9248, 10272
EPS = 1e-5
NEG = -30000.0


def _dsz(dt):
    if dt == F32:
        return 4
    if dt == BF16:
        return 2
    if dt == U8:
        return 1
    raise ValueError(dt)


class _Op:
    __slots__ = ('eng', 'fn', 'deps', 'dma', 'key', 'val', 'seq', 'hasdep')


class Sched:
    def __init__(self, nc, es):
        self.nc = nc
        self.es = es
        self.ops = []
        self.engobj = {'pe': nc.tensor, 'act': nc.scalar, 'dve': nc.vector, 'pool': nc.gpsimd, 'sp': nc.sync}
        self.esem = {k: es.enter_context(nc.semaphore('es_' + k)) for k in self.engobj}
        self.dkeys = {}
        self.tracked = {}
        self.acc = {}

    def track(self, name, const=False):
        self.tracked[name] = 'const' if const else True

    def reg(self, ap):
        name = ap.tensor.name
        tr = self.tracked.get(name)
        if tr is None:
            return None
        dims = ap.ap
        ps = dims[0][0]
        npart = dims[0][1]
        off = ap.offset
        if ps > 0:
            p0 = off // ps
            f0 = off % ps
        else:
            p0 = 0
            f0 = off
        lo = f0
        hi = f0
        for (s, c) in dims[1:]:
            if s >= 0:
                hi += s * (c - 1)
            else:
                lo += s * (c - 1)
        sz = _dsz(ap.dtype)
        return (name, p0, p0 + npart, lo * sz, (hi + 1) * sz)

    def add(self, eng, fn, reads=(), writes=(), dma_key=None):
        i = len(self.ops)
        deps = set()
        rr = []
        ww = []
        for a in reads:
            r = a if isinstance(a, tuple) else self.reg(a)
            if r is not None:
                if r[0].startswith('pb'):
                    ww.append((r[0], 0, 128, 0, 2048))
                else:
                    rr.append(r)
        for a in writes:
            r = a if isinstance(a, tuple) else self.reg(a)
            if r is not None:
                if r[0].startswith('pb'):
                    r = (r[0], 0, 128, 0, 2048)
                ww.append(r)
        for (k, p0, p1, lo, hi) in rr:
            for e in self.acc.get(k, ()):
                if e[5] and e[0] < p1 and p0 < e[1] and e[2] < hi and lo < e[3]:
                    deps.add(e[4])
        for (k, p0, p1, lo, hi) in ww:
            lst = self.acc.get(k)
            if lst is None:
                continue
            keep = []
            for e in lst:
                if e[0] < p1 and p0 < e[1] and e[2] < hi and lo < e[3]:
                    deps.add(e[4])
                    if p0 <= e[0] and e[1] <= p1 and lo <= e[2] and e[3] <= hi:
                        continue
                keep.append(e)
            self.acc[k] = keep
        deps.discard(i)
        for (k, p0, p1, lo, hi) in rr:
            if self.tracked.get(k) == 'const':
                continue
            self.acc.setdefault(k, []).append([p0, p1, lo, hi, i, False])
        for (k, p0, p1, lo, hi) in ww:
            self.acc.setdefault(k, []).append([p0, p1, lo, hi, i, True])
        op = _Op()
        op.eng = eng
        op.fn = fn
        op.deps = deps
        op.dma = dma_key is not None
        op.key = dma_key
        op.val = 0
        op.seq = 0
        op.hasdep = False
        if op.dma:
            ent = self.dkeys.get(dma_key)
            if ent is None:
                ent = [self.es.enter_context(self.nc.semaphore('d_%d' % len(self.dkeys))), 0, []]
                self.dkeys[dma_key] = ent
            ent[1] += 16
            op.val = ent[1]
            ent[2].append(i)
        self.ops.append(op)
        return i

    def group_barrier(self, key):
        ent = self.dkeys[key]
        for i in ent[2]:
            self.ops[i].val = ent[1]

    def group_last(self, key, n):
        ent = self.dkeys[key]
        for i in ent[2][-n:]:
            self.ops[i].val = ent[1]

    def emit(self):
        ops = self.ops
        for op in ops:
            for d in op.deps:
                ops[d].hasdep = True
        cnt = {k: 0 for k in self.engobj}
        for op in ops:
            if not op.dma and op.hasdep:
                cnt[op.eng] += 1
                op.seq = cnt[op.eng]
        waited = {k: {} for k in self.engobj}
        nw = 0
        for op in ops:
            eng = self.engobj[op.eng]
            wd = waited[op.eng]
            need = {}
            for d in op.deps:
                dop = ops[d]
                if dop.dma:
                    sem, val = self.dkeys[dop.key][0], dop.val
                else:
                    if dop.eng == 'pe' and op.eng == 'pe' and not op.dma:
                        continue
                    sem, val = self.esem[dop.eng], dop.seq
                k = id(sem)
                if need.get(k, (None, 0))[1] < val:
                    need[k] = (sem, val)
            for k, (sem, val) in need.items():
                if wd.get(k, 0) >= val:
                    continue
                eng.wait_ge(sem, val)
                wd[k] = val
                nw += 1
            inst = op.fn(eng)
            if op.dma:
                inst.then_inc(self.dkeys[op.key][0], 16)
            elif op.hasdep:
                inst.then_inc(self.esem[op.eng], 1)
        sp = self.engobj['sp']
        for key, ent in self.dkeys.items():
            sp.wait_ge(ent[0], ent[1])
        return len(ops), nw

    def mms(self, items):
        reads = []
        writes = []
        for it in items:
            reads += [it['lhsT'], it['rhs']]
            writes.append(it['out'])
            if not it['start']:
                reads.append(it['out'])

        def fn(e, items=items):
            inst = None
            for it in items:
                inst = e.matmul(it['out'], lhsT=it['lhsT'], rhs=it['rhs'], start=it['start'], stop=it['stop'],
                                skip_group_check=True)
            return inst
        return self.add('pe', fn, reads, writes)

    def trs(self, items, ident):
        reads = [ident] + [x[1] for x in items]
        writes = [x[0] for x in items]

        def fn(e, items=items, ident=ident):
            inst = None
            for (o, i) in items:
                inst = e.transpose(o, i, ident)
            return inst
        return self.add('pe', fn, reads, writes)

    def act(self, out, in_, func, bias=None, scale=None, accum_out=None, extra_w=()):
        reads = [in_]
        kw = {}
        if bias is not None:
            kw['bias'] = bias
            if not isinstance(bias, (int, float)):
                reads.append(bias)
        if scale is not None:
            kw['scale'] = scale
            if not isinstance(scale, (int, float)):
                reads.append(scale)
        writes = [out] + list(extra_w)
        if accum_out is not None:
            kw['accum_out'] = accum_out
            writes.append(accum_out)
        return self.add('act', lambda e: e.activation(out=out, in_=in_, func=func, **kw), reads, writes)

    def tt(self, eng, out, in0, in1, op):
        return self.add(eng, lambda e: e.tensor_tensor(out=out, in0=in0, in1=in1, op=op), [in0, in1], [out])

    def ts(self, eng, out, in0, s1, s2, op0, op1=None):
        reads = [in0]
        if not isinstance(s1, (int, float)):
            reads.append(s1)
        if s2 is not None and not isinstance(s2, (int, float)):
            reads.append(s2)
        if op1 is None:
            return self.add(eng, lambda e: e.tensor_scalar(out=out, in0=in0, scalar1=s1, scalar2=None, op0=op0),
                            reads, [out])
        return self.add(eng, lambda e: e.tensor_scalar(out=out, in0=in0, scalar1=s1, scalar2=s2, op0=op0, op1=op1),
                        reads, [out])

    def stt(self, eng, out, in0, scalar, in1, op0, op1):
        reads = [in0, in1]
        if not isinstance(scalar, (int, float)):
            reads.append(scalar)
        return self.add(eng, lambda e: e.scalar_tensor_tensor(out=out, in0=in0, scalar=scalar, in1=in1, op0=op0,
                                                              op1=op1), reads, [out])

    def copy(self, eng, out, in_):
        if eng == 'act':
            return self.add('act', lambda e: e.activation(out=out, in_=in_, func=AF.Identity), [in_], [out])
        return self.add(eng, lambda e: e.tensor_copy(out=out, in_=in_), [in_], [out])

    def memset(self, eng, ap, val):
        return self.add(eng, lambda e: e.memset(ap, val), [], [ap])

    def recip(self, out, in_):
        return self.add('dve', lambda e: e.reciprocal(out=out, in_=in_), [in_], [out])

    def dma(self, q, out, in_, key, extra_r=(), extra_w=(), slow=False):
        if slow:
            fn = lambda e: e.dma_start(out=out, in_=in_, allow_slow_non_contiguous=True)
        else:
            fn = lambda e: e.dma_start(out=out, in_=in_)
        return self.add(q, fn, [in_] + list(extra_r), [out] + list(extra_w), dma_key=key)


def bc(ap, dims, off=0):
    return bass.AP(ap.tensor, ap.offset + off, dims)


def pstr(ap):
    return ap.ap[0][0]


class Arena:
    def __init__(self, nc, es, S, name, nbytes):
        self.t = es.enter_context(nc.sbuf_tensor(name, [128, nbytes], U8))
        self.off = 0
        self.cap = nbytes
        self.name = name
        S.track(name)

    def alloc(self, free_shape, dt):
        n = int(np.prod(free_shape)) * _dsz(dt)
        off = (self.off + 31) // 32 * 32
        assert off + n <= self.cap, (self.name, off, n, self.cap)
        self.off = off + n
        v = self.t[:, off:off + n].bitcast(dt)
        if len(free_shape) == 2:
            v = v.rearrange("p (a b) -> p a b", a=free_shape[0])
        elif len(free_shape) == 3:
            v = v.rearrange("p (a b c) -> p a b c", a=free_shape[0], b=free_shape[1])
        return v


_STOP = (99, 0)
_NCORES = 8
_DBG = ''
_ENV = {}
_DUMP = None


class _StopBuild(Exception):
    pass


def build(nc, SEQ, dbg=None):
    es = ExitStack()
    S = Sched(nc, es)
    try:
        _body(nc, SEQ, es, S, dbg)
    except _StopBuild:
        pass
    if _DUMP is not None:
        _DUMP(S, _ENV)
    nops, nw = S.emit()
    es.close()
    return nops, nw


def _body(nc, SEQ, es, S, dbg=None):
    NTP = SEQ // 256
    KEEP = min(512, SEQ)

    def din(name, shape):
        return nc.dram_tensor(name, shape, F32, kind="ExternalInput").ap()

    def dout(name, shape):
        return nc.dram_tensor(name, shape, F32, kind="ExternalOutput").ap()

    x_p = din("x_p", [SEQ, D])
    x_s = din("x_s", [256, D])
    st_ssm = din("st_ssm", [4, 2048, 128])
    st_conv = din("st_conv", [4, 3, CD])
    ck = din("ck", [4, 512, D])
    cv = din("cv", [4, 512, D])
    norm_w = din("norm_w", [D])
    w_in = din("w_in", [D, DIN])
    conv_w = din("conv_w", [4, CD])
    conv_b = din("conv_b", [CD])
    dt_bias = din("dt_bias", [NH])
    a_log = din("a_log", [NH])
    d_skip = din("d_skip", [NH])
    ssm_norm_w = din("ssm_norm_w", [DI])
    w_out_ssm = din("w_out_ssm", [DI, D])
    rel_bias = din("rel_bias", [AH, 513])
    w_out_att = din("w_out_att", [D, D])
    w_o = din("w_o", [D, D])
    final_norm_w = din("final_norm_w", [D])

    y_p = dout("y_p", [SEQ, D])
    y_s = dout("y_s", [256, D])
    ssm_p = dout("ssm_p", [2048, 128])
    conv_p = dout("conv_p", [3, CD])
    k_p = dout("k_p", [KEEP, D])
    v_p = dout("v_p", [KEEP, D])
    ssm_s = dout("ssm_s", [4, 2048, 128])
    conv_s = dout("conv_s", [4, 3, CD])
    k_s = dout("k_s", [4, 512, D])
    v_s = dout("v_s", [4, 512, D])

    WB = {}
    blist = []

    def addblk(name, src, nrow_chunks, c0, ncols, N):
        WB[name] = dict(idx=len(blist), src=src, kc=nrow_chunks, c0=c0, ncols=ncols, N=N)
        blist.append(name)

    for i in range(2):
        addblk('q%d' % i, w_in, 8, C_Q + 512 * i, 512, DIN)
    for i in range(2):
        addblk('k%d' % i, w_in, 8, C_K + 512 * i, 512, DIN)
    for i in range(2):
        addblk('v%d' % i, w_in, 8, C_V + 512 * i, 512, DIN)
    for i in range(2):
        addblk('g%d' % i, w_in, 8, C_G + 512 * i, 512, DIN)
    for i in range(2):
        addblk('ga%d' % i, w_in, 8, C_GA + 512 * i, 512, DIN)
    for i in range(2):
        addblk('wa%d' % i, w_out_att, 8, 512 * i, 512, D)
    addblk('dt', w_in, 8, C_DT, 32, DIN)
    for i in range(6):
        addblk('x%d' % i, w_in, 8, C_X + 512 * i, 512, DIN)
    for i in range(4):
        addblk('z%d' % i, w_in, 8, C_Z + 512 * i, 512, DIN)
    for i in range(2):
        addblk('gs%d' % i, w_in, 8, C_GS + 512 * i, 512, DIN)
    for i in range(4):
        addblk('ws%d' % i, w_out_ssm, 16, 256 * i, 256, D)
    for i in range(2):
        addblk('wo%d' % i, w_o, 8, 512 * i, 512, D)
    NB = len(blist)
    wscr = nc.dram_tensor("wscr", [NB, 128, 4096], BF16, kind="Internal").ap()
    ext = nc.dram_tensor("ext_bias", [AH, 1024], F32, kind="Internal").ap()

    def wscr_view(name):
        b = WB[name]
        i = b['idx']
        kc, nc_ = b['kc'], b['ncols']
        base = wscr[i]
        return bc(base, [[4096, 128], [nc_, kc], [1, nc_]])

    def cstop(n):
        if _STOP == (n, 0):
            raise _StopBuild()

    convlist = []

    def issue_conversion(name, nth_conv):
        b = WB[name]
        src = bass.AP(b['src'].tensor, b['c0'], [[b['N'], 128], [128 * b['N'], b['kc']], [1, b['ncols']]])
        extra_r = []
        if nth_conv >= 16:
            prev = WB[convlist[nth_conv - 16]]
            extra_r = [('wscr', prev['idx'], prev['idx'] + 1, 0, 1)]
        S.dma('pool', wscr_view(name), src, 'cv%d' % (nth_conv % 16), extra_r=extra_r,
              extra_w=[('wscr', b['idx'], b['idx'] + 1, 0, 1)])
        convlist.append(name)

    CA = Arena(nc, es, S, "cst", 29 * 1024)
    S.tracked["cst"] = True
    A = Arena(nc, es, S, "main", 183040)

    identb = CA.alloc([128], BF16)
    identf = CA.alloc([128], F32)
    Jx = CA.alloc([128], BF16)
    tri = CA.alloc([128], BF16)
    SU = CA.alloc([128], BF16)
    onesb = CA.alloc([128], BF16)
    scr32 = CA.alloc([128], F32)
    cstg = CA.alloc([128], F32)
    cpar = CA.alloc([144], F32)
    normw = cpar[:, 0:8]
    nwssm = cpar[:, 8:24]
    convb = cpar[:, 24:48]
    convw = cpar[:, 48:144].rearrange("p (j c) -> p j c", j=4)
    dtb_bc = CA.alloc([32], F32)
    a_bc = CA.alloc([32], F32)
    dsk_bc = CA.alloc([32], F32)
    fnw_bc = CA.alloc([1024], F32)
    neghalf = CA.alloc([4], F32)
    biasF = CA.alloc([AH, 5, 128], BF16)
    convst = CA.alloc([4, 3, 24], BF16)

    xt = [A.alloc([1024], F32) for _ in range(4)]
    xsn = A.alloc([1024], BF16)
    TH = [A.alloc([512], BF16) for _ in range(3)]
    junk = bc(TH[0], [[pstr(TH[0]), 128], [1, 1024]])
    xnT = A.alloc([8, 256], BF16)
    NSLOT = 3
    Wr = [A.alloc([4096], BF16) for _ in range(NSLOT)]
    KT = A.alloc([8, 6, 128], BF16)
    Vr = A.alloc([6, AH, 66], BF16)
    hT = A.alloc([2048], F32)
    hTb = A.alloc([2048], BF16)
    ost = A.alloc([1024], F32)
    ogT = A.alloc([8, 256], BF16)
    ga_sig = A.alloc([8, 256], BF16)
    gs_sig = A.alloc([8, 256], BF16)
    accT = A.alloc([8, 256], BF16)
    yzT = A.alloc([16, 256], BF16)
    mT = A.alloc([8, 256], BF16)
    smallf = A.alloc([64], F32)
    carry = A.alloc([24, 3], BF16)
    base = A.off
    QT = A.alloc([8, 256], BF16)
    PT = [A.alloc([512], BF16) for _ in range(3)]
    rs = A.alloc([16], F32)
    o_n = A.alloc([1024], BF16)
    og = A.alloc([1024], BF16)
    gs2 = A.alloc([2, 1024], BF16)
    kvst = [A.alloc([512], F32) for _ in range(2)]
    ckst = A.alloc([4, 1024], BF16)
    KTs = A.alloc([8, 2, 64], BF16)
    Vs = A.alloc([2, AH, 66], BF16)
    att_end = A.off
    A.off = base
    zs2 = A.alloc([2, 2048], BF16)
    xs_tok = A.alloc([2048], BF16)
    B_tok = A.alloc([512], BF16)
    xdtb = [A.alloc([512], BF16) for _ in range(2)]
    xdt2b = [A.alloc([512], BF16) for _ in range(2)]
    xsdb = [A.alloc([512], BF16) for _ in range(2)]
    cbTm = A.alloc([4, 128], BF16)
    Bmh = A.alloc([8, 128], BF16)
    Bml = A.alloc([8, 128], BF16)
    Eb = A.alloc([8, 128], BF16)
    MTb = A.alloc([8, 128], BF16)
    yz = A.alloc([4, 512], BF16)
    ssq = A.alloc([4], F32)
    rstd4 = A.alloc([4], F32)
    A.off = max(A.off, att_end)
    XW = 3 + 256
    xraw = A.alloc([24, XW], BF16)
    cacc = [A.alloc([256], F32) for _ in range(2)]
    dtt = A.alloc([2, 32], F32)
    adt = A.alloc([2, 32], F32)
    adt_hi = A.alloc([2, 32], BF16)
    adt_lo = A.alloc([2, 32], BF16)
    dtmp = A.alloc([32], F32)
    ee = A.alloc([64], F32)
    ee2 = A.alloc([32], F32)
    xbcst = [A.alloc([512], F32) for _ in range(2)]
    stT = A.alloc([16, 128], F32)
    ssd_end = A.off
    A.off = max(att_end, ssd_end)
    print('SBUF use: cst', CA.off, 'main', A.off, 'att', att_end - base, 'ssd', ssd_end - base)

    _ENV.update(dict(accT=accT, mT=mT, yzT=yzT, ogT=ogT, gs_sig=gs_sig, ga_sig=ga_sig, xnT=xnT, QT=QT, zs2=zs2,
                     yz=yz, ost=ost, gs2=gs2, o_n=o_n, og=og, hT=hT, xraw=xraw, nc=nc, A=A, xt=xt))
    PB = []
    for i in range(8):
        t = es.enter_context(nc.psum_tensor("pb%d" % i, [128, 2048], U8))
        S.track("pb%d" % i)
        PB.append(t)

    def pf(i):
        return PB[i][:, :].bitcast(F32)

    def pbf(i):
        return PB[i][:, :].bitcast(BF16)

    rot = {'mm': [0, 1, 3, 4], 'tp': [2, 0, 1], 'S': [3, 4]}
    rotp = {k: 0 for k in rot}

    def nb(role):
        i = rot[role][rotp[role] % len(rot[role])]
        rotp[role] += 1
        return i
    OB = [5, 6, 7]
    SEGB = [3, 4]
    YB, UB, CBB, SMB = 5, 7, 6, 2

    P = 'pool'
    S.memset(P, scr32, 0.0)
    S.add(P, lambda e: e.affine_select(out=scr32, in_=scr32, pattern=[[-1, 128]], compare_op=ALU.not_equal, fill=1.0,
                                       base=0, channel_multiplier=1), [scr32], [scr32])
    S.copy('dve', identb, scr32)
    S.copy('dve', identf, scr32)
    S.memset(P, scr32, 0.0)
    S.add(P, lambda e: e.affine_select(out=scr32, in_=scr32, pattern=[[1, 128]], compare_op=ALU.not_equal, fill=1.0,
                                       base=-127, channel_multiplier=1), [scr32], [scr32])
    S.copy('dve', Jx, scr32)
    S.memset(P, scr32, 1.0)
    S.copy('dve', onesb, scr32)
    S.add(P, lambda e: e.affine_select(out=scr32, in_=scr32, pattern=[[1, 128]], compare_op=ALU.is_ge, fill=0.0,
                                       base=0, channel_multiplier=-1), [scr32], [scr32])
    S.copy('dve', tri, scr32)
    S.memset(P, scr32, 1.0)
    S.add(P, lambda e: e.affine_select(out=scr32, in_=scr32, pattern=[[-1, 128]], compare_op=ALU.is_ge, fill=0.0,
                                       base=-1, channel_multiplier=1), [scr32], [scr32])
    S.copy('dve', SU, scr32)
    S.memset(P, neghalf, -0.5)
    S.memset(P, hT, 0.0)
    S.memset(P, hTb, 0.0)
    S.memset(P, carry, 0.0)
    S.memset(P, Vr[:, :, :, 64:65], 1.0)

    cstop(-2)
    CK = 'const'
    S.dma('sp', dtb_bc, bass.AP(dt_bias.tensor, 0, [[0, 128], [1, 32]]), CK)
    S.dma('sp', a_bc, bass.AP(a_log.tensor, 0, [[0, 128], [1, 32]]), CK)
    S.dma('sp', dsk_bc, bass.AP(d_skip.tensor, 0, [[0, 128], [1, 32]]), CK)
    S.dma('sp', fnw_bc, bass.AP(final_norm_w.tensor, 0, [[0, 128], [1, 1024]]), CK)
    S.group_barrier(CK)
    cstop(-3)
    S.dma('sp', cstg[0:8, :], bass.AP(norm_w.tensor, 0, [[128, 8], [1, 128]]), 'cstg')
    S.dma('sp', cstg[8:24, :], bass.AP(ssm_norm_w.tensor, 0, [[128, 16], [1, 128]]), 'cstg')
    S.dma('sp', cstg[24:48, :], bass.AP(conv_b.tensor, 0, [[128, 24], [1, 128]]), 'cstg')
    S.group_last('cstg', 3)
    S.trs([(pf(0)[:, 0:48], cstg[0:48, :])], identf[0:48, 0:48])
    S.copy('dve', cpar[:, 0:48], pf(0)[:, 0:48])
    S.dma('sp', cstg[0:96, :], bass.AP(conv_w.tensor, 0, [[128, 96], [1, 128]]), 'cstg')
    S.trs([(pf(0)[:, 0:96], cstg[0:96, :])], identf[0:96, 0:96])
    S.copy('dve', cpar[:, 48:144], pf(0)[:, 0:96])
    for c3 in range(3):
        S.dma('sp', cstg[0:96, :], bass.AP(st_conv.tensor, c3 * 96 * 128, [[128, 96], [1, 128]]), 'cstg')
        S.trs([(pf(0)[:, 0:96], cstg[0:96, :])], identf[0:96, 0:96])
        cf = convst.rearrange("p s j c -> p (s j c)")
        S.copy('dve', cf[:, c3 * 96:(c3 + 1) * 96], pf(0)[:, 0:96])
    cstop(-4)
    EXT = ('ext', 0, 1, 0, 1)
    S.dma('sp', xt[2][0:16, 0:513], rel_bias, 'xt2')
    S.copy('dve', xt[1][0:16, 0:513], xt[2][0:16, 0:513])
    rb512 = xt[2][0:16, 512:513]
    S.copy('dve', xt[1][0:16, 513:1024], bc(rb512, [[rb512.ap[0][0], 16], [0, 511]]))
    S.dma('sp', ext, xt[1][0:16, 0:1024], 'xt1', extra_w=[EXT])
    stgv = bc(xt[2], [[pstr(xt[2]), 128], [128, AH], [1, 128]])
    for t in range(5):
        src = bass.AP(ext.tensor, 641 - 128 * t, [[1, 128], [1024, AH], [1, 128]])
        S.dma('sp', stgv, src, 'xt2', extra_r=[EXT])
        S.copy('dve', biasF[:, :, t, :], stgv)
    S.memset(P, biasF[64:128, :, 0, 64:128], NEG)
    S.memset(P, biasF[0:64, :, 4, 0:64], NEG)
    cstop(-5)
    S.act(a_bc, a_bc, AF.Exp)
    S.ts('dve', a_bc, a_bc, -1.0, None, ALU.mult)
    S.ts('dve', convw, convw, 0.5, None, ALU.mult)
    S.ts('dve', convb, convb, 0.5, None, ALU.mult)

    S.tracked['cst'] = 'const'

    uses = []
    tiles = [('p', i) for i in range(NTP)] + [('s', 0), ('s', 1)]
    per_tile = ['q0', 'q1', 'k0', 'k1', 'v0', 'v1', 'g0', 'g1', 'ga0', 'ga1', 'dt',
                'x0', 'x1', 'x2', 'x3', 'x4', 'x5', 'wa0', 'wa1', 'z0', 'z1', 'z2', 'z3', 'gs0', 'gs1',
                'ws0', 'ws1', 'ws2', 'ws3', 'wo0', 'wo1']
    for _ in tiles:
        uses += per_tile
    wstate = {'issued': 0, 'cur': 0, 'conv': set()}

    def w_issue(upto):
        while wstate['issued'] < min(upto, len(uses)):
            u = wstate['issued']
            name = uses[u]
            b = WB[name]
            slot = u % NSLOT
            for name2 in uses[u:u + 10]:
                if name2 not in wstate['conv']:
                    issue_conversion(name2, len(wstate['conv']))
                    wstate['conv'].add(name2)
            n = b['kc'] * b['ncols']
            dst = bc(Wr[slot], [[pstr(Wr[slot]), 128], [1, n]])
            srcv = bc(wscr[b['idx']], [[4096, 128], [1, n]])
            S.dma('sp', dst, srcv, 'W%d' % slot, extra_r=[('wscr', b['idx'], b['idx'] + 1, 0, 1)])
            wstate['issued'] += 1

    def w_use(name, issue=True):
        u = wstate['cur']
        assert uses[u] == name, (uses[u], name)
        if issue:
            w_issue(u + NSLOT)
        wstate['cur'] += 1
        b = WB[name]
        slot = u % NSLOT
        return bc(Wr[slot], [[pstr(Wr[slot]), 128], [b['ncols'], b['kc']], [1, b['ncols']]])

    def xrows(kind, idx, sub, L):
        if kind == 'p':
            r0 = idx * 256 + sub * 128
            return x_p[r0:r0 + L, :], y_p[r0:r0 + L, :]
        seq = idx * 2 + sub
        return x_s[seq * 64:seq * 64 + 64, :], y_s[seq * 64:seq * 64 + 64, :]

    def load_x(ti):
        kind, idx = tiles[ti]
        L = 128 if kind == 'p' else 64
        for sub in range(2):
            b = (ti % 2) * 2 + sub
            src, _ = xrows(kind, idx, sub, L)
            S.dma('sp', xt[b][0:L, :], src, 'xt%d' % b)

    def rms_rstd(src_sum, n, eps, dst, L, col=1):
        S.ts('dve', dst, src_sum, 1.0 / n, eps, ALU.mult, ALU.add)
        S.tt('pool', dst, dst, neghalf[0:L, 0:col], ALU.pow)

    def fm_block(wv, ncols_tiles, T, evac):
        for et in range(ncols_tiles):
            bk = nb('mm')
            out = pf(bk)[:, 0:T]
            S.mms([dict(out=out, lhsT=wv[:, kc, et * 128:(et + 1) * 128], rhs=xnT[:, kc, 0:T],
                        start=(kc == 0), stop=(kc == 7)) for kc in range(8)])
            evac(et, out)

    def tm_block(wv, ncols, L, subs, evac):
        for sub in subs:
            bk = nb('mm')
            out = pf(bk)[0:L, 0:ncols]
            S.mms([dict(out=out, lhsT=xnT[:, kc, sub * L:(sub + 1) * L], rhs=wv[:, kc, 0:ncols],
                        start=(kc == 0), stop=(kc == 7)) for kc in range(8)])
            evac(sub, out)

    thp = [0]

    def nth():
        thp[0] += 1
        return TH[thp[0] % 3]

    def store_state(dst_ap):
        for q in range(4):
            bk = nb('mm')
            S.trs([(pf(bk)[:, j * 128:(j + 1) * 128], hT[:, (q * 4 + j) * 128:(q * 4 + j + 1) * 128])
                   for j in range(4)], identf)
            S.copy('dve', stT[:, q * 4:(q + 1) * 4, :], pf(bk)[:, 0:512].rearrange("p (a b) -> p a b", a=4))
        S.dma('pool', bass.AP(dst_ap.tensor, dst_ap.offset, [[128, 128], [128 * 128, 16], [1, 128]]), stT, 'stT')

    def load_state(seq):
        src = st_ssm[seq]
        if 'nold' not in _DBG:
            S.dma('pool' if 'poolld' in _DBG else 'sp', stT,
                  bass.AP(src.tensor, src.offset, [[128, 128], [128 * 128, 16], [1, 128]]), 'stT')
        if 'notr' in _DBG:
            return
        for q in range(4):
            bk = nb('mm')
            S.trs([(pf(bk)[:, j * 128:(j + 1) * 128], stT[:, q * 4 + j, :]) for j in range(4)], identf)
            if 'nocp' in _DBG:
                continue
            S.copy('dve', hT[:, q * 512:(q + 1) * 512], pf(bk)[:, 0:512])
            if 'noact' in _DBG:
                continue
            S.copy('act', hTb[:, q * 512:(q + 1) * 512], pf(bk)[:, 0:512])

    def norm_tile(tj):
        kind_, idx_ = tiles[tj]
        L = 128 if kind_ == 'p' else 64
        xb = [xt[(tj % 2) * 2 + sub] for sub in range(2)]
        for sub in range(2):
            ssx = smallf[0:L, 0:1]
            S.act(junk[0:L, :], xb[sub][0:L, :], AF.Square, accum_out=ssx)
            rms_rstd(ssx, 1024.0, EPS, smallf[0:L, 1:2], L)
            S.act(xsn[0:L, :], xb[sub][0:L, :], AF.Identity, scale=smallf[0:L, 1:2])
            bk = nb('tp')
            S.trs([(pbf(bk)[:, kc * L:(kc + 1) * L], xsn[0:L, kc * 128:(kc + 1) * 128]) for kc in range(8)],
                  identb[0:L, 0:L])
            S.tt('dve', xnT[:, :, sub * L:(sub + 1) * L], pbf(bk)[:, 0:8 * L].rearrange("p (a b) -> p a b", a=8),
                 bc(normw, [[pstr(normw), 128], [1, 8], [0, L]]), ALU.mult)


    def stop(n, ti):
        if _STOP == (n, ti):
            raise _StopBuild()

    load_x(0)
    norm_tile(0)
    for ti, (kind, idx) in enumerate(tiles):
        stop(0, ti)
        L = 128 if kind == 'p' else 64
        T = 2 * L
        last_p = (kind == 'p' and idx == NTP - 1)
        def kv_out_row(sub):
            if kind == 'p':
                r0 = idx * 256 + sub * 128
                if r0 >= SEQ - KEEP:
                    return r0 - (SEQ - KEEP)
                return None
            return 448
        if ti + 1 < len(tiles):
            load_x(ti + 1)
        xb = [xt[(ti % 2) * 2 + sub] for sub in range(2)]

        stop(1, ti)
        if kind == 'p':
            g0 = idx * 2
            slot0 = g0 % 6
        for qi in range(2):
            wv = w_use('q%d' % qi)

            def ev(et, out, qi=qi):
                S.act(QT[:, qi * 4 + et, 0:T], out, AF.Identity, scale=0.125)
            fm_block(wv, 4, T, ev)
        stop(1.2, ti)
        for ki in range(2):
            wv = w_use('k%d' % ki)

            def ev(et, out, ki=ki):
                pair = ki * 4 + et
                if kind == 'p':
                    S.copy('act', KT[:, pair, slot0:slot0 + 2, :], out.rearrange("p (s l) -> p s l", s=2))
                else:
                    S.copy('dve', KTs[:, pair, :, :], out.rearrange("p (s l) -> p s l", s=2))
            fm_block(wv, 4, T, ev)
            subs = [s for s in range(2) if kv_out_row(s) is not None]

            def evk(sub, out, ki=ki):
                st = kvst[(ki + sub) % 2]
                S.copy('dve', st[0:L, :], out)
                r = kv_out_row(sub)
                if kind == 'p':
                    S.dma('pool', k_p[r:r + L, ki * 512:(ki + 1) * 512], st[0:L, :], 'kvst%d' % ((ki + sub) % 2))
                else:
                    seq = idx * 2 + sub
                    S.dma('pool', k_s[seq, 448:512, ki * 512:(ki + 1) * 512], st[0:L, :],
                          'kvst%d' % ((ki + sub) % 2))
            if subs:
                tm_block(wv, 512, L, subs, evk)
        stop(1.4, ti)
        if kind == 's':
            S.memset('pool', Vs[:, :, :, 64:65], 1.0)
        for vi in range(2):
            wv = w_use('v%d' % vi)

            def evv(sub, out, vi=vi):
                if kind == 'p':
                    dst = Vr[0:L, slot0 + sub, vi * 8:(vi + 1) * 8, 0:64]
                else:
                    dst = Vs[0:L, sub, vi * 8:(vi + 1) * 8, 0:64]
                S.copy('act', dst, out.rearrange("p (h d) -> p h d", h=8))
                r = kv_out_row(sub)
                if r is not None:
                    st = kvst[(vi + sub) % 2]
                    S.copy('dve', st[0:L, :], out)
                    if kind == 'p':
                        S.dma('pool', v_p[r:r + L, vi * 512:(vi + 1) * 512], st[0:L, :],
                              'kvst%d' % ((vi + sub) % 2))
                    else:
                        seq = idx * 2 + sub
                        S.dma('pool', v_s[seq, 448:512, vi * 512:(vi + 1) * 512], st[0:L, :],
                              'kvst%d' % ((vi + sub) % 2))
            tm_block(wv, 512, L, [0, 1], evv)
        stop(1.6, ti)
        for gi in range(2):
            wv = w_use('g%d' % gi)

            def evg(sub, out, gi=gi):
                th = nth()
                S.act(th[0:L, :], out, AF.Tanh, scale=0.5)
                S.stt('dve', gs2[0:L, sub, gi * 512:(gi + 1) * 512], th[0:L, :], 1.0, out, ALU.add, ALU.mult)
            tm_block(wv, 512, L, [0, 1], evg)
        stop(1.8, ti)
        for gi in range(2):
            wv = w_use('ga%d' % gi)

            def evga(et, out, gi=gi):
                th = nth()
                S.act(th[:, 0:T], out, AF.Tanh, scale=0.5)
                S.act(ga_sig[:, gi * 4 + et, 0:T], th[:, 0:T], AF.Identity, scale=0.25, bias=0.25)
            fm_block(wv, 4, T, evga)

        stop(1.9, ti)
        wv = w_use('dt')

        def evdt(sub, out):
            S.tt('dve', dtmp[0:L, :], out, dtb_bc[0:L, :], ALU.add)
            S.act(dtmp[0:L, :], dtmp[0:L, :], AF.Exp)
            S.act(dtt[0:L, sub, :], dtmp[0:L, :], AF.Ln, bias=1.0)
            S.tt('dve', adt[0:L, sub, :], dtt[0:L, sub, :], a_bc[0:L, :], ALU.mult)
            S.copy('dve', adt_hi[0:L, sub, :], adt[0:L, sub, :])
            S.tt('dve', adt_lo[0:L, sub, :], adt[0:L, sub, :], adt_hi[0:L, sub, :], ALU.subtract)
        tm_block(wv, 32, L, [0, 1], evdt)

        if kind == 'p':
            segs = [(0, T)]
            SW = T + 3
            S.copy('pool', xraw[:, :, 0:3], carry)
        else:
            SW = 67
            segs = [(0, 64), (67, 64)]
            for s in range(2):
                S.copy('pool', xraw[:, :, s * 67:s * 67 + 3], convst[:, idx * 2 + s].rearrange("p j c -> p c j"))
        for xi in range(6):
            wv = w_use('x%d' % xi)

            def evx(et, out, xi=xi):
                ct = xi * 4 + et
                if kind == 'p':
                    S.copy('act', xraw[:, ct, 3:3 + T], out)
                else:
                    S.copy('dve', xraw[:, ct, 0:134].rearrange("p (s c) -> p s c", c=67)[:, :, 3:67],
                           out.rearrange("p (s c) -> p s c", c=64))
            fm_block(wv, 4, T, evx)
            if last_p or kind == 's':
                osubs = [1] if kind == 'p' else [0, 1]

                def evc(sub, out, xi=xi):
                    st = xbcst[(xi + sub) % 2]
                    S.copy('dve', st[L - 32:L, :], out[L - 32:L, :])
                    dst = conv_p if kind == 'p' else conv_s[idx * 2 + sub]
                    S.dma('pool', dst[:, xi * 512:(xi + 1) * 512], st[L - 3:L, :], 'xbcst%d' % ((xi + sub) % 2))
                tm_block(wv, 512, L, osubs, evc)
        def conv_block(xj):
            accs = [cacc[0], cacc[1], xbcst[0][:, 0:256], xbcst[1][:, 0:256]]
            for (c0, nt) in segs:
                cts = [xj * 4 + e for e in range(4)]
                for e, ct in enumerate(cts):
                    S.ts('dve', accs[e][:, 0:nt], xraw[:, ct, c0:c0 + nt], convw[:, 0, ct:ct + 1],
                         convb[:, ct:ct + 1], ALU.mult, ALU.add)
                for j in range(1, 4):
                    for e, ct in enumerate(cts):
                        S.stt('dve', accs[e][:, 0:nt], xraw[:, ct, c0 + j:c0 + j + nt],
                              convw[:, j, ct:ct + 1], accs[e][:, 0:nt], ALU.mult, ALU.add)
                if kind == 'p' and c0 == 0:
                    S.copy('pool', carry[:, xj * 4:(xj + 1) * 4, :], xraw[:, xj * 4:(xj + 1) * 4, T:T + 3])
                ths = []
                for e, ct in enumerate(cts):
                    th = nth()
                    ths.append(th)
                    S.act(th[:, 0:nt], accs[e][:, 0:nt], AF.Tanh)
                    if e >= 1:
                        S.stt('dve', xraw[:, cts[e - 1], c0:c0 + nt], ths[e - 1][:, 0:nt], 1.0,
                              accs[e - 1][:, 0:nt], ALU.add, ALU.mult)
                S.stt('dve', xraw[:, cts[3], c0:c0 + nt], ths[3][:, 0:nt], 1.0, accs[3][:, 0:nt],
                      ALU.add, ALU.mult)


        conv_todo = [0, 1, 2, 3, 4, 5]
        stop(2, ti)
        for sub in range(2):
            if kind == 'p':
                g = idx * 2 + sub
                tl = [t for t in range(5) if g - 4 + t >= 0]
            else:
                seq = idx * 2 + sub
                tl = [0, 1, 2, 3] if 'not4' in _DBG else ([4] if 'only4' in _DBG else [0, 1, 2, 3, 4])
                kv32 = bc(kvst[0], [[pstr(kvst[0]), 128], [1, 1024]])
                for tt_ in range(4):
                    S.dma('sp', kv32, cv[seq, tt_ * 128:(tt_ + 1) * 128, :], 'kvst0')
                    S.copy('dve', Vr[:, tt_, :, 0:64], kv32.rearrange("p (h d) -> p h d", h=AH))
                    S.dma('sp', ost, ck[seq, tt_ * 128:(tt_ + 1) * 128, :], 'ost')
                    S.copy('act', ckst[:, tt_, :], ost)
                    bk = nb('tp')
                    S.trs([(pbf(bk)[:, pr * 128:(pr + 1) * 128], ckst[:, tt_, pr * 128:(pr + 1) * 128])
                           for pr in range(8)], identb)
                    S.copy('dve', KT[:, :, tt_, :], pbf(bk)[:, 0:1024].rearrange("p (a b) -> p a b", a=8))
                S.dma('pool', k_s[seq, 0:448, :], ck[seq, 64:512, :], 'd2d')
                S.dma('pool', v_s[seq, 0:448, :], cv[seq, 64:512, :], 'd2d')
            stop(2.2, ti)
            for ob in OB:
                S.memset('dve', pf(ob)[0:L, :], 0.0)
            blocks = [(h, t) for h in range(AH) for t in tl]
            per = 512 // L
            groups = [blocks[i:i + per] for i in range(0, len(blocks), per)]
            pend = None

            def emit_pv(pg):
                grp, pt = pg
                if 'nopv' in _DBG:
                    return
                items = []
                for cb, (h, t) in enumerate(grp):
                    Mk = 64 if (kind == 's' and t == 4) else 128
                    if kind == 'p':
                        vt = Vr[0:Mk, (g - 4 + t) % 6, h, 0:65]
                    elif t < 4:
                        vt = Vr[0:Mk, t, h, 0:65]
                    else:
                        vt = Vs[0:Mk, sub, h, 0:65]
                    ob = OB[h // 6]
                    oc = (h % 6) * 65
                    items.append(dict(out=pf(ob)[0:L, oc:oc + 65], lhsT=pt[0:Mk, cb * L:(cb + 1) * L], rhs=vt,
                                      start=False, stop=True))
                S.mms(items)
            for gi_, grp in enumerate(groups):
                sbk = nb('S')
                items = []
                for cb, (h, t) in enumerate(grp):
                    pair, pb_ = h // 2, (h % 2) * 64
                    Mk = 64 if (kind == 's' and t == 4) else 128
                    out = pf(sbk)[0:Mk, cb * L:(cb + 1) * L]
                    if Mk == 128:
                        items.append(dict(out=out, lhsT=Jx, rhs=biasF[:, h, t, 0:L], start=True, stop=False))
                    else:
                        items.append(dict(out=out, lhsT=Jx[:, 0:64], rhs=biasF[:, h, t, 0:L],
                                          start=True, stop=False))
                    if kind == 'p':
                        kt = KT[pb_:pb_ + 64, pair, (g - 4 + t) % 6, 0:Mk]
                    elif t < 4:
                        kt = KT[pb_:pb_ + 64, pair, t, 0:Mk]
                    else:
                        kt = KTs[pb_:pb_ + 64, pair, sub, 0:Mk]
                    items.append(dict(out=out, lhsT=kt, rhs=QT[pb_:pb_ + 64, pair, sub * L:(sub + 1) * L],
                                      start=False, stop=True))
                S.mms(items)
                pt = PT[gi_ % 3]
                ncol = len(grp) * L
                S.act(pt[:, 0:ncol], pf(sbk)[:, 0:ncol], AF.Exp)
                if pend is not None:
                    emit_pv(pend)
                pend = (grp, pt)
                if conv_todo and (gi_ % 6 == 5 or len(groups) < 12):
                    conv_block(conv_todo.pop(0))
            emit_pv(pend)
            stop(2.4, ti)
            for obi, ob in enumerate(OB):
                h0 = obi * 6
                nh_ = min(6, AH - h0)
                ov = pf(ob)[0:L, 0:nh_ * 65].rearrange("p (h d) -> p h d", h=nh_)
                S.recip(rs[0:L, h0:h0 + nh_], ov[:, :, 64])
                S.tt('dve', o_n[0:L, h0 * 64:(h0 + nh_) * 64].rearrange("p (h d) -> p h d", h=nh_), ov[:, :, 0:64],
                     bc(rs[0:L, h0:h0 + nh_], [[pstr(rs), L], [1, nh_], [0, 64]]), ALU.mult)
            S.tt('pool', og[0:L, :], o_n[0:L, :], gs2[0:L, sub, :], ALU.mult)
            bk = nb('tp')
            S.trs([(pbf(bk)[:, kc * L:(kc + 1) * L], og[0:L, kc * 128:(kc + 1) * 128]) for kc in range(8)],
                  identb[0:L, 0:L])
            S.copy('dve', ogT[:, :, sub * L:(sub + 1) * L], pbf(bk)[:, 0:8 * L].rearrange("p (a b) -> p a b", a=8))

        while conv_todo:
            conv_block(conv_todo.pop(0))
        stop(3, ti)
        stop(4, ti)
        for wi in range(2):
            wv = w_use('wa%d' % wi)
            for et in range(4):
                bk = nb('mm')
                out = pf(bk)[:, 0:T]
                S.mms([dict(out=out, lhsT=wv[:, kc, et * 128:(et + 1) * 128], rhs=ogT[:, kc, 0:T],
                            start=(kc == 0), stop=(kc == 7)) for kc in range(8)])
                S.tt('dve', accT[:, wi * 4 + et, 0:T], out, ga_sig[:, wi * 4 + et, 0:T], ALU.mult)

        stop(5, ti)
        for zi in range(4):
            wv = w_use('z%d' % zi)

            def evz(sub, out, zi=zi):
                th = nth()
                S.act(th[0:L, :], out, AF.Tanh, scale=0.5)
                S.stt('dve', zs2[0:L, sub, zi * 512:(zi + 1) * 512], th[0:L, :], 1.0, out, ALU.add, ALU.mult)
            tm_block(wv, 512, L, [0, 1], evz)
        for gi in range(2):
            wv = w_use('gs%d' % gi)

            def evgs(et, out, gi=gi):
                th = nth()
                S.act(th[:, 0:T], out, AF.Tanh, scale=0.5)
                S.act(gs_sig[:, gi * 4 + et, 0:T], th[:, 0:T], AF.Identity, scale=0.5, bias=0.5)
            fm_block(wv, 4, T, evgs)

        if ti + 1 < len(tiles):
            norm_tile(ti + 1)
        stop(6, ti)
        def cview(t, n):
            return bc(t, [[pstr(t), L], [L, n], [1, L]])
        Bmh_v, Bml_v, Eb_v, MT_v, cb_v = cview(Bmh, 8), cview(Bml, 8), cview(Eb, 8), cview(MTb, 8), cview(cbTm, 4)
        for sub in range(2):
            if kind == 'p':
                cs = sub * 128
            else:
                cs = sub * 67
                load_state(idx * 2 + sub)
            stop(6.2, ti)
            for hb in range(2):
                bk = nb('tp')
                S.trs([(pbf(bk)[0:L, j * 128:(j + 1) * 128], xraw[:, hb * 8 + j, cs:cs + L]) for j in range(8)],
                      identb)
                S.copy('act' if hb else 'dve', xs_tok[0:L, hb * 1024:(hb + 1) * 1024], pbf(bk)[0:L, 0:1024])
            bk = nb('tp')
            S.trs([(pbf(bk)[0:L, j * 128:(j + 1) * 128], xraw[:, 16 + j, cs:cs + L]) for j in range(4)], identb)
            S.copy('dve', B_tok[0:L, :], pbf(bk)[0:L, 0:512])
            stop(6.4, ti)
            sm = pf(SMB)
            ah, al = adt_hi[0:L, sub, :], adt_lo[0:L, sub, :]
            S.mms([dict(out=sm[0:L, 0:32], lhsT=tri[0:L, 0:L], rhs=ah, start=True, stop=False),
                   dict(out=sm[0:L, 0:32], lhsT=tri[0:L, 0:L], rhs=al, start=False, stop=True),
                   dict(out=sm[0:L, 32:64], lhsT=SU[0:L, 0:L], rhs=ah, start=True, stop=False),
                   dict(out=sm[0:L, 32:64], lhsT=SU[0:L, 0:L], rhs=al, start=False, stop=True),
                   dict(out=sm[:, 64:96], lhsT=onesb[0:L, :], rhs=ah, start=True, stop=False),
                   dict(out=sm[:, 64:96], lhsT=onesb[0:L, :], rhs=al, start=False, stop=True)])
            S.act(ee[0:L, :], sm[0:L, 0:64], AF.Exp)
            S.act(ee2, sm[:, 64:96], AF.Exp)
            cbb = pf(CBB)
            S.mms([dict(out=cbb[0:L, gq * L:(gq + 1) * L], lhsT=xraw[:, 16 + gq, cs:cs + L],
                        rhs=xraw[:, 20 + gq, cs:cs + L], start=True, stop=True) for gq in range(4)])
            S.tt('dve', cb_v, cbb[0:L, 0:4 * L].rearrange("p (g l) -> p g l", g=4),
                 bc(tri, [[pstr(tri), L], [0, 4], [1, L]]), ALU.mult)
            stop(6.6, ti)
            def b8(ap2):
                return bc(ap2, [[ap2.ap[0][0], L], [1, 8], [0, 64]])
            x3 = lambda a: a[0:L, :].rearrange("p (r d) -> p r d", r=8)
            nhalf = 2 if L == 128 else 1
            hh = 8 // nhalf

            def s0_pool(gq):
                xdt, xdt2, xsd = xdtb[gq % 2], xdt2b[gq % 2], xsdb[gq % 2]
                xsg = xs_tok[0:L, gq * 512:(gq + 1) * 512].rearrange("p (r d) -> p r d", r=8)
                S.tt('pool', x3(xdt), xsg, b8(dtt[0:L, sub, gq * 8:(gq + 1) * 8]), ALU.mult)
                S.tt('pool', x3(xdt2), x3(xdt), b8(ee[0:L, 32 + gq * 8:32 + (gq + 1) * 8]), ALU.mult)
                S.tt('pool', x3(xsd), xsg, b8(dsk_bc[0:L, gq * 8:(gq + 1) * 8]), ALU.mult)

            def s1a(gq):
                for (src, dst) in ((adt_hi, Bmh_v), (adt_lo, Bml_v)):
                    a8 = src[0:L, sub, gq * 8:(gq + 1) * 8]
                    S.tt('dve', dst, bc(a8, [[a8.ap[0][0], L], [1, 8], [0, L]]),
                         bc(tri, [[pstr(tri), L], [0, 8], [1, L]]), ALU.mult)

            def s1b(gq):
                items = []
                for hf in range(nhalf):
                    out = pf(SEGB[hf])[0:L, 0:hh * L]
                    items.append(dict(out=out, lhsT=SU[0:L, 0:L], rhs=Bmh_v[:, hf * hh:(hf + 1) * hh, :],
                                      start=True, stop=False))
                    items.append(dict(out=out, lhsT=SU[0:L, 0:L], rhs=Bml_v[:, hf * hh:(hf + 1) * hh, :],
                                      start=False, stop=True))
                S.mms(items)
                for hf in range(nhalf):
                    S.act(Eb_v[:, hf * hh:(hf + 1) * hh, :],
                          pf(SEGB[hf])[0:L, 0:hh * L].rearrange("p (r l) -> p r l", r=hh), AF.Exp)

            def s2(gq):
                xdt, xdt2, xsd = xdtb[gq % 2], xdt2b[gq % 2], xsdb[gq % 2]
                cg = cb_v[:, gq, :]
                S.tt('dve', MT_v, Eb_v, bc(cg, [[cg.ap[0][0], L], [0, 8], [1, L]]), ALU.mult)
                yb = pf(YB)
                S.mms([dict(out=yb[0:L, 0:512], lhsT=xraw[:, 20 + gq, cs:cs + L], rhs=hTb[:, gq * 512:(gq + 1) * 512],
                            start=True, stop=False)])
                S.tt('dve', yb[0:L, 0:512].rearrange("p (r d) -> p r d", r=8),
                     yb[0:L, 0:512].rearrange("p (r d) -> p r d", r=8), b8(ee[0:L, gq * 8:(gq + 1) * 8]), ALU.mult)
                items = [dict(out=yb[0:L, 0:512], lhsT=identb[0:L, 0:L], rhs=xsd[0:L, :], start=False, stop=False)]
                for r in range(8):
                    items.append(dict(out=yb[0:L, r * 64:(r + 1) * 64], lhsT=MT_v[:, r, :],
                                      rhs=xdt[0:L, r * 64:(r + 1) * 64], start=False, stop=(r == 7)))
                S.mms(items)
                ub = pf(UB)
                S.mms([dict(out=ub[:, 0:512], lhsT=B_tok[0:L, gq * 128:(gq + 1) * 128], rhs=xdt2[0:L, :],
                            start=True, stop=True)])

            def s3(gq):
                yb = pf(YB)
                ub = pf(UB)
                S.tt('dve', yz[0:L, gq, :], yb[0:L, 0:512], zs2[0:L, sub, gq * 512:(gq + 1) * 512], ALU.mult)
                S.act(junk[0:L, 0:512], yz[0:L, gq, :], AF.Square, accum_out=ssq[0:L, gq:gq + 1])
                hg = hT[:, gq * 512:(gq + 1) * 512]
                e8 = ee2[:, gq * 8:(gq + 1) * 8]
                S.tt('pool', hg.rearrange("p (r d) -> p r d", r=8), hg.rearrange("p (r d) -> p r d", r=8),
                     bc(e8, [[e8.ap[0][0], 128], [1, 8], [0, 64]]), ALU.mult)
                S.tt('dve', hg, hg, ub[:, 0:512], ALU.add)
                S.copy('act', hTb[:, gq * 512:(gq + 1) * 512], hg)

            s0_pool(0)
            s1a(0)
            s1b(0)
            for gq in range(4):
                if gq < 3:
                    s0_pool(gq + 1)
                s2(gq)
                if gq < 3:
                    s1a(gq + 1)
                    s1b(gq + 1)
                s3(gq)
            stop(6.8, ti)
            S.ts('dve', rstd4[0:L, :], ssq[0:L, :], 1.0 / 512.0, 4.0 * EPS, ALU.mult, ALU.add)
            S.tt('pool', rstd4[0:L, :], rstd4[0:L, :], neghalf[0:L, :], ALU.pow)
            S.tt('pool', yz[0:L, :, :], yz[0:L, :, :], bc(rstd4, [[pstr(rstd4), L], [1, 4], [0, 512]]), ALU.mult)
            yzf = yz[0:L, :, :].rearrange("p a b -> p (a b)")
            for hb in range(2):
                bk = nb('tp')
                S.trs([(pbf(bk)[:, j * L:(j + 1) * L], yzf[:, (hb * 8 + j) * 128:(hb * 8 + j + 1) * 128])
                       for j in range(8)], identb[0:L, 0:L])
                nwv = nwssm[:, hb * 8:(hb + 1) * 8]
                S.tt('dve', yzT[:, hb * 8:(hb + 1) * 8, sub * L:(sub + 1) * L],
                     pbf(bk)[:, 0:8 * L].rearrange("p (a b) -> p a b", a=8),
                     bc(nwv, [[nwv.ap[0][0], 128], [1, 8], [0, L]]), ALU.mult)
            stop(6.9, ti)
            if kind == 's':
                store_state(ssm_s[idx * 2 + sub])
            elif last_p and sub == 1:
                store_state(ssm_p)

        stop(7, ti)
        for wi in range(4):
            wv = w_use('ws%d' % wi)
            for et in range(2):
                e8 = wi * 2 + et
                bk = nb('mm')
                out = pf(bk)[:, 0:T]
                S.mms([dict(out=out, lhsT=wv[:, kc, et * 128:(et + 1) * 128], rhs=yzT[:, kc, 0:T],
                            start=(kc == 0), stop=(kc == 15)) for kc in range(16)])
                th = nth()
                S.tt('dve', th[:, 0:T], out, gs_sig[:, e8, 0:T], ALU.mult)
                S.tt('pool', mT[:, e8, 0:T], th[:, 0:T], accT[:, e8, 0:T], ALU.add)

        stop(8, ti)
        wvs = [w_use('wo0'), w_use('wo1', issue=False)]
        for sub in range(2):
            for hf in range(2):
                bk = nb('mm')
                out = pf(bk)[0:L, 0:512]
                S.mms([dict(out=out, lhsT=mT[:, kc, sub * L:(sub + 1) * L], rhs=wvs[hf][:, kc, :],
                            start=(kc == 0), stop=(kc == 7)) for kc in range(8)])
                S.tt('dve', ost[0:L, hf * 512:(hf + 1) * 512], out, xb[sub][0:L, hf * 512:(hf + 1) * 512], ALU.add)
            ssx = smallf[0:L, 2:3]
            S.act(junk[0:L, :], ost[0:L, :], AF.Square, accum_out=ssx)
            rms_rstd(ssx, 1024.0, EPS, smallf[0:L, 3:4], L)
            S.stt('dve', ost[0:L, :], ost[0:L, :], smallf[0:L, 3:4], fnw_bc[0:L, :], ALU.mult, ALU.mult)
            _, dsty = xrows(kind, idx, sub, L)
            S.dma('pool', dsty, ost[0:L, :], 'ost')

    if dbg is not None:
        dbg(S, locals())


_CACHE = {}


def _get_nc(SEQ):
    if SEQ not in _CACHE:
        nc = bass.Bass("TRN2", target_bir_lowering=False)
        build(nc, SEQ)
        _CACHE[SEQ] = nc
    return _CACHE[SEQ]


def kernel(x_prompt, x_sample, state_ssm, state_conv, cache_k, cache_v, norm_w, w_in, conv_w, conv_b, dt_bias,
           a_log, d_skip, ssm_norm_w, w_out_ssm, rel_bias, w_out_att, w_o, final_norm_w):
    f = lambda a: np.ascontiguousarray(np.asarray(a, dtype=np.float32))
    B, SEQ, _ = x_prompt.shape
    assert B == 8 and SEQ % 256 == 0
    KEEP = min(512, SEQ)
    nc = _get_nc(SEQ)
    shared = dict(norm_w=f(norm_w[0]), w_in=f(w_in[0]), conv_w=f(conv_w[0]), conv_b=f(conv_b[0]),
                  dt_bias=f(dt_bias[0]), a_log=f(a_log[0]), d_skip=f(d_skip[0]), ssm_norm_w=f(ssm_norm_w[0]),
                  w_out_ssm=f(w_out_ssm[0]), rel_bias=f(rel_bias[0]), w_out_att=f(w_out_att[0]), w_o=f(w_o[0]),
                  final_norm_w=f(final_norm_w))
    in_maps = []
    for c in range(_NCORES):
        m = dict(shared)
        m['x_p'] = f(x_prompt[c])
        m['x_s'] = f(x_sample[4 * c:4 * c + 4]).reshape(256, D)
        m['st_ssm'] = f(state_ssm[0, 4 * c:4 * c + 4]).reshape(4, 2048, 128)
        m['st_conv'] = f(state_conv[0, 4 * c:4 * c + 4])
        m['ck'] = f(cache_k[0, 4 * c:4 * c + 4]).reshape(4, 512, D)
        m['cv'] = f(cache_v[0, 4 * c:4 * c + 4]).reshape(4, 512, D)
        in_maps.append(m)
    res = run_bass_kernel_spmd(nc, in_maps, core_ids=list(range(_NCORES)))
    R = list(res.results) + [res.results[0]] * (8 - _NCORES)
    y_prompt = np.stack([R[c]['y_p'] for c in range(8)]).astype(np.float32)
    y_sample = np.concatenate([R[c]['y_s'].reshape(4, 64, D) for c in range(8)]).astype(np.float32)
    ssm_p = np.stack([R[c]['ssm_p'].reshape(NH, 64, 128) for c in range(8)])[None].astype(np.float32)
    conv_p = np.stack([R[c]['conv_p'] for c in range(8)])[None].astype(np.float32)
    k_p = np.stack([R[c]['k_p'].reshape(KEEP, AH, 64) for c in range(8)])[None].astype(np.float32)
    v_p = np.stack([R[c]['v_p'].reshape(KEEP, AH, 64) for c in range(8)])[None].astype(np.float32)
    ssm_s = np.concatenate([R[c]['ssm_s'].reshape(4, NH, 64, 128) for c in range(8)])[None].astype(np.float32)
    conv_s = np.concatenate([R[c]['conv_s'] for c in range(8)])[None].astype(np.float32)
    k_s = np.concatenate([R[c]['k_s'].reshape(4, 512, AH, 64) for c in range(8)])[None].astype(np.float32)
    v_s = np.concatenate([R[c]['v_s'].reshape(4, 512, AH, 64) for c in range(8)])[None].astype(np.float32)
    return (y_prompt, y_sample, ssm_p, conv_p, k_p, v_p, ssm_s, conv_s, k_s, v_s)
```

```python
import numpy as np
from contextlib import ExitStack
import concourse.bass as bass
import concourse.mybir as mybir
from concourse.bass_utils import run_bass_kernel_spmd

F32 = mybir.dt.float32
BF16 = mybir.dt.bfloat16
U8 = mybir.dt.uint8
AF = mybir.ActivationFunctionType
ALU = mybir.AluOpType

D = 1024
KC = 8
DI = 2048
CD = 3072
NH = 32
AH = 16
DIN = 11296
C_Z, C_X, C_DT, C_Q, C_K, C_V, C_G, C_GS, C_GA = 0, 2048, 5120, 5152, 6176, 7200, 8224, 9248, 10272
EPS = 1e-5
NEG = -30000.0


def _dsz(dt):
    if dt == F32:
        return 4
    if dt == BF16:
        return 2
    if dt == U8:
        return 1
    raise ValueError(dt)


class _Op:
    __slots__ = ('eng', 'fn', 'deps', 'dma', 'key', 'val', 'seq', 'hasdep')


class Sched:
    def __init__(self, nc, es):
        self.nc = nc
        self.es = es
        self.ops = []
        self.engobj = {'pe': nc.tensor, 'act': nc.scalar, 'dve': nc.vector, 'pool': nc.gpsimd, 'sp': nc.sync}
        self.esem = {k: es.enter_context(nc.semaphore('es_' + k)) for k in self.engobj}
        self.dkeys = {}
        self.tracked = {}
        self.acc = {}

    def track(self, name, const=False):
        self.tracked[name] = 'const' if const else True

    def reg(self, ap):
        name = ap.tensor.name
        tr = self.tracked.get(name)
        if tr is None:
            return None
        dims = ap.ap
        ps = dims[0][0]
        npart = dims[0][1]
        off = ap.offset
        if ps > 0:
            p0 = off // ps
            f0 = off % ps
        else:
            p0 = 0
            f0 = off
        lo = f0
        hi = f0
        for (s, c) in dims[1:]:
            if s >= 0:
                hi += s * (c - 1)
            else:
                lo += s * (c - 1)
        sz = _dsz(ap.dtype)
        return (name, p0, p0 + npart, lo * sz, (hi + 1) * sz)

    def add(self, eng, fn, reads=(), writes=(), dma_key=None):
        i = len(self.ops)
        deps = set()
        rr = []
        ww = []
        for a in reads:
            r = a if isinstance(a, tuple) else self.reg(a)
            if r is not None:
                if r[0].startswith('pb'):
                    ww.append((r[0], 0, 128, 0, 2048))
                else:
                    rr.append(r)
        for a in writes:
            r = a if isinstance(a, tuple) else self.reg(a)
            if r is not None:
                if r[0].startswith('pb'):
                    r = (r[0], 0, 128, 0, 2048)
                ww.append(r)
        for (k, p0, p1, lo, hi) in rr:
            for e in self.acc.get(k, ()):
                if e[5] and e[0] < p1 and p0 < e[1] and e[2] < hi and lo < e[3]:
                    deps.add(e[4])
        for (k, p0, p1, lo, hi) in ww:
            lst = self.acc.get(k)
            if lst is None:
                continue
            keep = []
            for e in lst:
                if e[0] < p1 and p0 < e[1] and e[2] < hi and lo < e[3]:
                    deps.add(e[4])
                    if p0 <= e[0] and e[1] <= p1 and lo <= e[2] and e[3] <= hi:
                        continue
                keep.append(e)
            self.acc[k] = keep
        deps.discard(i)
        for (k, p0, p1, lo, hi) in rr:
            if self.tracked.get(k) == 'const':
                continue
            self.acc.setdefault(k, []).append([p0, p1, lo, hi, i, False])
        for (k, p0, p1, lo, hi) in ww:
            self.acc.setdefault(k, []).append([p0, p1, lo, hi, i, True])
        op = _Op()
        op.eng = eng
        op.fn = fn
        op.deps = deps
        op.dma = dma_key is not None
        op.key = dma_key
        op.val = 0
        op.seq = 0
        op.hasdep = False
        if op.dma:
            ent = self.dkeys.get(dma_key)
            if ent is None:
                ent = [self.es.enter_context(self.nc.semaphore('d_%d' % len(self.dkeys))), 0, []]
                self.dkeys[dma_key] = ent
            ent[1] += 16
            op.val = ent[1]
            ent[2].append(i)
        self.ops.append(op)
        return i

    def group_barrier(self, key):
        ent = self.dkeys[key]
        for i in ent[2]:
            self.ops[i].val = ent[1]

    def group_last(self, key, n):
        ent = self.dkeys[key]
        for i in ent[2][-n:]:
            self.ops[i].val = ent[1]

    def emit(self):
        ops = self.ops
        for op in ops:
            for d in op.deps:
                ops[d].hasdep = True
        cnt = {k: 0 for k in self.engobj}
        for op in ops:
            if not op.dma and op.hasdep:
                cnt[op.eng] += 1
                op.seq = cnt[op.eng]
        waited = {k: {} for k in self.engobj}
        nw = 0
        for op in ops:
            eng = self.engobj[op.eng]
            wd = waited[op.eng]
            need = {}
            for d in op.deps:
                dop = ops[d]
                if dop.dma:
                    sem, val = self.dkeys[dop.key][0], dop.val
                else:
                    if dop.eng == 'pe' and op.eng == 'pe' and not op.dma:
                        continue
                    sem, val = self.esem[dop.eng], dop.seq
                k = id(sem)
                if need.get(k, (None, 0))[1] < val:
                    need[k] = (sem, val)
            for k, (sem, val) in need.items():
                if wd.get(k, 0) >= val:
                    continue
                eng.wait_ge(sem, val)
                wd[k] = val
                nw += 1
            inst = op.fn(eng)
            if op.dma:
                inst.then_inc(self.dkeys[op.key][0], 16)
            elif op.hasdep:
                inst.then_inc(self.esem[op.eng], 1)
        sp = self.engobj['sp']
        for key, ent in self.dkeys.items():
            sp.wait_ge(ent[0], ent[1])
        return len(ops), nw

    def mms(self, items):
        reads = []
        writes = []
        for it in items:
            reads += [it['lhsT'], it['rhs']]
            writes.append(it['out'])
            if not it['start']:
                reads.append(it['out'])

        def fn(e, items=items):
            inst = None
            for it in items:
                inst = e.matmul(it['out'], lhsT=it['lhsT'], rhs=it['rhs'], start=it['start'], stop=it['stop'],
                                skip_group_check=True)
            return inst
        return self.add('pe', fn, reads, writes)

    def trs(self, items, ident):
        reads = [ident] + [x[1] for x in items]
        writes = [x[0] for x in items]

        def fn(e, items=items, ident=ident):
            inst = None
            for (o, i) in items:
                inst = e.transpose(o, i, ident)
            return inst
        return self.add('pe', fn, reads, writes)

    def act(self, out, in_, func, bias=None, scale=None, accum_out=None, extra_w=()):
        reads = [in_]
        kw = {}
        if bias is not None:
            kw['bias'] = bias
            if not isinstance(bias, (int, float)):
                reads.append(bias)
        if scale is not None:
            kw['scale'] = scale
            if not isinstance(scale, (int, float)):
                reads.append(scale)
        writes = [out] + list(extra_w)
        if accum_out is not None:
            kw['accum_out'] = accum_out
            writes.append(accum_out)
        return self.add('act', lambda e: e.activation(out=out, in_=in_, func=func, **kw), reads, writes)

    def tt(self, eng, out, in0, in1, op):
        return self.add(eng, lambda e: e.tensor_tensor(out=out, in0=in0, in1=in1, op=op), [in0, in1], [out])

    def ts(self, eng, out, in0, s1, s2, op0, op1=None):
        reads = [in0]
        if not isinstance(s1, (int, float)):
            reads.append(s1)
        if s2 is not None and not isinstance(s2, (int, float)):
            reads.append(s2)
        if op1 is None:
            return self.add(eng, lambda e: e.tensor_scalar(out=out, in0=in0, scalar1=s1, scalar2=None, op0=op0),
                            reads, [out])
        return self.add(eng, lambda e: e.tensor_scalar(out=out, in0=in0, scalar1=s1, scalar2=s2, op0=op0, op1=op1),
                        reads, [out])

    def stt(self, eng, out, in0, scalar, in1, op0, op1):
        reads = [in0, in1]
        if not isinstance(scalar, (int, float)):
            reads.append(scalar)
        return self.add(eng, lambda e: e.scalar_tensor_tensor(out=out, in0=in0, scalar=scalar, in1=in1, op0=op0,
                                                              op1=op1), reads, [out])

    def copy(self, eng, out, in_):
        if eng == 'act':
            return self.add('act', lambda e: e.activation(out=out, in_=in_, func=AF.Identity), [in_], [out])
        return self.add(eng, lambda e: e.tensor_copy(out=out, in_=in_), [in_], [out])

    def memset(self, eng, ap, val):
        return self.add(eng, lambda e: e.memset(ap, val), [], [ap])

    def recip(self, out, in_):
        return self.add('dve', lambda e: e.reciprocal(out=out, in_=in_), [in_], [out])

    def dma(self, q, out, in_, key, extra_r=(), extra_w=(), slow=False):
        if slow:
            fn = lambda e: e.dma_start(out=out, in_=in_, allow_slow_non_contiguous=True)
        else:
            fn = lambda e: e.dma_start(out=out, in_=in_)
        return self.add(q, fn, [in_] + list(extra_r), [out] + list(extra_w), dma_key=key)


def bc(ap, dims, off=0):
    return bass.AP(ap.tensor, ap.offset + off, dims)


def pstr(ap):
    return ap.ap[0][0]


class Arena:
    def __init__(self, nc, es, S, name, nbytes):
        self.t = es.enter_context(nc.sbuf_tensor(name, [128, nbytes], U8))
        self.off = 0
        self.cap = nbytes
        self.name = name
        S.track(name)

    def alloc(self, free_shape, dt):
        n = int(np.prod(free_shape)) * _dsz(dt)
        off = (self.off + 31) // 32 * 32
        assert off + n <= self.cap, (self.name, off, n, self.cap)
        self.off = off + n
        v = self.t[:, off:off + n].bitcast(dt)
        if len(free_shape) == 2:
            v = v.rearrange("p (a b) -> p a b", a=free_shape[0])
        elif len(free_shape) == 3:
            v = v.rearrange("p (a b c) -> p a b c", a=free_shape[0], b=free_shape[1])
        return v


_STOP = (99, 0)
_NCORES = 8
_DBG = ''
_ENV = {}
_DUMP = None


class _StopBuild(Exception):
    pass


def build(nc, SEQ, dbg=None):
    es = ExitStack()
    S = Sched(nc, es)
    try:
        _body(nc, SEQ, es, S, dbg)
    except _StopBuild:
        pass
    if _DUMP is not None:
        _DUMP(S, _ENV)
    nops, nw = S.emit()
    es.close()
    return nops, nw


def _body(nc, SEQ, es, S, dbg=None):
    NTP = SEQ // 256
    KEEP = min(512, SEQ)

    def din(name, shape):
        return nc.dram_tensor(name, shape, F32, kind="ExternalInput").ap()

    def dout(name, shape):
        return nc.dram_tensor(name, shape, F32, kind="ExternalOutput").ap()

    x_p = din("x_p", [SEQ, D])
    x_s = din("x_s", [256, D])
    st_ssm = din("st_ssm", [4, 2048, 128])
    st_conv = din("st_conv", [4, 3, CD])
    ck = din("ck", [4, 512, D])
    cv = din("cv", [4, 512, D])
    norm_w = din("norm_w", [D])
    w_in = din("w_in", [D, DIN])
    conv_w = din("conv_w", [4, CD])
    conv_b = din("conv_b", [CD])
    dt_bias = din("dt_bias", [NH])
    a_log = din("a_log", [NH])
    d_skip = din("d_skip", [NH])
    ssm_norm_w = din("ssm_norm_w", [DI])
    w_out_ssm = din("w_out_ssm", [DI, D])
    rel_bias = din("rel_bias", [AH, 513])
    w_out_att = din("w_out_att", [D, D])
    w_o = din("w_o", [D, D])
    final_norm_w = din("final_norm_w", [D])

    y_p = dout("y_p", [SEQ, D])
    y_s = dout("y_s", [256, D])
    ssm_p = dout("ssm_p", [2048, 128])
    conv_p = dout("conv_p", [3, CD])
    k_p = dout("k_p", [KEEP, D])
    v_p = dout("v_p", [KEEP, D])
    ssm_s = dout("ssm_s", [4, 2048, 128])
    conv_s = dout("conv_s", [4, 3, CD])
    k_s = dout("k_s", [4, 512, D])
    v_s = dout("v_s", [4, 512, D])

    WB = {}
    blist = []

    def addblk(name, src, nrow_chunks, c0, ncols, N):
        WB[name] = dict(idx=len(blist), src=src, kc=nrow_chunks, c0=c0, ncols=ncols, N=N)
        blist.append(name)

    for i in range(2):
        addblk('q%d' % i, w_in, 8, C_Q + 512 * i, 512, DIN)
    for i in range(2):
        addblk('k%d' % i, w_in, 8, C_K + 512 * i, 512, DIN)
    for i in range(2):
        addblk('v%d' % i, w_in, 8, C_V + 512 * i, 512, DIN)
    for i in range(2):
        addblk('g%d' % i, w_in, 8, C_G + 512 * i, 512, DIN)
    for i in range(2):
        addblk('ga%d' % i, w_in, 8, C_GA + 512 * i, 512, DIN)
    for i in range(2):
        addblk('wa%d' % i, w_out_att, 8, 512 * i, 512, D)
    addblk('dt', w_in, 8, C_DT, 32, DIN)
    for i in range(6):
        addblk('x%d' % i, w_in, 8, C_X + 512 * i, 512, DIN)
    for i in range(4):
        addblk('z%d' % i, w_in, 8, C_Z + 512 * i, 512, DIN)
    for i in range(2):
        addblk('gs%d' % i, w_in, 8, C_GS + 512 * i, 512, DIN)
    for i in range(4):
        addblk('ws%d' % i, w_out_ssm, 16, 256 * i, 256, D)
    for i in range(2):
        addblk('wo%d' % i, w_o, 8, 512 * i, 512, D)
    NB = len(blist)
    wscr = nc.dram_tensor("wscr", [NB, 128, 4096], BF16, kind="Internal").ap()
    ext = nc.dram_tensor("ext_bias", [AH, 1024], F32, kind="Internal").ap()

    def wscr_view(name):
        b = WB[name]
        i = b['idx']
        kc, nc_ = b['kc'], b['ncols']
        base = wscr[i]
        return bc(base, [[4096, 128], [nc_, kc], [1, nc_]])

    NGRP = 4
    for n, name in enumerate(blist):
        b = WB[name]
        src = bass.AP(b['src'].tensor, b['c0'], [[b['N'], 128], [128 * b['N'], b['kc']], [1, b['ncols']]])
        grp = min(NGRP - 1, n * NGRP // NB)
        S.dma('pool', wscr_view(name), src, 'cv%d' % grp, extra_w=[('wscr', b['idx'], b['idx'] + 1, 0, 1)])
        b['grp'] = grp
    for g in range(NGRP):
        S.group_barrier('cv%d' % g)

    def cstop(n):
        if _STOP == (n, 0):
            raise _StopBuild()
    cstop(-1)

    CA = Arena(nc, es, S, "cst", 29 * 1024)
    S.tracked["cst"] = True
    A = Arena(nc, es, S, "main", 183040)

    identb = CA.alloc([128], BF16)
    identf = CA.alloc([128], F32)
    Jx = CA.alloc([128], BF16)
    tri = CA.alloc([128], BF16)
    SU = CA.alloc([128], BF16)
    onesb = CA.alloc([128], BF16)
    scr32 = CA.alloc([128], F32)
    cstg = CA.alloc([128], F32)
    cpar = CA.alloc([144], F32)
    normw = cpar[:, 0:8]
    nwssm = cpar[:, 8:24]
    convb = cpar[:, 24:48]
    convw = cpar[:, 48:144].rearrange("p (j c) -> p j c", j=4)
    dtb_bc = CA.alloc([32], F32)
    a_bc = CA.alloc([32], F32)
    dsk_bc = CA.alloc([32], F32)
    fnw_bc = CA.alloc([1024], F32)
    neghalf = CA.alloc([4], F32)
    biasF = CA.alloc([AH, 5, 128], BF16)
    convst = CA.alloc([4, 3, 24], BF16)

    xt = [A.alloc([1024], F32) for _ in range(4)]
    xsn = A.alloc([1024], BF16)
    TH = [A.alloc([512], BF16) for _ in range(3)]
    junk = bc(TH[0], [[pstr(TH[0]), 128], [1, 1024]])
    xnT = A.alloc([8, 256], BF16)
    NSLOT = 3
    Wr = [A.alloc([4096], BF16) for _ in range(NSLOT)]
    KT = A.alloc([8, 6, 128], BF16)
    Vr = A.alloc([6, AH, 66], BF16)
    hT = A.alloc([2048], F32)
    hTb = A.alloc([2048], BF16)
    ost = A.alloc([1024], F32)
    ogT = A.alloc([8, 256], BF16)
    ga_sig = A.alloc([8, 256], BF16)
    gs_sig = A.alloc([8, 256], BF16)
    accT = A.alloc([8, 256], BF16)
    yzT = A.alloc([16, 256], BF16)
    mT = A.alloc([8, 256], BF16)
    smallf = A.alloc([64], F32)
    carry = A.alloc([24, 3], BF16)
    base = A.off
    QT = A.alloc([8, 256], BF16)
    PT = [A.alloc([512], BF16) for _ in range(3)]
    rs = A.alloc([16], F32)
    o_n = A.alloc([1024], BF16)
    og = A.alloc([1024], BF16)
    gs2 = A.alloc([2, 1024], BF16)
    kvst = [A.alloc([512], F32) for _ in range(2)]
    ckst = A.alloc([4, 1024], BF16)
    KTs = A.alloc([8, 2, 64], BF16)
    Vs = A.alloc([2, AH, 66], BF16)
    att_end = A.off
    A.off = base
    zs2 = A.alloc([2, 2048], BF16)
    xs_tok = A.alloc([2048], BF16)
    B_tok = A.alloc([512], BF16)
    xdtb = [A.alloc([512], BF16) for _ in range(2)]
    xdt2b = [A.alloc([512], BF16) for _ in range(2)]
    xsdb = [A.alloc([512], BF16) for _ in range(2)]
    cbTm = A.alloc([4, 128], BF16)
    Bmh = A.alloc([8, 128], BF16)
    Bml = A.alloc([8, 128], BF16)
    Eb = A.alloc([8, 128], BF16)
    MTb = A.alloc([8, 128], BF16)
    yz = A.alloc([4, 512], BF16)
    ssq = A.alloc([4], F32)
    rstd4 = A.alloc([4], F32)
    A.off = max(A.off, att_end)
    XW = 3 + 256
    xraw = A.alloc([24, XW], BF16)
    cacc = [A.alloc([256], F32) for _ in range(2)]
    dtt = A.alloc([2, 32], F32)
    adt = A.alloc([2, 32], F32)
    adt_hi = A.alloc([2, 32], BF16)
    adt_lo = A.alloc([2, 32], BF16)
    dtmp = A.alloc([32], F32)
    ee = A.alloc([64], F32)
    ee2 = A.alloc([32], F32)
    xbcst = [A.alloc([512], F32) for _ in range(2)]
    stT = A.alloc([16, 128], F32)
    ssd_end = A.off
    A.off = max(att_end, ssd_end)
    print('SBUF use: cst', CA.off, 'main', A.off, 'att', att_end - base, 'ssd', ssd_end - base)

    _ENV.update(dict(accT=accT, mT=mT, yzT=yzT, ogT=ogT, gs_sig=gs_sig, ga_sig=ga_sig, xnT=xnT, QT=QT, zs2=zs2,
                     yz=yz, ost=ost, gs2=gs2, o_n=o_n, og=og, hT=hT, xraw=xraw, nc=nc, A=A, xt=xt))
    PB = []
    for i in range(8):
        t = es.enter_context(nc.psum_tensor("pb%d" % i, [128, 2048], U8))
        S.track("pb%d" % i)
        PB.append(t)

    def pf(i):
        return PB[i][:, :].bitcast(F32)

    def pbf(i):
        return PB[i][:, :].bitcast(BF16)

    rot = {'mm': [0, 1, 3, 4], 'mm2': [0, 1], 'tp': [2, 0, 1], 'S': [3, 4]}
    rotp = {k: 0 for k in rot}

    def nb(role):
        i = rot[role][rotp[role] % len(rot[role])]
        rotp[role] += 1
        return i
    OB = [5, 6, 7]
    SEGB = [3, 4]
    YB, UB, CBB, SMB = 5, 7, 6, 2

    P = 'pool'
    S.memset(P, scr32, 0.0)
    S.add(P, lambda e: e.affine_select(out=scr32, in_=scr32, pattern=[[-1, 128]], compare_op=ALU.not_equal, fill=1.0,
                                       base=0, channel_multiplier=1), [scr32], [scr32])
    S.copy('dve', identb, scr32)
    S.copy('dve', identf, scr32)
    S.memset(P, scr32, 0.0)
    S.add(P, lambda e: e.affine_select(out=scr32, in_=scr32, pattern=[[1, 128]], compare_op=ALU.not_equal, fill=1.0,
                                       base=-127, channel_multiplier=1), [scr32], [scr32])
    S.copy('dve', Jx, scr32)
    S.memset(P, scr32, 1.0)
    S.copy('dve', onesb, scr32)
    S.add(P, lambda e: e.affine_select(out=scr32, in_=scr32, pattern=[[1, 128]], compare_op=ALU.is_ge, fill=0.0,
                                       base=0, channel_multiplier=-1), [scr32], [scr32])
    S.copy('dve', tri, scr32)
    S.memset(P, scr32, 1.0)
    S.add(P, lambda e: e.affine_select(out=scr32, in_=scr32, pattern=[[-1, 128]], compare_op=ALU.is_ge, fill=0.0,
                                       base=-1, channel_multiplier=1), [scr32], [scr32])
    S.copy('dve', SU, scr32)
    S.memset(P, neghalf, -0.5)
    S.memset(P, hT, 0.0)
    S.memset(P, hTb, 0.0)
    S.memset(P, carry, 0.0)
    S.memset(P, Vr[:, :, :, 64:65], 1.0)

    cstop(-2)
    CK = 'const'
    S.dma('sp', dtb_bc, bass.AP(dt_bias.tensor, 0, [[0, 128], [1, 32]]), CK)
    S.dma('sp', a_bc, bass.AP(a_log.tensor, 0, [[0, 128], [1, 32]]), CK)
    S.dma('sp', dsk_bc, bass.AP(d_skip.tensor, 0, [[0, 128], [1, 32]]), CK)
    S.dma('sp', fnw_bc, bass.AP(final_norm_w.tensor, 0, [[0, 128], [1, 1024]]), CK)
    S.group_barrier(CK)
    cstop(-3)
    S.dma('sp', cstg[0:8, :], bass.AP(norm_w.tensor, 0, [[128, 8], [1, 128]]), 'cstg')
    S.dma('sp', cstg[8:24, :], bass.AP(ssm_norm_w.tensor, 0, [[128, 16], [1, 128]]), 'cstg')
    S.dma('sp', cstg[24:48, :], bass.AP(conv_b.tensor, 0, [[128, 24], [1, 128]]), 'cstg')
    S.group_last('cstg', 3)
    S.trs([(pf(0)[:, 0:48], cstg[0:48, :])], identf[0:48, 0:48])
    S.copy('dve', cpar[:, 0:48], pf(0)[:, 0:48])
    S.dma('sp', cstg[0:96, :], bass.AP(conv_w.tensor, 0, [[128, 96], [1, 128]]), 'cstg')
    S.trs([(pf(0)[:, 0:96], cstg[0:96, :])], identf[0:96, 0:96])
    S.copy('dve', cpar[:, 48:144], pf(0)[:, 0:96])
    for c3 in range(3):
        S.dma('sp', cstg[0:96, :], bass.AP(st_conv.tensor, c3 * 96 * 128, [[128, 96], [1, 128]]), 'cstg')
        S.trs([(pf(0)[:, 0:96], cstg[0:96, :])], identf[0:96, 0:96])
        cf = convst.rearrange("p s j c -> p (s j c)")
        S.copy('dve', cf[:, c3 * 96:(c3 + 1) * 96], pf(0)[:, 0:96])
    cstop(-4)
    EXT = ('ext', 0, 1, 0, 1)
    S.dma('sp', xt[2][0:16, 0:513], rel_bias, 'xt2')
    S.copy('dve', xt[1][0:16, 0:513], xt[2][0:16, 0:513])
    rb512 = xt[2][0:16, 512:513]
    S.copy('dve', xt[1][0:16, 513:1024], bc(rb512, [[rb512.ap[0][0], 16], [0, 511]]))
    S.dma('sp', ext, xt[1][0:16, 0:1024], 'xt1', extra_w=[EXT])
    stgv = bc(xt[2], [[pstr(xt[2]), 128], [128, AH], [1, 128]])
    for t in range(5):
        src = bass.AP(ext.tensor, 641 - 128 * t, [[1, 128], [1024, AH], [1, 128]])
        S.dma('sp', stgv, src, 'xt2', extra_r=[EXT])
        S.copy('dve', biasF[:, :, t, :], stgv)
    S.memset(P, biasF[64:128, :, 0, 64:128], NEG)
    S.memset(P, biasF[0:64, :, 4, 0:64], NEG)
    cstop(-5)
    S.act(a_bc, a_bc, AF.Exp)
    S.ts('dve', a_bc, a_bc, -1.0, None, ALU.mult)
    S.ts('dve', convw, convw, 0.5, None, ALU.mult)
    S.ts('dve', convb, convb, 0.5, None, ALU.mult)

    S.tracked['cst'] = 'const'

    uses = []
    tiles = [('p', i) for i in range(NTP)] + [('s', 0), ('s', 1)]
    def can_prefetch0(tj):
        if tj >= len(tiles):
            return False
        kind_, idx_ = tiles[tj]
        return kind_ == 'p' and idx_ >= 1 and (idx_ * 256 + 128) < SEQ - KEEP
    for tj in range(len(tiles)):
        p_now = can_prefetch0(tj) and tj >= 1
        p_next = can_prefetch0(tj + 1)
        uses += ['q0', 'q1'] + ([] if p_now else ['k0', 'k1', 'v0', 'v1']) + ['g0', 'g1']
        uses += ([] if p_now else ['ga0', 'ga1'])
        uses += ['dt', 'x0', 'x1', 'x2', 'x3', 'x4', 'x5', 'wa0', 'wa1', 'z0', 'z1', 'z2', 'z3', 'gs0', 'gs1']
        uses += (['k0', 'k1', 'v0', 'v1', 'ga0', 'ga1'] if p_next else [])
        uses += ['ws0', 'ws1', 'ws2', 'ws3', 'wo0', 'wo1']
    wstate = {'issued': 0, 'cur': 0}

    def w_issue(upto):
        while wstate['issued'] < min(upto, len(uses)):
            u = wstate['issued']
            name = uses[u]
            b = WB[name]
            slot = u % NSLOT
            n = b['kc'] * b['ncols']
            dst = bc(Wr[slot], [[pstr(Wr[slot]), 128], [1, n]])
            srcv = bc(wscr[b['idx']], [[4096, 128], [1, n]])
            S.dma('sp', dst, srcv, 'W%d' % slot, extra_r=[('wscr', b['idx'], b['idx'] + 1, 0, 1)])
            wstate['issued'] += 1

    def w_use(name, issue=True):
        u = wstate['cur']
        assert uses[u] == name, (uses[u], name)
        if issue:
            w_issue(u + NSLOT)
        wstate['cur'] += 1
        b = WB[name]
        slot = u % NSLOT
        return bc(Wr[slot], [[pstr(Wr[slot]), 128], [b['ncols'], b['kc']], [1, b['ncols']]])

    def xrows(kind, idx, sub, L):
        if kind == 'p':
            r0 = idx * 256 + sub * 128
            return x_p[r0:r0 + L, :], y_p[r0:r0 + L, :]
        seq = idx * 2 + sub
        return x_s[seq * 64:seq * 64 + 64, :], y_s[seq * 64:seq * 64 + 64, :]

    def load_x(ti):
        kind, idx = tiles[ti]
        L = 128 if kind == 'p' else 64
        for sub in range(2):
            b = (ti % 2) * 2 + sub
            src, _ = xrows(kind, idx, sub, L)
            S.dma('sp', xt[b][0:L, :], src, 'xt%d' % b)

    def rms_rstd(src_sum, n, eps, dst, L, col=1):
        S.ts('dve', dst, src_sum, 1.0 / n, eps, ALU.mult, ALU.add)
        S.tt('pool', dst, dst, neghalf[0:L, 0:col], ALU.pow)

    def fm_block(wv, ncols_tiles, T, evac, role='mm'):
        for et in range(ncols_tiles):
            bk = nb(role)
            out = pf(bk)[:, 0:T]
            S.mms([dict(out=out, lhsT=wv[:, kc, et * 128:(et + 1) * 128], rhs=xnT[:, kc, 0:T],
                        start=(kc == 0), stop=(kc == 7)) for kc in range(8)])
            evac(et, out)

    def tm_block(wv, ncols, L, subs, evac, role='mm'):
        for sub in subs:
            bk = nb(role)
            out = pf(bk)[0:L, 0:ncols]
            S.mms([dict(out=out, lhsT=xnT[:, kc, sub * L:(sub + 1) * L], rhs=wv[:, kc, 0:ncols],
                        start=(kc == 0), stop=(kc == 7)) for kc in range(8)])
            evac(sub, out)

    thp = [0]

    def nth():
        thp[0] += 1
        return TH[thp[0] % 3]

    def store_state(dst_ap):
        for q in range(4):
            bk = nb('mm')
            S.trs([(pf(bk)[:, j * 128:(j + 1) * 128], hT[:, (q * 4 + j) * 128:(q * 4 + j + 1) * 128])
                   for j in range(4)], identf)
            S.copy('dve', stT[:, q * 4:(q + 1) * 4, :], pf(bk)[:, 0:512].rearrange("p (a b) -> p a b", a=4))
        S.dma('pool', bass.AP(dst_ap.tensor, dst_ap.offset, [[128, 128], [128 * 128, 16], [1, 128]]), stT, 'stT')

    def load_state(seq):
        src = st_ssm[seq]
        if 'nold' not in _DBG:
            S.dma('pool' if 'poolld' in _DBG else 'sp', stT,
                  bass.AP(src.tensor, src.offset, [[128, 128], [128 * 128, 16], [1, 128]]), 'stT')
        if 'notr' in _DBG:
            return
        for q in range(4):
            bk = nb('mm')
            S.trs([(pf(bk)[:, j * 128:(j + 1) * 128], stT[:, q * 4 + j, :]) for j in range(4)], identf)
            if 'nocp' in _DBG:
                continue
            S.copy('dve', hT[:, q * 512:(q + 1) * 512], pf(bk)[:, 0:512])
            if 'noact' in _DBG:
                continue
            S.copy('act', hTb[:, q * 512:(q + 1) * 512], pf(bk)[:, 0:512])

    def norm_tile(tj):
        kind_, idx_ = tiles[tj]
        L = 128 if kind_ == 'p' else 64
        xb = [xt[(tj % 2) * 2 + sub] for sub in range(2)]
        for sub in range(2):
            ssx = smallf[0:L, 0:1]
            S.act(junk[0:L, :], xb[sub][0:L, :], AF.Square, accum_out=ssx)
            rms_rstd(ssx, 1024.0, EPS, smallf[0:L, 1:2], L)
            S.act(xsn[0:L, :], xb[sub][0:L, :], AF.Identity, scale=smallf[0:L, 1:2])
            bk = nb('tp')
            S.trs([(pbf(bk)[:, kc * L:(kc + 1) * L], xsn[0:L, kc * 128:(kc + 1) * 128]) for kc in range(8)],
                  identb[0:L, 0:L])
            S.tt('dve', xnT[:, :, sub * L:(sub + 1) * L], pbf(bk)[:, 0:8 * L].rearrange("p (a b) -> p a b", a=8),
                 bc(normw, [[pstr(normw), 128], [1, 8], [0, L]]), ALU.mult)


    def kvga(tj, role):
        kind, idx = tiles[tj]
        L = 128 if kind == 'p' else 64
        T = 2 * L
        slot0 = (idx * 2) % 6

        def kv_out_row(sub):
            if kind == 'p':
                r0 = idx * 256 + sub * 128
                if r0 >= SEQ - KEEP:
                    return r0 - (SEQ - KEEP)
                return None
            return 448

        def kblock(ki):
            wv = w_use('k%d' % ki)

            def ev(et, out):
                pair = ki * 4 + et
                if kind == 'p':
                    S.copy('act', KT[:, pair, slot0:slot0 + 2, :], out.rearrange("p (s l) -> p s l", s=2))
                else:
                    S.copy('dve', KTs[:, pair, :, :], out.rearrange("p (s l) -> p s l", s=2))
            fm_block(wv, 4, T, ev, role)
            subs = [s_ for s_ in range(2) if kv_out_row(s_) is not None]

            def evk(sub, out):
                st = kvst[(ki + sub) % 2]
                S.copy('dve', st[0:L, :], out)
                r = kv_out_row(sub)
                if kind == 'p':
                    S.dma('pool', k_p[r:r + L, ki * 512:(ki + 1) * 512], st[0:L, :], 'kvst%d' % ((ki + sub) % 2))
                else:
                    seq = idx * 2 + sub
                    S.dma('pool', k_s[seq, 448:512, ki * 512:(ki + 1) * 512], st[0:L, :],
                          'kvst%d' % ((ki + sub) % 2))
            if subs:
                tm_block(wv, 512, L, subs, evk, role)

        def vblock(vi):
            if kind == 's' and vi == 0:
                S.memset('pool', Vs[:, :, :, 64:65], 1.0)
            wv = w_use('v%d' % vi)

            def evv(sub, out):
                if kind == 'p':
                    dst = Vr[0:L, slot0 + sub, vi * 8:(vi + 1) * 8, 0:64]
                else:
                    dst = Vs[0:L, sub, vi * 8:(vi + 1) * 8, 0:64]
                S.copy('act', dst, out.rearrange("p (h d) -> p h d", h=8))
                r = kv_out_row(sub)
                if r is not None:
                    st = kvst[(vi + sub) % 2]
                    S.copy('dve', st[0:L, :], out)
                    if kind == 'p':
                        S.dma('pool', v_p[r:r + L, vi * 512:(vi + 1) * 512], st[0:L, :],
                              'kvst%d' % ((vi + sub) % 2))
                    else:
                        seq = idx * 2 + sub
                        S.dma('pool', v_s[seq, 448:512, vi * 512:(vi + 1) * 512], st[0:L, :],
                              'kvst%d' % ((vi + sub) % 2))
            tm_block(wv, 512, L, [0, 1], evv, role)

        def gablock(gi):
            wv = w_use('ga%d' % gi)

            def evga(et, out):
                th = nth()
                S.act(th[:, 0:T], out, AF.Tanh, scale=0.5)
                S.act(ga_sig[:, gi * 4 + et, 0:T], th[:, 0:T], AF.Identity, scale=0.25, bias=0.25)
            fm_block(wv, 4, T, evga, role)
        return [lambda: kblock(0), lambda: kblock(1), lambda: vblock(0), lambda: vblock(1),
                lambda: gablock(0), lambda: gablock(1)]

    def can_prefetch(tj):
        if tj >= len(tiles):
            return False
        kind_, idx_ = tiles[tj]
        return kind_ == 'p' and idx_ >= 1 and (idx_ * 256 + 128) < SEQ - KEEP
    pre = [False] + [can_prefetch(tj) for tj in range(1, len(tiles))] + [False]

    def stop(n, ti):
        if _STOP == (n, ti):
            raise _StopBuild()

    load_x(0)
    norm_tile(0)
    for ti, (kind, idx) in enumerate(tiles):
        stop(0, ti)
        L = 128 if kind == 'p' else 64
        T = 2 * L
        last_p = (kind == 'p' and idx == NTP - 1)
        def kv_out_row(sub):
            if kind == 'p':
                r0 = idx * 256 + sub * 128
                if r0 >= SEQ - KEEP:
                    return r0 - (SEQ - KEEP)
                return None
            return 448
        if ti + 1 < len(tiles):
            load_x(ti + 1)
        xb = [xt[(ti % 2) * 2 + sub] for sub in range(2)]

        stop(1, ti)
        if kind == 'p':
            g0 = idx * 2
            slot0 = g0 % 6
        for qi in range(2):
            wv = w_use('q%d' % qi)

            def ev(et, out, qi=qi):
                S.act(QT[:, qi * 4 + et, 0:T], out, AF.Identity, scale=0.125)
            fm_block(wv, 4, T, ev)
        kvga_fns = kvga(ti, 'mm')
        if not pre[ti]:
            for f in kvga_fns[0:4]:
                f()
        stop(1.6, ti)
        for gi in range(2):
            wv = w_use('g%d' % gi)

            def evg(sub, out, gi=gi):
                th = nth()
                S.act(th[0:L, :], out, AF.Tanh, scale=0.5)
                S.stt('dve', gs2[0:L, sub, gi * 512:(gi + 1) * 512], th[0:L, :], 1.0, out, ALU.add, ALU.mult)
            tm_block(wv, 512, L, [0, 1], evg)
        if not pre[ti]:
            for f in kvga_fns[4:6]:
                f()
        stop(1.9, ti)
        wv = w_use('dt')

        def evdt(sub, out):
            S.tt('dve', dtmp[0:L, :], out, dtb_bc[0:L, :], ALU.add)
            S.act(dtmp[0:L, :], dtmp[0:L, :], AF.Exp)
            S.act(dtt[0:L, sub, :], dtmp[0:L, :], AF.Ln, bias=1.0)
            S.tt('dve', adt[0:L, sub, :], dtt[0:L, sub, :], a_bc[0:L, :], ALU.mult)
            S.copy('dve', adt_hi[0:L, sub, :], adt[0:L, sub, :])
            S.tt('dve', adt_lo[0:L, sub, :], adt[0:L, sub, :], adt_hi[0:L, sub, :], ALU.subtract)
        tm_block(wv, 32, L, [0, 1], evdt)

        if kind == 'p':
            segs = [(0, T)]
            SW = T + 3
            S.copy('pool', xraw[:, :, 0:3], carry)
        else:
            SW = 67
            segs = [(0, 64), (67, 64)]
            for s in range(2):
                S.copy('pool', xraw[:, :, s * 67:s * 67 + 3], convst[:, idx * 2 + s].rearrange("p j c -> p c j"))
        for xi in range(6):
            wv = w_use('x%d' % xi)

            def evx(et, out, xi=xi):
                ct = xi * 4 + et
                if kind == 'p':
                    S.copy('act', xraw[:, ct, 3:3 + T], out)
                else:
                    S.copy('dve', xraw[:, ct, 0:134].rearrange("p (s c) -> p s c", c=67)[:, :, 3:67],
                           out.rearrange("p (s c) -> p s c", c=64))
            fm_block(wv, 4, T, evx)
            if last_p or kind == 's':
                osubs = [1] if kind == 'p' else [0, 1]

                def evc(sub, out, xi=xi):
                    st = xbcst[(xi + sub) % 2]
                    S.copy('dve', st[L - 32:L, :], out[L - 32:L, :])
                    dst = conv_p if kind == 'p' else conv_s[idx * 2 + sub]
                    S.dma('pool', dst[:, xi * 512:(xi + 1) * 512], st[L - 3:L, :], 'xbcst%d' % ((xi + sub) % 2))
                tm_block(wv, 512, L, osubs, evc)
        def conv_block(xj):
            accs = [cacc[0], cacc[1], xbcst[0][:, 0:256], xbcst[1][:, 0:256]]
            for (c0, nt) in segs:
                cts = [xj * 4 + e for e in range(4)]
                for e, ct in enumerate(cts):
                    S.ts('dve', accs[e][:, 0:nt], xraw[:, ct, c0:c0 + nt], convw[:, 0, ct:ct + 1],
                         convb[:, ct:ct + 1], ALU.mult, ALU.add)
                for j in range(1, 4):
                    for e, ct in enumerate(cts):
                        S.stt('dve', accs[e][:, 0:nt], xraw[:, ct, c0 + j:c0 + j + nt],
                              convw[:, j, ct:ct + 1], accs[e][:, 0:nt], ALU.mult, ALU.add)
                if kind == 'p' and c0 == 0:
                    S.copy('pool', carry[:, xj * 4:(xj + 1) * 4, :], xraw[:, xj * 4:(xj + 1) * 4, T:T + 3])
                ths = []
                for e, ct in enumerate(cts):
                    th = nth()
                    ths.append(th)
                    S.act(th[:, 0:nt], accs[e][:, 0:nt], AF.Tanh)
                    if e >= 1:
                        S.stt('dve', xraw[:, cts[e - 1], c0:c0 + nt], ths[e - 1][:, 0:nt], 1.0,
                              accs[e - 1][:, 0:nt], ALU.add, ALU.mult)
                S.stt('dve', xraw[:, cts[3], c0:c0 + nt], ths[3][:, 0:nt], 1.0, accs[3][:, 0:nt],
                      ALU.add, ALU.mult)


        conv_todo = [0, 1, 2, 3, 4, 5]
        stop(2, ti)
        for sub in range(2):
            if kind == 'p':
                g = idx * 2 + sub
                tl = [t for t in range(5) if g - 4 + t >= 0]
            else:
                seq = idx * 2 + sub
                tl = [0, 1, 2, 3] if 'not4' in _DBG else ([4] if 'only4' in _DBG else [0, 1, 2, 3, 4])
                kv32 = bc(kvst[0], [[pstr(kvst[0]), 128], [1, 1024]])
                for tt_ in range(4):
                    S.dma('sp', kv32, cv[seq, tt_ * 128:(tt_ + 1) * 128, :], 'kvst0')
                    S.copy('dve', Vr[:, tt_, :, 0:64], kv32.rearrange("p (h d) -> p h d", h=AH))
                    S.dma('sp', ost, ck[seq, tt_ * 128:(tt_ + 1) * 128, :], 'ost')
                    S.copy('act', ckst[:, tt_, :], ost)
                    bk = nb('tp')
                    S.trs([(pbf(bk)[:, pr * 128:(pr + 1) * 128], ckst[:, tt_, pr * 128:(pr + 1) * 128])
                           for pr in range(8)], identb)
                    S.copy('dve', KT[:, :, tt_, :], pbf(bk)[:, 0:1024].rearrange("p (a b) -> p a b", a=8))
                S.dma('pool', k_s[seq, 0:448, :], ck[seq, 64:512, :], 'd2d')
                S.dma('pool', v_s[seq, 0:448, :], cv[seq, 64:512, :], 'd2d')
            stop(2.2, ti)
            for ob in OB:
                S.memset('dve', pf(ob)[0:L, :], 0.0)
            blocks = [(h, t) for h in range(AH) for t in tl]
            per = 512 // L
            groups = [blocks[i:i + per] for i in range(0, len(blocks), per)]
            pend = None

            def emit_pv(pg):
                grp, pt = pg
                if 'nopv' in _DBG:
                    return
                items = []
                for cb, (h, t) in enumerate(grp):
                    Mk = 64 if (kind == 's' and t == 4) else 128
                    if kind == 'p':
                        vt = Vr[0:Mk, (g - 4 + t) % 6, h, 0:65]
                    elif t < 4:
                        vt = Vr[0:Mk, t, h, 0:65]
                    else:
                        vt = Vs[0:Mk, sub, h, 0:65]
                    ob = OB[h // 6]
                    oc = (h % 6) * 65
                    items.append(dict(out=pf(ob)[0:L, oc:oc + 65], lhsT=pt[0:Mk, cb * L:(cb + 1) * L], rhs=vt,
                                      start=False, stop=True))
                S.mms(items)
            for gi_, grp in enumerate(groups):
                sbk = nb('S')
                items = []
                for cb, (h, t) in enumerate(grp):
                    pair, pb_ = h // 2, (h % 2) * 64
                    Mk = 64 if (kind == 's' and t == 4) else 128
                    out = pf(sbk)[0:Mk, cb * L:(cb + 1) * L]
                    if Mk == 128:
                        items.append(dict(out=out, lhsT=Jx, rhs=biasF[:, h, t, 0:L], start=True, stop=False))
                    else:
                        items.append(dict(out=out, lhsT=Jx[:, 0:64], rhs=biasF[:, h, t, 0:L],
                                          start=True, stop=False))
                    if kind == 'p':
                        kt = KT[pb_:pb_ + 64, pair, (g - 4 + t) % 6, 0:Mk]
                    elif t < 4:
                        kt = KT[pb_:pb_ + 64, pair, t, 0:Mk]
                    else:
                        kt = KTs[pb_:pb_ + 64, pair, sub, 0:Mk]
                    items.append(dict(out=out, lhsT=kt, rhs=QT[pb_:pb_ + 64, pair, sub * L:(sub + 1) * L],
                                      start=False, stop=True))
                S.mms(items)
                pt = PT[gi_ % 3]
                ncol = len(grp) * L
                S.act(pt[:, 0:ncol], pf(sbk)[:, 0:ncol], AF.Exp)
                if pend is not None:
                    emit_pv(pend)
                pend = (grp, pt)
                if conv_todo and (gi_ % 6 == 5 or len(groups) < 12):
                    conv_block(conv_todo.pop(0))
            emit_pv(pend)
            stop(2.4, ti)
            for obi, ob in enumerate(OB):
                h0 = obi * 6
                nh_ = min(6, AH - h0)
                ov = pf(ob)[0:L, 0:nh_ * 65].rearrange("p (h d) -> p h d", h=nh_)
                S.recip(rs[0:L, h0:h0 + nh_], ov[:, :, 64])
                S.tt('dve', o_n[0:L, h0 * 64:(h0 + nh_) * 64].rearrange("p (h d) -> p h d", h=nh_), ov[:, :, 0:64],
                     bc(rs[0:L, h0:h0 + nh_], [[pstr(rs), L], [1, nh_], [0, 64]]), ALU.mult)
            S.tt('pool', og[0:L, :], o_n[0:L, :], gs2[0:L, sub, :], ALU.mult)
            bk = nb('tp')
            S.trs([(pbf(bk)[:, kc * L:(kc + 1) * L], og[0:L, kc * 128:(kc + 1) * 128]) for kc in range(8)],
                  identb[0:L, 0:L])
            S.copy('dve', ogT[:, :, sub * L:(sub + 1) * L], pbf(bk)[:, 0:8 * L].rearrange("p (a b) -> p a b", a=8))

        while conv_todo:
            conv_block(conv_todo.pop(0))
        stop(3, ti)
        stop(4, ti)
        for wi in range(2):
            wv = w_use('wa%d' % wi)
            for et in range(4):
                bk = nb('mm')
                out = pf(bk)[:, 0:T]
                S.mms([dict(out=out, lhsT=wv[:, kc, et * 128:(et + 1) * 128], rhs=ogT[:, kc, 0:T],
                            start=(kc == 0), stop=(kc == 7)) for kc in range(8)])
                S.tt('dve', accT[:, wi * 4 + et, 0:T], out, ga_sig[:, wi * 4 + et, 0:T], ALU.mult)

        stop(5, ti)
        for zi in range(4):
            wv = w_use('z%d' % zi)

            def evz(sub, out, zi=zi):
                th = nth()
                S.act(th[0:L, :], out, AF.Tanh, scale=0.5)
                S.stt('dve', zs2[0:L, sub, zi * 512:(zi + 1) * 512], th[0:L, :], 1.0, out, ALU.add, ALU.mult)
            tm_block(wv, 512, L, [0, 1], evz)
        for gi in range(2):
            wv = w_use('gs%d' % gi)

            def evgs(et, out, gi=gi):
                th = nth()
                S.act(th[:, 0:T], out, AF.Tanh, scale=0.5)
                S.act(gs_sig[:, gi * 4 + et, 0:T], th[:, 0:T], AF.Identity, scale=0.5, bias=0.5)
            fm_block(wv, 4, T, evgs)

        if ti + 1 < len(tiles):
            norm_tile(ti + 1)
        stop(6, ti)
        pf_fns = kvga(ti + 1, 'mm2') if pre[ti + 1] else []
        def cview(t, n):
            return bc(t, [[pstr(t), L], [L, n], [1, L]])
        Bmh_v, Bml_v, Eb_v, MT_v, cb_v = cview(Bmh, 8), cview(Bml, 8), cview(Eb, 8), cview(MTb, 8), cview(cbTm, 4)
        for sub in range(2):
            if kind == 'p':
                cs = sub * 128
            else:
                cs = sub * 67
                load_state(idx * 2 + sub)
            stop(6.2, ti)
            for hb in range(2):
                bk = nb('tp')
                S.trs([(pbf(bk)[0:L, j * 128:(j + 1) * 128], xraw[:, hb * 8 + j, cs:cs + L]) for j in range(8)],
                      identb)
                S.copy('act' if hb else 'dve', xs_tok[0:L, hb * 1024:(hb + 1) * 1024], pbf(bk)[0:L, 0:1024])
            bk = nb('tp')
            S.trs([(pbf(bk)[0:L, j * 128:(j + 1) * 128], xraw[:, 16 + j, cs:cs + L]) for j in range(4)], identb)
            S.copy('dve', B_tok[0:L, :], pbf(bk)[0:L, 0:512])
            stop(6.4, ti)
            sm = pf(SMB)
            ah, al = adt_hi[0:L, sub, :], adt_lo[0:L, sub, :]
            S.mms([dict(out=sm[0:L, 0:32], lhsT=tri[0:L, 0:L], rhs=ah, start=True, stop=False),
                   dict(out=sm[0:L, 0:32], lhsT=tri[0:L, 0:L], rhs=al, start=False, stop=True),
                   dict(out=sm[0:L, 32:64], lhsT=SU[0:L, 0:L], rhs=ah, start=True, stop=False),
                   dict(out=sm[0:L, 32:64], lhsT=SU[0:L, 0:L], rhs=al, start=False, stop=True),
                   dict(out=sm[:, 64:96], lhsT=onesb[0:L, :], rhs=ah, start=True, stop=False),
                   dict(out=sm[:, 64:96], lhsT=onesb[0:L, :], rhs=al, start=False, stop=True)])
            S.act(ee[0:L, :], sm[0:L, 0:64], AF.Exp)
            S.act(ee2, sm[:, 64:96], AF.Exp)
            cbb = pf(CBB)
            S.mms([dict(out=cbb[0:L, gq * L:(gq + 1) * L], lhsT=xraw[:, 16 + gq, cs:cs + L],
                        rhs=xraw[:, 20 + gq, cs:cs + L], start=True, stop=True) for gq in range(4)])
            S.tt('dve', cb_v, cbb[0:L, 0:4 * L].rearrange("p (g l) -> p g l", g=4),
                 bc(tri, [[pstr(tri), L], [0, 4], [1, L]]), ALU.mult)
            stop(6.6, ti)
            def b8(ap2):
                return bc(ap2, [[ap2.ap[0][0], L], [1, 8], [0, 64]])
            x3 = lambda a: a[0:L, :].rearrange("p (r d) -> p r d", r=8)
            nhalf = 2 if L == 128 else 1
            hh = 8 // nhalf

            def s0_pool(gq):
                xdt, xdt2, xsd = xdtb[gq % 2], xdt2b[gq % 2], xsdb[gq % 2]
                xsg = xs_tok[0:L, gq * 512:(gq + 1) * 512].rearrange("p (r d) -> p r d", r=8)
                S.tt('pool', x3(xdt), xsg, b8(dtt[0:L, sub, gq * 8:(gq + 1) * 8]), ALU.mult)
                S.tt('pool', x3(xdt2), x3(xdt), b8(ee[0:L, 32 + gq * 8:32 + (gq + 1) * 8]), ALU.mult)
                S.tt('pool', x3(xsd), xsg, b8(dsk_bc[0:L, gq * 8:(gq + 1) * 8]), ALU.mult)

            def s1a(gq):
                for (src, dst) in ((adt_hi, Bmh_v), (adt_lo, Bml_v)):
                    a8 = src[0:L, sub, gq * 8:(gq + 1) * 8]
                    S.tt('dve', dst, bc(a8, [[a8.ap[0][0], L], [1, 8], [0, L]]),
                         bc(tri, [[pstr(tri), L], [0, 8], [1, L]]), ALU.mult)

            def s1b(gq):
                items = []
                for hf in range(nhalf):
                    out = pf(SEGB[hf])[0:L, 0:hh * L]
                    items.append(dict(out=out, lhsT=SU[0:L, 0:L], rhs=Bmh_v[:, hf * hh:(hf + 1) * hh, :],
                                      start=True, stop=False))
                    items.append(dict(out=out, lhsT=SU[0:L, 0:L], rhs=Bml_v[:, hf * hh:(hf + 1) * hh, :],
                                      start=False, stop=True))
                S.mms(items)
                for hf in range(nhalf):
                    S.act(Eb_v[:, hf * hh:(hf + 1) * hh, :],
                          pf(SEGB[hf])[0:L, 0:hh * L].rearrange("p (r l) -> p r l", r=hh), AF.Exp)

            def s2(gq):
                xdt, xdt2, xsd = xdtb[gq % 2], xdt2b[gq % 2], xsdb[gq % 2]
                cg = cb_v[:, gq, :]
                S.tt('dve', MT_v, Eb_v, bc(cg, [[cg.ap[0][0], L], [0, 8], [1, L]]), ALU.mult)
                yb = pf(YB)
                S.mms([dict(out=yb[0:L, 0:512], lhsT=xraw[:, 20 + gq, cs:cs + L], rhs=hTb[:, gq * 512:(gq + 1) * 512],
                            start=True, stop=False)])
                S.tt('dve', yb[0:L, 0:512].rearrange("p (r d) -> p r d", r=8),
                     yb[0:L, 0:512].rearrange("p (r d) -> p r d", r=8), b8(ee[0:L, gq * 8:(gq + 1) * 8]), ALU.mult)
                items = [dict(out=yb[0:L, 0:512], lhsT=identb[0:L, 0:L], rhs=xsd[0:L, :], start=False, stop=False)]
                for r in range(8):
                    items.append(dict(out=yb[0:L, r * 64:(r + 1) * 64], lhsT=MT_v[:, r, :],
                                      rhs=xdt[0:L, r * 64:(r + 1) * 64], start=False, stop=(r == 7)))
                S.mms(items)
                ub = pf(UB)
                S.mms([dict(out=ub[:, 0:512], lhsT=B_tok[0:L, gq * 128:(gq + 1) * 128], rhs=xdt2[0:L, :],
                            start=True, stop=True)])

            def s3(gq):
                yb = pf(YB)
                ub = pf(UB)
                S.tt('dve', yz[0:L, gq, :], yb[0:L, 0:512], zs2[0:L, sub, gq * 512:(gq + 1) * 512], ALU.mult)
                S.act(junk[0:L, 0:512], yz[0:L, gq, :], AF.Square, accum_out=ssq[0:L, gq:gq + 1])
                hg = hT[:, gq * 512:(gq + 1) * 512]
                e8 = ee2[:, gq * 8:(gq + 1) * 8]
                S.tt('pool', hg.rearrange("p (r d) -> p r d", r=8), hg.rearrange("p (r d) -> p r d", r=8),
                     bc(e8, [[e8.ap[0][0], 128], [1, 8], [0, 64]]), ALU.mult)
                S.tt('dve', hg, hg, ub[:, 0:512], ALU.add)
                S.copy('act', hTb[:, gq * 512:(gq + 1) * 512], hg)

            s0_pool(0)
            s1a(0)
            s1b(0)
            for gq in range(4):
                if gq < 3:
                    s0_pool(gq + 1)
                s2(gq)
                if gq < 3:
                    s1a(gq + 1)
                    s1b(gq + 1)
                s3(gq)
                if pf_fns:
                    pf_fns.pop(0)()
            stop(6.8, ti)
            S.ts('dve', rstd4[0:L, :], ssq[0:L, :], 1.0 / 512.0, 4.0 * EPS, ALU.mult, ALU.add)
            S.tt('pool', rstd4[0:L, :], rstd4[0:L, :], neghalf[0:L, :], ALU.pow)
            S.tt('pool', yz[0:L, :, :], yz[0:L, :, :], bc(rstd4, [[pstr(rstd4), L], [1, 4], [0, 512]]), ALU.mult)
            yzf = yz[0:L, :, :].rearrange("p a b -> p (a b)")
            for hb in range(2):
                bk = nb('tp')
                S.trs([(pbf(bk)[:, j * L:(j + 1) * L], yzf[:, (hb * 8 + j) * 128:(hb * 8 + j + 1) * 128])
                       for j in range(8)], identb[0:L, 0:L])
                nwv = nwssm[:, hb * 8:(hb + 1) * 8]
                S.tt('dve', yzT[:, hb * 8:(hb + 1) * 8, sub * L:(sub + 1) * L],
                     pbf(bk)[:, 0:8 * L].rearrange("p (a b) -> p a b", a=8),
                     bc(nwv, [[nwv.ap[0][0], 128], [1, 8], [0, L]]), ALU.mult)
            stop(6.9, ti)
            if kind == 's':
                store_state(ssm_s[idx * 2 + sub])
            elif last_p and sub == 1:
                store_state(ssm_p)

        stop(7, ti)
        while pf_fns:
            pf_fns.pop(0)()
        for wi in range(4):
            wv = w_use('ws%d' % wi)
            for et in range(2):
                e8 = wi * 2 + et
                bk = nb('mm')
                out = pf(bk)[:, 0:T]
                S.mms([dict(out=out, lhsT=wv[:, kc, et * 128:(et + 1) * 128], rhs=yzT[:, kc, 0:T],
                            start=(kc == 0), stop=(kc == 15)) for kc in range(16)])
                th = nth()
                S.tt('dve', th[:, 0:T], out, gs_sig[:, e8, 0:T], ALU.mult)
                S.tt('pool', mT[:, e8, 0:T], th[:, 0:T], accT[:, e8, 0:T], ALU.add)

        stop(8, ti)
        wvs = [w_use('wo0'), w_use('wo1', issue=False)]
        for sub in range(2):
            for hf in range(2):
                bk = nb('mm')
                out = pf(bk)[0:L, 0:512]
                S.mms([dict(out=out, lhsT=mT[:, kc, sub * L:(sub + 1) * L], rhs=wvs[hf][:, kc, :],
                            start=(kc == 0), stop=(kc == 7)) for kc in range(8)])
                S.tt('dve', ost[0:L, hf * 512:(hf + 1) * 512], out, xb[sub][0:L, hf * 512:(hf + 1) * 512], ALU.add)
            ssx = smallf[0:L, 2:3]
            S.act(junk[0:L, :], ost[0:L, :], AF.Square, accum_out=ssx)
            rms_rstd(ssx, 1024.0, EPS, smallf[0:L, 3:4], L)
            S.stt('dve', ost[0:L, :], ost[0:L, :], smallf[0:L, 3:4], fnw_bc[0:L, :], ALU.mult, ALU.mult)
            _, dsty = xrows(kind, idx, sub, L)
            S.dma('pool', dsty, ost[0:L, :], 'ost')

    if dbg is not None:
        dbg(S, locals())


_CACHE = {}


def _get_nc(SEQ):
    if SEQ not in _CACHE:
        nc = bass.Bass("TRN2", target_bir_lowering=False)
        build(nc, SEQ)
        _CACHE[SEQ] = nc
    return _CACHE[SEQ]


def kernel(x_prompt, x_sample, state_ssm, state_conv, cache_k, cache_v, norm_w, w_in, conv_w, conv_b, dt_bias,
           a_log, d_skip, ssm_norm_w, w_out_ssm, rel_bias, w_out_att, w_o, final_norm_w):
    f = lambda a: np.ascontiguousarray(np.asarray(a, dtype=np.float32))
    B, SEQ, _ = x_prompt.shape
    assert B == 8 and SEQ % 256 == 0
    KEEP = min(512, SEQ)
    nc = _get_nc(SEQ)
    shared = dict(norm_w=f(norm_w[0]), w_in=f(w_in[0]), conv_w=f(conv_w[0]), conv_b=f(conv_b[0]),
                  dt_bias=f(dt_bias[0]), a_log=f(a_log[0]), d_skip=f(d_skip[0]), ssm_norm_w=f(ssm_norm_w[0]),
                  w_out_ssm=f(w_out_ssm[0]), rel_bias=f(rel_bias[0]), w_out_att=f(w_out_att[0]), w_o=f(w_o[0]),
                  final_norm_w=f(final_norm_w))
    in_maps = []
    for c in range(_NCORES):
        m = dict(shared)
        m['x_p'] = f(x_prompt[c])
        m['x_s'] = f(x_sample[4 * c:4 * c + 4]).reshape(256, D)
        m['st_ssm'] = f(state_ssm[0, 4 * c:4 * c + 4]).reshape(4, 2048, 128)
        m['st_conv'] = f(state_conv[0, 4 * c:4 * c + 4])
        m['ck'] = f(cache_k[0, 4 * c:4 * c + 4]).reshape(4, 512, D)
        m['cv'] = f(cache_v[0, 4 * c:4 * c + 4]).reshape(4, 512, D)
        in_maps.append(m)
    res = run_bass_kernel_spmd(nc, in_maps, core_ids=list(range(_NCORES)))
    R = list(res.results) + [res.results[0]] * (8 - _NCORES)
    y_prompt = np.stack([R[c]['y_p'] for c in range(8)]).astype(np.float32)
    y_sample = np.concatenate([R[c]['y_s'].reshape(4, 64, D) for c in range(8)]).astype(np.float32)
    ssm_p = np.stack([R[c]['ssm_p'].reshape(NH, 64, 128) for c in range(8)])[None].astype(np.float32)
    conv_p = np.stack([R[c]['conv_p'] for c in range(8)])[None].astype(np.float32)
    k_p = np.stack([R[c]['k_p'].reshape(KEEP, AH, 64) for c in range(8)])[None].astype(np.float32)
    v_p = np.stack([R[c]['v_p'].reshape(KEEP, AH, 64) for c in range(8)])[None].astype(np.float32)
    ssm_s = np.concatenate([R[c]['ssm_s'].reshape(4, NH, 64, 128) for c in range(8)])[None].astype(np.float32)
    conv_s = np.concatenate([R[c]['conv_s'] for c in range(8)])[None].astype(np.float32)
    k_s = np.concatenate([R[c]['k_s'].reshape(4, 512, AH, 64) for c in range(8)])[None].astype(np.float32)
    v_s = np.concatenate([R[c]['v_s'].reshape(4, 512, AH, 64) for c in range(8)])[None].astype(np.float32)
    return (y_prompt, y_sample, ssm_p, conv_p, k_p, v_p, ssm_s, conv_s, k_s, v_s)
```

```python
import numpy as np
from contextlib import ExitStack
import concourse.bass as bass
import concourse.mybir as mybir
from concourse.bass_utils import run_bass_kernel_spmd

F32 = mybir.dt.float32
BF16 = mybir.dt.bfloat16
U8 = mybir.dt.uint8
AF = mybir.ActivationFunctionType
ALU = mybir.AluOpType

D = 1024
KC = 8
DI = 2048
CD = 3072
NH = 32
AH = 16
DIN = 11296
C_Z, C_X, C_DT, C_Q, C_K, C_V, C_G, C_GS, C_GA = 0, 2048, 5120, 5152, 6176, 7200, 8224, 9248, 10272
EPS = 1e-5
NEG = -30000.0


def _dsz(dt):
    if dt == F32:
        return 4
    if dt == BF16:
        return 2
    if dt == U8:
        return 1
    raise ValueError(dt)


class _Op:
    __slots__ = ('eng', 'fn', 'deps', 'dma', 'key', 'val', 'seq', 'hasdep')


class Sched:
    def __init__(self, nc, es):
        self.nc = nc
        self.es = es
        self.ops = []
        self.engobj = {'pe': nc.tensor, 'act': nc.scalar, 'dve': nc.vector, 'pool': nc.gpsimd, 'sp': nc.sync}
        self.esem = {k: es.enter_context(nc.semaphore('es_' + k)) for k in self.engobj}
        self.dkeys = {}
        self.tracked = {}
        self.acc = {}

    def track(self, name, const=False):
        self.tracked[name] = 'const' if const else True

    def reg(self, ap):
        name = ap.tensor.name
        tr = self.tracked.get(name)
        if tr is None:
            return None
        dims = ap.ap
        ps = dims[0][0]
        npart = dims[0][1]
        off = ap.offset
        if ps > 0:
            p0 = off // ps
            f0 = off % ps
        else:
            p0 = 0
            f0 = off
        lo = f0
        hi = f0
        for (s, c) in dims[1:]:
            if s >= 0:
                hi += s * (c - 1)
            else:
                lo += s * (c - 1)
        sz = _dsz(ap.dtype)
        return (name, p0, p0 + npart, lo * sz, (hi + 1) * sz)

    def add(self, eng, fn, reads=(), writes=(), dma_key=None):
        i = len(self.ops)
        deps = set()
        rr = []
        ww = []
        for a in reads:
            r = a if isinstance(a, tuple) else self.reg(a)
            if r is not None:
                if r[0].startswith('pb'):
                    ww.append((r[0], 0, 128, 0, 2048))
                else:
                    rr.append(r)
        for a in writes:
            r = a if isinstance(a, tuple) else self.reg(a)
            if r is not None:
                if r[0].startswith('pb'):
                    r = (r[0], 0, 128, 0, 2048)
                ww.append(r)
        for (k, p0, p1, lo, hi) in rr:
            for e in self.acc.get(k, ()):
                if e[5] and e[0] < p1 and p0 < e[1] and e[2] < hi and lo < e[3]:
                    deps.add(e[4])
        for (k, p0, p1, lo, hi) in ww:
            lst = self.acc.get(k)
            if lst is None:
                continue
            keep = []
            for e in lst:
                if e[0] < p1 and p0 < e[1] and e[2] < hi and lo < e[3]:
                    deps.add(e[4])
                    if p0 <= e[0] and e[1] <= p1 and lo <= e[2] and e[3] <= hi:
                        continue
                keep.append(e)
            self.acc[k] = keep
        deps.discard(i)
        for (k, p0, p1, lo, hi) in rr:
            if self.tracked.get(k) == 'const':
                continue
            self.acc.setdefault(k, []).append([p0, p1, lo, hi, i, False])
        for (k, p0, p1, lo, hi) in ww:
            self.acc.setdefault(k, []).append([p0, p1, lo, hi, i, True])
        op = _Op()
        op.eng = eng
        op.fn = fn
        op.deps = deps
        op.dma = dma_key is not None
        op.key = dma_key
        op.val = 0
        op.seq = 0
        op.hasdep = False
        if op.dma:
            ent = self.dkeys.get(dma_key)
            if ent is None:
                ent = [self.es.enter_context(self.nc.semaphore('d_%d' % len(self.dkeys))), 0, []]
                self.dkeys[dma_key] = ent
            ent[1] += 16
            op.val = ent[1]
            ent[2].append(i)
        self.ops.append(op)
        return i

    def group_barrier(self, key):
        ent = self.dkeys[key]
        for i in ent[2]:
            self.ops[i].val = ent[1]

    def group_last(self, key, n):
        ent = self.dkeys[key]
        for i in ent[2][-n:]:
            self.ops[i].val = ent[1]

    def emit(self):
        ops = self.ops
        for op in ops:
            for d in op.deps:
                ops[d].hasdep = True
        cnt = {k: 0 for k in self.engobj}
        for op in ops:
            if not op.dma and op.hasdep:
                cnt[op.eng] += 1
                op.seq = cnt[op.eng]
        waited = {k: {} for k in self.engobj}
        nw = 0
        for op in ops:
            eng = self.engobj[op.eng]
            wd = waited[op.eng]
            need = {}
            for d in op.deps:
                dop = ops[d]
                if dop.dma:
                    sem, val = self.dkeys[dop.key][0], dop.val
                else:
                    if dop.eng == 'pe' and op.eng == 'pe' and not op.dma:
                        continue
                    sem, val = self.esem[dop.eng], dop.seq
                k = id(sem)
                if need.get(k, (None, 0))[1] < val:
                    need[k] = (sem, val)
            for k, (sem, val) in need.items():
                if wd.get(k, 0) >= val:
                    continue
                eng.wait_ge(sem, val)
                wd[k] = val
                nw += 1
            inst = op.fn(eng)
            if op.dma:
                inst.then_inc(self.dkeys[op.key][0], 16)
            elif op.hasdep:
                inst.then_inc(self.esem[op.eng], 1)
        sp = self.engobj['sp']
        for key, ent in self.dkeys.items():
            sp.wait_ge(ent[0], ent[1])
        return len(ops), nw

    def mms(self, items):
        reads = []
        writes = []
        for it in items:
            reads += [it['lhsT'], it['rhs']]
            writes.append(it['out'])
            if not it['start']:
                reads.append(it['out'])

        def fn(e, items=items):
            inst = None
            for it in items:
                inst = e.matmul(it['out'], lhsT=it['lhsT'], rhs=it['rhs'], start=it['start'], stop=it['stop'],
                                skip_group_check=True)
            return inst
        return self.add('pe', fn, reads, writes)

    def trs(self, items, ident):
        reads = [ident] + [x[1] for x in items]
        writes = [x[0] for x in items]

        def fn(e, items=items, ident=ident):
            inst = None
            for (o, i) in items:
                inst = e.transpose(o, i, ident)
            return inst
        return self.add('pe', fn, reads, writes)

    def act(self, out, in_, func, bias=None, scale=None, accum_out=None, extra_w=()):
        reads = [in_]
        kw = {}
        if bias is not None:
            kw['bias'] = bias
            if not isinstance(bias, (int, float)):
                reads.append(bias)
        if scale is not None:
            kw['scale'] = scale
            if not isinstance(scale, (int, float)):
                reads.append(scale)
        writes = [out] + list(extra_w)
        if accum_out is not None:
            kw['accum_out'] = accum_out
            writes.append(accum_out)
        return self.add('act', lambda e: e.activation(out=out, in_=in_, func=func, **kw), reads, writes)

    def tt(self, eng, out, in0, in1, op):
        return self.add(eng, lambda e: e.tensor_tensor(out=out, in0=in0, in1=in1, op=op), [in0, in1], [out])

    def ts(self, eng, out, in0, s1, s2, op0, op1=None):
        reads = [in0]
        if not isinstance(s1, (int, float)):
            reads.append(s1)
        if s2 is not None and not isinstance(s2, (int, float)):
            reads.append(s2)
        if op1 is None:
            return self.add(eng, lambda e: e.tensor_scalar(out=out, in0=in0, scalar1=s1, scalar2=None, op0=op0),
                            reads, [out])
        return self.add(eng, lambda e: e.tensor_scalar(out=out, in0=in0, scalar1=s1, scalar2=s2, op0=op0, op1=op1),
                        reads, [out])

    def stt(self, eng, out, in0, scalar, in1, op0, op1):
        reads = [in0, in1]
        if not isinstance(scalar, (int, float)):
            reads.append(scalar)
        return self.add(eng, lambda e: e.scalar_tensor_tensor(out=out, in0=in0, scalar=scalar, in1=in1, op0=op0,
                                                              op1=op1), reads, [out])

    def copy(self, eng, out, in_):
        if eng == 'act':
            return self.add('act', lambda e: e.activation(out=out, in_=in_, func=AF.Identity), [in_], [out])
        return self.add(eng, lambda e: e.tensor_copy(out=out, in_=in_), [in_], [out])

    def memset(self, eng, ap, val):
        return self.add(eng, lambda e: e.memset(ap, val), [], [ap])

    def recip(self, out, in_):
        return self.add('dve', lambda e: e.reciprocal(out=out, in_=in_), [in_], [out])

    def dma(self, q, out, in_, key, extra_r=(), extra_w=(), slow=False):
        if slow:
            fn = lambda e: e.dma_start(out=out, in_=in_, allow_slow_non_contiguous=True)
        else:
            fn = lambda e: e.dma_start(out=out, in_=in_)
        return self.add(q, fn, [in_] + list(extra_r), [out] + list(extra_w), dma_key=key)


def bc(ap, dims, off=0):
    return bass.AP(ap.tensor, ap.offset + off, dims)


def pstr(ap):
    return ap.ap[0][0]


class Arena:
    def __init__(self, nc, es, S, name, nbytes):
        self.t = es.enter_context(nc.sbuf_tensor(name, [128, nbytes], U8))
        self.off = 0
        self.cap = nbytes
        self.name = name
        S.track(name)

    def alloc(self, free_shape, dt):
        n = int(np.prod(free_shape)) * _dsz(dt)
        off = (self.off + 31) // 32 * 32
        assert off + n <= self.cap, (self.name, off, n, self.cap)
        self.off = off + n
        v = self.t[:, off:off + n].bitcast(dt)
        if len(free_shape) == 2:
            v = v.rearrange("p (a b) -> p a b", a=free_shape[0])
        elif len(free_shape) == 3:
            v = v.rearrange("p (a b c) -> p a b c", a=free_shape[0], b=free_shape[1])
        return v


_STOP = (99, 0)
_NCORES = 8
_DBG = ''
_ENV = {}
_DUMP = None


class _StopBuild(Exception):
    pass


def build(nc, SEQ, dbg=None):
    es = ExitStack()
    S = Sched(nc, es)
    try:
        _body(nc, SEQ, es, S, dbg)
    except _StopBuild:
        pass
    if _DUMP is not None:
        _DUMP(S, _ENV)
    nops, nw = S.emit()
    es.close()
    return nops, nw


def _body(nc, SEQ, es, S, dbg=None):
    NTP = SEQ // 256
    KEEP = min(512, SEQ)

    def din(name, shape):
        return nc.dram_tensor(name, shape, F32, kind="ExternalInput").ap()

    def dout(name, shape):
        return nc.dram_tensor(name, shape, F32, kind="ExternalOutput").ap()

    x_p = din("x_p", [SEQ, D])
    x_s = din("x_s", [256, D])
    st_ssm = din("st_ssm", [4, 2048, 128])
    st_conv = din("st_conv", [4, 3, CD])
    ck = din("ck", [4, 512, D])
    cv = din("cv", [4, 512, D])
    norm_w = din("norm_w", [D])
    w_in = din("w_in", [D, DIN])
    conv_w = din("conv_w", [4, CD])
    conv_b = din("conv_b", [CD])
    dt_bias = din("dt_bias", [NH])
    a_log = din("a_log", [NH])
    d_skip = din("d_skip", [NH])
    ssm_norm_w = din("ssm_norm_w", [DI])
    w_out_ssm = din("w_out_ssm", [DI, D])
    rel_bias = din("rel_bias", [AH, 513])
    w_out_att = din("w_out_att", [D, D])
    w_o = din("w_o", [D, D])
    final_norm_w = din("final_norm_w", [D])

    y_p = dout("y_p", [SEQ, D])
    y_s = dout("y_s", [256, D])
    ssm_p = dout("ssm_p", [2048, 128])
    conv_p = dout("conv_p", [3, CD])
    k_p = dout("k_p", [KEEP, D])
    v_p = dout("v_p", [KEEP, D])
    ssm_s = dout("ssm_s", [4, 2048, 128])
    conv_s = dout("conv_s", [4, 3, CD])
    k_s = dout("k_s", [4, 512, D])
    v_s = dout("v_s", [4, 512, D])

    WB = {}
    blist = []

    def addblk(name, src, nrow_chunks, c0, ncols, N):
        WB[name] = dict(idx=len(blist), src=src, kc=nrow_chunks, c0=c0, ncols=ncols, N=N)
        blist.append(name)

    for i in range(2):
        addblk('q%d' % i, w_in, 8, C_Q + 512 * i, 512, DIN)
    for i in range(2):
        addblk('k%d' % i, w_in, 8, C_K + 512 * i, 512, DIN)
    for i in range(2):
        addblk('v%d' % i, w_in, 8, C_V + 512 * i, 512, DIN)
    for i in range(2):
        addblk('g%d' % i, w_in, 8, C_G + 512 * i, 512, DIN)
    for i in range(2):
        addblk('ga%d' % i, w_in, 8, C_GA + 512 * i, 512, DIN)
    for i in range(2):
        addblk('wa%d' % i, w_out_att, 8, 512 * i, 512, D)
    addblk('dt', w_in, 8, C_DT, 32, DIN)
    for i in range(6):
        addblk('x%d' % i, w_in, 8, C_X + 512 * i, 512, DIN)
    for i in range(4):
        addblk('z%d' % i, w_in, 8, C_Z + 512 * i, 512, DIN)
    for i in range(2):
        addblk('gs%d' % i, w_in, 8, C_GS + 512 * i, 512, DIN)
    for i in range(4):
        addblk('ws%d' % i, w_out_ssm, 16, 256 * i, 256, D)
    for i in range(2):
        addblk('wo%d' % i, w_o, 8, 512 * i, 512, D)
    NB = len(blist)
    wscr = nc.dram_tensor("wscr", [NB, 128, 4096], BF16, kind="Internal").ap()
    ext = nc.dram_tensor("ext_bias", [AH, 1024], F32, kind="Internal").ap()

    def wscr_view(name):
        b = WB[name]
        i = b['idx']
        kc, nc_ = b['kc'], b['ncols']
        base = wscr[i]
        return bc(base, [[4096, 128], [nc_, kc], [1, nc_]])

    NGRP = 4
    for n, name in enumerate(blist):
        b = WB[name]
        src = bass.AP(b['src'].tensor, b['c0'], [[b['N'], 128], [128 * b['N'], b['kc']], [1, b['ncols']]])
        grp = min(NGRP - 1, n * NGRP // NB)
        S.dma('pool', wscr_view(name), src, 'cv%d' % grp, extra_w=[('wscr', b['idx'], b['idx'] + 1, 0, 1)])
        b['grp'] = grp
    for g in range(NGRP):
        S.group_barrier('cv%d' % g)

    def cstop(n):
        if _STOP == (n, 0):
            raise _StopBuild()
    cstop(-1)

    CA = Arena(nc, es, S, "cst", 29 * 1024)
    S.tracked["cst"] = True
    A = Arena(nc, es, S, "main", 183040)

    identb = CA.alloc([128], BF16)
    identf = CA.alloc([128], F32)
    Jx = CA.alloc([128], BF16)
    tri = CA.alloc([128], BF16)
    SU = CA.alloc([128], BF16)
    onesb = CA.alloc([128], BF16)
    scr32 = CA.alloc([128], F32)
    cstg = CA.alloc([128], F32)
    cpar = CA.alloc([144], F32)
    normw = cpar[:, 0:8]
    nwssm = cpar[:, 8:24]
    convb = cpar[:, 24:48]
    convw = cpar[:, 48:144].rearrange("p (j c) -> p j c", j=4)
    dtb_bc = CA.alloc([32], F32)
    a_bc = CA.alloc([32], F32)
    dsk_bc = CA.alloc([32], F32)
    fnw_bc = CA.alloc([1024], F32)
    neghalf = CA.alloc([4], F32)
    biasF = CA.alloc([AH, 5, 128], BF16)
    convst = CA.alloc([4, 3, 24], BF16)

    xt = [A.alloc([1024], F32) for _ in range(4)]
    xsn = A.alloc([1024], BF16)
    TH = [A.alloc([512], BF16) for _ in range(3)]
    junk = bc(TH[0], [[pstr(TH[0]), 128], [1, 1024]])
    xnT = A.alloc([8, 256], BF16)
    NSLOT = 3
    Wr = [A.alloc([4096], BF16) for _ in range(NSLOT)]
    KT = A.alloc([8, 6, 128], BF16)
    Vr = A.alloc([6, AH, 66], BF16)
    hT = A.alloc([2048], F32)
    hTb = A.alloc([2048], BF16)
    ost = A.alloc([1024], F32)
    ogT = A.alloc([8, 256], BF16)
    ga_sig = A.alloc([8, 256], BF16)
    gs_sig = A.alloc([8, 256], BF16)
    accT = A.alloc([8, 256], BF16)
    yzT = A.alloc([16, 256], BF16)
    mT = A.alloc([8, 256], BF16)
    smallf = A.alloc([64], F32)
    carry = A.alloc([24, 3], BF16)
    QT = A.alloc([8, 256], BF16)
    gs2 = A.alloc([2, 1024], BF16)
    base = A.off
    PT = [A.alloc([512], BF16) for _ in range(3)]
    rs = A.alloc([16], F32)
    o_n = A.alloc([1024], BF16)
    og = A.alloc([1024], BF16)
    kvst = [A.alloc([512], F32) for _ in range(2)]
    ckst = A.alloc([4, 1024], BF16)
    KTs = A.alloc([8, 2, 64], BF16)
    Vs = A.alloc([2, AH, 66], BF16)
    att_end = A.off
    A.off = base
    zs2 = A.alloc([2, 2048], BF16)
    xs_tok = A.alloc([2048], BF16)
    B_tok = A.alloc([512], BF16)
    xdtb = [A.alloc([512], BF16) for _ in range(2)]
    xdt2b = [A.alloc([512], BF16) for _ in range(2)]
    xsdb = [A.alloc([512], BF16) for _ in range(2)]
    cbTm = A.alloc([4, 128], BF16)
    bm_off = (A.off + 31) // 32 * 32
    Bmh = A.alloc([8, 128], BF16)
    Bml = A.alloc([8, 128], BF16)
    Eb = A.alloc([8, 128], BF16)
    MTb = A.alloc([8, 128], BF16)
    yz = A.alloc([4, 512], BF16)
    ssq = A.alloc([4], F32)
    rstd4 = A.alloc([4], F32)
    A.off = max(A.off, att_end)
    XW = 3 + 256
    xraw = A.alloc([24, XW], BF16)
    cacc = [A.alloc([256], F32) for _ in range(2)]
    dtt = A.alloc([2, 32], F32)
    adt = A.alloc([2, 32], F32)
    adt_hi = A.alloc([2, 32], BF16)
    adt_lo = A.alloc([2, 32], BF16)
    dtmp = A.alloc([32], F32)
    ee = A.alloc([64], F32)
    ee2 = A.alloc([32], F32)
    xbcst = [A.alloc([512], F32) for _ in range(2)]
    stT = A.t[:, bm_off:bm_off + 8192].bitcast(F32).rearrange("p (a b) -> p a b", a=16)
    ssd_end = A.off
    A.off = max(att_end, ssd_end)
    print('SBUF use: cst', CA.off, 'main', A.off, 'att', att_end - base, 'ssd', ssd_end - base)

    _ENV.update(dict(accT=accT, mT=mT, yzT=yzT, ogT=ogT, gs_sig=gs_sig, ga_sig=ga_sig, xnT=xnT, QT=QT, zs2=zs2,
                     yz=yz, ost=ost, gs2=gs2, o_n=o_n, og=og, hT=hT, xraw=xraw, nc=nc, A=A, xt=xt))
    PB = []
    for i in range(8):
        t = es.enter_context(nc.psum_tensor("pb%d" % i, [128, 2048], U8))
        S.track("pb%d" % i)
        PB.append(t)

    def pf(i):
        return PB[i][:, :].bitcast(F32)

    def pbf(i):
        return PB[i][:, :].bitcast(BF16)

    rot = {'mm': [0, 1, 3, 4], 'mm2': [0, 1], 'tp': [2, 0, 1], 'S': [3, 4]}
    rotp = {k: 0 for k in rot}

    def nb(role):
        i = rot[role][rotp[role] % len(rot[role])]
        rotp[role] += 1
        return i
    OB = [5, 6, 7]
    SEGB = [3, 4]
    YB, UB, CBB, SMB = 5, 7, 6, 2

    P = 'pool'
    S.memset(P, scr32, 0.0)
    S.add(P, lambda e: e.affine_select(out=scr32, in_=scr32, pattern=[[-1, 128]], compare_op=ALU.not_equal, fill=1.0,
                                       base=0, channel_multiplier=1), [scr32], [scr32])
    S.copy('dve', identb, scr32)
    S.copy('dve', identf, scr32)
    S.memset(P, scr32, 0.0)
    S.add(P, lambda e: e.affine_select(out=scr32, in_=scr32, pattern=[[1, 128]], compare_op=ALU.not_equal, fill=1.0,
                                       base=-127, channel_multiplier=1), [scr32], [scr32])
    S.copy('dve', Jx, scr32)
    S.memset(P, scr32, 1.0)
    S.copy('dve', onesb, scr32)
    S.add(P, lambda e: e.affine_select(out=scr32, in_=scr32, pattern=[[1, 128]], compare_op=ALU.is_ge, fill=0.0,
                                       base=0, channel_multiplier=-1), [scr32], [scr32])
    S.copy('dve', tri, scr32)
    S.memset(P, scr32, 1.0)
    S.add(P, lambda e: e.affine_select(out=scr32, in_=scr32, pattern=[[-1, 128]], compare_op=ALU.is_ge, fill=0.0,
                                       base=-1, channel_multiplier=1), [scr32], [scr32])
    S.copy('dve', SU, scr32)
    S.memset(P, neghalf, -0.5)
    S.memset(P, hT, 0.0)
    S.memset(P, hTb, 0.0)
    S.memset(P, carry, 0.0)
    S.memset(P, Vr[:, :, :, 64:65], 1.0)

    cstop(-2)
    CK = 'const'
    S.dma('sp', dtb_bc, bass.AP(dt_bias.tensor, 0, [[0, 128], [1, 32]]), CK)
    S.dma('sp', a_bc, bass.AP(a_log.tensor, 0, [[0, 128], [1, 32]]), CK)
    S.dma('sp', dsk_bc, bass.AP(d_skip.tensor, 0, [[0, 128], [1, 32]]), CK)
    S.dma('sp', fnw_bc, bass.AP(final_norm_w.tensor, 0, [[0, 128], [1, 1024]]), CK)
    S.group_barrier(CK)
    cstop(-3)
    S.dma('sp', cstg[0:8, :], bass.AP(norm_w.tensor, 0, [[128, 8], [1, 128]]), 'cstg')
    S.dma('sp', cstg[8:24, :], bass.AP(ssm_norm_w.tensor, 0, [[128, 16], [1, 128]]), 'cstg')
    S.dma('sp', cstg[24:48, :], bass.AP(conv_b.tensor, 0, [[128, 24], [1, 128]]), 'cstg')
    S.group_last('cstg', 3)
    S.trs([(pf(0)[:, 0:48], cstg[0:48, :])], identf[0:48, 0:48])
    S.copy('dve', cpar[:, 0:48], pf(0)[:, 0:48])
    S.dma('sp', cstg[0:96, :], bass.AP(conv_w.tensor, 0, [[128, 96], [1, 128]]), 'cstg')
    S.trs([(pf(0)[:, 0:96], cstg[0:96, :])], identf[0:96, 0:96])
    S.copy('dve', cpar[:, 48:144], pf(0)[:, 0:96])
    for c3 in range(3):
        S.dma('sp', cstg[0:96, :], bass.AP(st_conv.tensor, c3 * 96 * 128, [[128, 96], [1, 128]]), 'cstg')
        S.trs([(pf(0)[:, 0:96], cstg[0:96, :])], identf[0:96, 0:96])
        cf = convst.rearrange("p s j c -> p (s j c)")
        S.copy('dve', cf[:, c3 * 96:(c3 + 1) * 96], pf(0)[:, 0:96])
    cstop(-4)
    EXT = ('ext', 0, 1, 0, 1)
    S.dma('sp', xt[2][0:16, 0:513], rel_bias, 'xt2')
    S.copy('dve', xt[1][0:16, 0:513], xt[2][0:16, 0:513])
    rb512 = xt[2][0:16, 512:513]
    S.copy('dve', xt[1][0:16, 513:1024], bc(rb512, [[rb512.ap[0][0], 16], [0, 511]]))
    S.dma('sp', ext, xt[1][0:16, 0:1024], 'xt1', extra_w=[EXT])
    stgv = bc(xt[2], [[pstr(xt[2]), 128], [128, AH], [1, 128]])
    for t in range(5):
        src = bass.AP(ext.tensor, 641 - 128 * t, [[1, 128], [1024, AH], [1, 128]])
        S.dma('sp', stgv, src, 'xt2', extra_r=[EXT])
        S.copy('dve', biasF[:, :, t, :], stgv)
    S.memset(P, biasF[64:128, :, 0, 64:128], NEG)
    S.memset(P, biasF[0:64, :, 4, 0:64], NEG)
    cstop(-5)
    S.act(a_bc, a_bc, AF.Exp)
    S.ts('dve', a_bc, a_bc, -1.0, None, ALU.mult)
    S.ts('dve', convw, convw, 0.5, None, ALU.mult)
    S.ts('dve', convb, convb, 0.5, None, ALU.mult)

    S.tracked['cst'] = 'const'

    uses = []
    tiles = [('p', i) for i in range(NTP)] + [('s', 0), ('s', 1)]
    def can_prefetch0(tj):
        if tj >= len(tiles):
            return False
        kind_, idx_ = tiles[tj]
        return kind_ == 'p' and idx_ >= 1 and (idx_ * 256 + 128) < SEQ - KEEP
    for tj in range(len(tiles)):
        p_now = can_prefetch0(tj) and tj >= 1
        p_next = can_prefetch0(tj + 1)
        uses += ([] if p_now else ['q0', 'q1', 'k0', 'k1', 'v0', 'v1', 'g0', 'g1', 'ga0', 'ga1'])
        uses += ['dt', 'x0', 'x1', 'x2', 'x3', 'x4', 'x5', 'wa0', 'wa1', 'z0', 'z1', 'z2', 'z3', 'gs0', 'gs1']
        uses += (['q0', 'q1', 'k0', 'k1', 'v0', 'v1', 'g0', 'g1', 'ga0', 'ga1'] if p_next else [])
        uses += ['ws0', 'ws1', 'ws2', 'ws3', 'wo0', 'wo1']
    wstate = {'issued': 0, 'cur': 0}

    def w_issue(upto):
        while wstate['issued'] < min(upto, len(uses)):
            u = wstate['issued']
            name = uses[u]
            b = WB[name]
            slot = u % NSLOT
            n = b['kc'] * b['ncols']
            dst = bc(Wr[slot], [[pstr(Wr[slot]), 128], [1, n]])
            srcv = bc(wscr[b['idx']], [[4096, 128], [1, n]])
            S.dma('sp', dst, srcv, 'W%d' % slot, extra_r=[('wscr', b['idx'], b['idx'] + 1, 0, 1)])
            wstate['issued'] += 1

    def w_use(name, issue=True):
        u = wstate['cur']
        assert uses[u] == name, (uses[u], name)
        if issue:
            w_issue(u + NSLOT)
        wstate['cur'] += 1
        b = WB[name]
        slot = u % NSLOT
        return bc(Wr[slot], [[pstr(Wr[slot]), 128], [b['ncols'], b['kc']], [1, b['ncols']]])

    def xrows(kind, idx, sub, L):
        if kind == 'p':
            r0 = idx * 256 + sub * 128
            return x_p[r0:r0 + L, :], y_p[r0:r0 + L, :]
        seq = idx * 2 + sub
        return x_s[seq * 64:seq * 64 + 64, :], y_s[seq * 64:seq * 64 + 64, :]

    def load_x(ti):
        kind, idx = tiles[ti]
        L = 128 if kind == 'p' else 64
        for sub in range(2):
            b = (ti % 2) * 2 + sub
            src, _ = xrows(kind, idx, sub, L)
            S.dma('sp', xt[b][0:L, :], src, 'xt%d' % b)

    def rms_rstd(src_sum, n, eps, dst, L, col=1):
        S.ts('dve', dst, src_sum, 1.0 / n, eps, ALU.mult, ALU.add)
        S.tt('pool', dst, dst, neghalf[0:L, 0:col], ALU.pow)

    def fm_block(wv, ncols_tiles, T, evac, role='mm'):
        for et in range(ncols_tiles):
            bk = nb(role)
            out = pf(bk)[:, 0:T]
            S.mms([dict(out=out, lhsT=wv[:, kc, et * 128:(et + 1) * 128], rhs=xnT[:, kc, 0:T],
                        start=(kc == 0), stop=(kc == 7)) for kc in range(8)])
            evac(et, out)

    def tm_block(wv, ncols, L, subs, evac, role='mm'):
        for sub in subs:
            bk = nb(role)
            out = pf(bk)[0:L, 0:ncols]
            S.mms([dict(out=out, lhsT=xnT[:, kc, sub * L:(sub + 1) * L], rhs=wv[:, kc, 0:ncols],
                        start=(kc == 0), stop=(kc == 7)) for kc in range(8)])
            evac(sub, out)

    thp = [0]

    def nth():
        thp[0] += 1
        return TH[thp[0] % 3]

    def store_state(dst_ap):
        for q in range(4):
            bk = nb('mm')
            S.trs([(pf(bk)[:, j * 128:(j + 1) * 128], hT[:, (q * 4 + j) * 128:(q * 4 + j + 1) * 128])
                   for j in range(4)], identf)
            S.copy('dve', stT[:, q * 4:(q + 1) * 4, :], pf(bk)[:, 0:512].rearrange("p (a b) -> p a b", a=4))
        S.dma('pool', bass.AP(dst_ap.tensor, dst_ap.offset, [[128, 128], [128 * 128, 16], [1, 128]]), stT, 'stT')

    def load_state(seq):
        src = st_ssm[seq]
        if 'nold' not in _DBG:
            S.dma('pool' if 'poolld' in _DBG else 'sp', stT,
                  bass.AP(src.tensor, src.offset, [[128, 128], [128 * 128, 16], [1, 128]]), 'stT')
        if 'notr' in _DBG:
            return
        for q in range(4):
            bk = nb('mm')
            S.trs([(pf(bk)[:, j * 128:(j + 1) * 128], stT[:, q * 4 + j, :]) for j in range(4)], identf)
            if 'nocp' in _DBG:
                continue
            S.copy('dve', hT[:, q * 512:(q + 1) * 512], pf(bk)[:, 0:512])
            if 'noact' in _DBG:
                continue
            S.copy('act', hTb[:, q * 512:(q + 1) * 512], pf(bk)[:, 0:512])

    def norm_tile(tj):
        kind_, idx_ = tiles[tj]
        L = 128 if kind_ == 'p' else 64
        xb = [xt[(tj % 2) * 2 + sub] for sub in range(2)]
        for sub in range(2):
            ssx = smallf[0:L, 0:1]
            S.act(junk[0:L, :], xb[sub][0:L, :], AF.Square, accum_out=ssx)
            rms_rstd(ssx, 1024.0, EPS, smallf[0:L, 1:2], L)
            S.act(xsn[0:L, :], xb[sub][0:L, :], AF.Identity, scale=smallf[0:L, 1:2])
            bk = nb('tp')
            S.trs([(pbf(bk)[:, kc * L:(kc + 1) * L], xsn[0:L, kc * 128:(kc + 1) * 128]) for kc in range(8)],
                  identb[0:L, 0:L])
            S.tt('dve', xnT[:, :, sub * L:(sub + 1) * L], pbf(bk)[:, 0:8 * L].rearrange("p (a b) -> p a b", a=8),
                 bc(normw, [[pstr(normw), 128], [1, 8], [0, L]]), ALU.mult)


    def kvga(tj, role):
        kind, idx = tiles[tj]
        L = 128 if kind == 'p' else 64
        T = 2 * L
        slot0 = (idx * 2) % 6

        def kv_out_row(sub):
            if kind == 'p':
                r0 = idx * 256 + sub * 128
                if r0 >= SEQ - KEEP:
                    return r0 - (SEQ - KEEP)
                return None
            return 448

        def kblock(ki):
            wv = w_use('k%d' % ki)

            def ev(et, out):
                pair = ki * 4 + et
                if kind == 'p':
                    S.copy('act', KT[:, pair, slot0:slot0 + 2, :], out.rearrange("p (s l) -> p s l", s=2))
                else:
                    S.copy('dve', KTs[:, pair, :, :], out.rearrange("p (s l) -> p s l", s=2))
            fm_block(wv, 4, T, ev, role)
            subs = [s_ for s_ in range(2) if kv_out_row(s_) is not None]

            def evk(sub, out):
                st = kvst[(ki + sub) % 2]
                S.copy('dve', st[0:L, :], out)
                r = kv_out_row(sub)
                if kind == 'p':
                    S.dma('pool', k_p[r:r + L, ki * 512:(ki + 1) * 512], st[0:L, :], 'kvst%d' % ((ki + sub) % 2))
                else:
                    seq = idx * 2 + sub
                    S.dma('pool', k_s[seq, 448:512, ki * 512:(ki + 1) * 512], st[0:L, :],
                          'kvst%d' % ((ki + sub) % 2))
            if subs:
                tm_block(wv, 512, L, subs, evk, role)

        def vblock(vi):
            if kind == 's' and vi == 0:
                S.memset('pool', Vs[:, :, :, 64:65], 1.0)
            wv = w_use('v%d' % vi)

            def evv(sub, out):
                if kind == 'p':
                    dst = Vr[0:L, slot0 + sub, vi * 8:(vi + 1) * 8, 0:64]
                else:
                    dst = Vs[0:L, sub, vi * 8:(vi + 1) * 8, 0:64]
                S.copy('act', dst, out.rearrange("p (h d) -> p h d", h=8))
                r = kv_out_row(sub)
                if r is not None:
                    st = kvst[(vi + sub) % 2]
                    S.copy('dve', st[0:L, :], out)
                    if kind == 'p':
                        S.dma('pool', v_p[r:r + L, vi * 512:(vi + 1) * 512], st[0:L, :],
                              'kvst%d' % ((vi + sub) % 2))
                    else:
                        seq = idx * 2 + sub
                        S.dma('pool', v_s[seq, 448:512, vi * 512:(vi + 1) * 512], st[0:L, :],
                              'kvst%d' % ((vi + sub) % 2))
            tm_block(wv, 512, L, [0, 1], evv, role)

        def gablock(gi):
            wv = w_use('ga%d' % gi)

            def evga(et, out):
                th = nth()
                S.act(th[:, 0:T], out, AF.Tanh, scale=0.5)
                S.act(ga_sig[:, gi * 4 + et, 0:T], th[:, 0:T], AF.Identity, scale=0.25, bias=0.25)
            fm_block(wv, 4, T, evga, role)
        def qblock(qi):
            wv = w_use('q%d' % qi)

            def ev(et, out):
                S.act(QT[:, qi * 4 + et, 0:T], out, AF.Identity, scale=0.125)
            fm_block(wv, 4, T, ev, role)

        def gblock(gi):
            wv = w_use('g%d' % gi)

            def evg(sub, out):
                th = nth()
                S.act(th[0:L, :], out, AF.Tanh, scale=0.5)
                S.stt('dve', gs2[0:L, sub, gi * 512:(gi + 1) * 512], th[0:L, :], 1.0, out, ALU.add, ALU.mult)
            tm_block(wv, 512, L, [0, 1], evg, role)
        return [lambda: qblock(0), lambda: qblock(1), lambda: kblock(0), lambda: kblock(1),
                lambda: vblock(0), lambda: vblock(1), lambda: gblock(0), lambda: gblock(1),
                lambda: gablock(0), lambda: gablock(1)]

    def can_prefetch(tj):
        if tj >= len(tiles):
            return False
        kind_, idx_ = tiles[tj]
        return kind_ == 'p' and idx_ >= 1 and (idx_ * 256 + 128) < SEQ - KEEP
    pre = [False] + [can_prefetch(tj) for tj in range(1, len(tiles))] + [False]

    def stop(n, ti):
        if _STOP == (n, ti):
            raise _StopBuild()

    load_x(0)
    norm_tile(0)
    for ti, (kind, idx) in enumerate(tiles):
        stop(0, ti)
        L = 128 if kind == 'p' else 64
        T = 2 * L
        last_p = (kind == 'p' and idx == NTP - 1)
        def kv_out_row(sub):
            if kind == 'p':
                r0 = idx * 256 + sub * 128
                if r0 >= SEQ - KEEP:
                    return r0 - (SEQ - KEEP)
                return None
            return 448
        if ti + 1 < len(tiles):
            load_x(ti + 1)
        xb = [xt[(ti % 2) * 2 + sub] for sub in range(2)]

        stop(1, ti)
        if kind == 'p':
            g0 = idx * 2
            slot0 = g0 % 6
        kvga_fns = kvga(ti, 'mm')
        if not pre[ti]:
            for f in kvga_fns:
                f()
        stop(1.9, ti)
        wv = w_use('dt')

        def evdt(sub, out):
            S.tt('dve', dtmp[0:L, :], out, dtb_bc[0:L, :], ALU.add)
            S.act(dtmp[0:L, :], dtmp[0:L, :], AF.Exp)
            S.act(dtt[0:L, sub, :], dtmp[0:L, :], AF.Ln, bias=1.0)
            S.tt('dve', adt[0:L, sub, :], dtt[0:L, sub, :], a_bc[0:L, :], ALU.mult)
            S.copy('dve', adt_hi[0:L, sub, :], adt[0:L, sub, :])
            S.tt('dve', adt_lo[0:L, sub, :], adt[0:L, sub, :], adt_hi[0:L, sub, :], ALU.subtract)
        tm_block(wv, 32, L, [0, 1], evdt)

        if kind == 'p':
            segs = [(0, T)]
            SW = T + 3
            S.copy('pool', xraw[:, :, 0:3], carry)
        else:
            SW = 67
            segs = [(0, 64), (67, 64)]
            for s in range(2):
                S.copy('pool', xraw[:, :, s * 67:s * 67 + 3], convst[:, idx * 2 + s].rearrange("p j c -> p c j"))
        for xi in range(6):
            wv = w_use('x%d' % xi)

            def evx(et, out, xi=xi):
                ct = xi * 4 + et
                if kind == 'p':
                    S.copy('act', xraw[:, ct, 3:3 + T], out)
                else:
                    S.copy('dve', xraw[:, ct, 0:134].rearrange("p (s c) -> p s c", c=67)[:, :, 3:67],
                           out.rearrange("p (s c) -> p s c", c=64))
            fm_block(wv, 4, T, evx)
            if last_p or kind == 's':
                osubs = [1] if kind == 'p' else [0, 1]

                def evc(sub, out, xi=xi):
                    st = xbcst[(xi + sub) % 2]
                    S.copy('dve', st[L - 32:L, :], out[L - 32:L, :])
                    dst = conv_p if kind == 'p' else conv_s[idx * 2 + sub]
                    S.dma('pool', dst[:, xi * 512:(xi + 1) * 512], st[L - 3:L, :], 'xbcst%d' % ((xi + sub) % 2))
                tm_block(wv, 512, L, osubs, evc)
        def conv_block(xj):
            accs = [cacc[0], cacc[1], xbcst[0][:, 0:256], xbcst[1][:, 0:256]]
            for (c0, nt) in segs:
                cts = [xj * 4 + e for e in range(4)]
                for e, ct in enumerate(cts):
                    S.ts('dve', accs[e][:, 0:nt], xraw[:, ct, c0:c0 + nt], convw[:, 0, ct:ct + 1],
                         convb[:, ct:ct + 1], ALU.mult, ALU.add)
                for j in range(1, 4):
                    for e, ct in enumerate(cts):
                        S.stt('dve', accs[e][:, 0:nt], xraw[:, ct, c0 + j:c0 + j + nt],
                              convw[:, j, ct:ct + 1], accs[e][:, 0:nt], ALU.mult, ALU.add)
                if kind == 'p' and c0 == 0:
                    S.copy('pool', carry[:, xj * 4:(xj + 1) * 4, :], xraw[:, xj * 4:(xj + 1) * 4, T:T + 3])
                ths = []
                for e, ct in enumerate(cts):
                    th = nth()
                    ths.append(th)
                    S.act(th[:, 0:nt], accs[e][:, 0:nt], AF.Tanh)
                    if e >= 1:
                        S.stt('dve', xraw[:, cts[e - 1], c0:c0 + nt], ths[e - 1][:, 0:nt], 1.0,
                              accs[e - 1][:, 0:nt], ALU.add, ALU.mult)
                S.stt('dve', xraw[:, cts[3], c0:c0 + nt], ths[3][:, 0:nt], 1.0, accs[3][:, 0:nt],
                      ALU.add, ALU.mult)


        conv_todo = [0, 1, 2, 3, 4, 5]
        stop(2, ti)
        for sub in range(2):
            if kind == 'p':
                g = idx * 2 + sub
                tl = [t for t in range(5) if g - 4 + t >= 0]
            else:
                seq = idx * 2 + sub
                tl = [0, 1, 2, 3] if 'not4' in _DBG else ([4] if 'only4' in _DBG else [0, 1, 2, 3, 4])
                kv32 = bc(kvst[0], [[pstr(kvst[0]), 128], [1, 1024]])
                for tt_ in range(4):
                    S.dma('sp', kv32, cv[seq, tt_ * 128:(tt_ + 1) * 128, :], 'kvst0')
                    S.copy('dve', Vr[:, tt_, :, 0:64], kv32.rearrange("p (h d) -> p h d", h=AH))
                    S.dma('sp', ost, ck[seq, tt_ * 128:(tt_ + 1) * 128, :], 'ost')
                    S.copy('act', ckst[:, tt_, :], ost)
                    bk = nb('tp')
                    S.trs([(pbf(bk)[:, pr * 128:(pr + 1) * 128], ckst[:, tt_, pr * 128:(pr + 1) * 128])
                           for pr in range(8)], identb)
                    S.copy('dve', KT[:, :, tt_, :], pbf(bk)[:, 0:1024].rearrange("p (a b) -> p a b", a=8))
                S.dma('pool', k_s[seq, 0:448, :], ck[seq, 64:512, :], 'd2d')
                S.dma('pool', v_s[seq, 0:448, :], cv[seq, 64:512, :], 'd2d')
            stop(2.2, ti)
            for ob in OB:
                S.memset('dve', pf(ob)[0:L, :], 0.0)
            blocks = [(h, t) for h in range(AH) for t in tl]
            per = 512 // L
            groups = [blocks[i:i + per] for i in range(0, len(blocks), per)]
            pend = None

            def emit_pv(pg):
                grp, pt = pg
                if 'nopv' in _DBG:
                    return
                items = []
                for cb, (h, t) in enumerate(grp):
                    Mk = 64 if (kind == 's' and t == 4) else 128
                    if kind == 'p':
                        vt = Vr[0:Mk, (g - 4 + t) % 6, h, 0:65]
                    elif t < 4:
                        vt = Vr[0:Mk, t, h, 0:65]
                    else:
                        vt = Vs[0:Mk, sub, h, 0:65]
                    ob = OB[h // 6]
                    oc = (h % 6) * 65
                    items.append(dict(out=pf(ob)[0:L, oc:oc + 65], lhsT=pt[0:Mk, cb * L:(cb + 1) * L], rhs=vt,
                                      start=False, stop=True))
                S.mms(items)
            for gi_, grp in enumerate(groups):
                sbk = nb('S')
                items = []
                for cb, (h, t) in enumerate(grp):
                    pair, pb_ = h // 2, (h % 2) * 64
                    Mk = 64 if (kind == 's' and t == 4) else 128
                    out = pf(sbk)[0:Mk, cb * L:(cb + 1) * L]
                    if Mk == 128:
                        items.append(dict(out=out, lhsT=Jx, rhs=biasF[:, h, t, 0:L], start=True, stop=False))
                    else:
                        items.append(dict(out=out, lhsT=Jx[:, 0:64], rhs=biasF[:, h, t, 0:L],
                                          start=True, stop=False))
                    if kind == 'p':
                        kt = KT[pb_:pb_ + 64, pair, (g - 4 + t) % 6, 0:Mk]
                    elif t < 4:
                        kt = KT[pb_:pb_ + 64, pair, t, 0:Mk]
                    else:
                        kt = KTs[pb_:pb_ + 64, pair, sub, 0:Mk]
                    items.append(dict(out=out, lhsT=kt, rhs=QT[pb_:pb_ + 64, pair, sub * L:(sub + 1) * L],
                                      start=False, stop=True))
                S.mms(items)
                pt = PT[gi_ % 3]
                ncol = len(grp) * L
                S.act(pt[:, 0:ncol], pf(sbk)[:, 0:ncol], AF.Exp)
                if pend is not None:
                    emit_pv(pend)
                pend = (grp, pt)
                if conv_todo and (gi_ % 6 == 5 or len(groups) < 12):
                    conv_block(conv_todo.pop(0))
            emit_pv(pend)
            stop(2.4, ti)
            for obi, ob in enumerate(OB):
                h0 = obi * 6
                nh_ = min(6, AH - h0)
                ov = pf(ob)[0:L, 0:nh_ * 65].rearrange("p (h d) -> p h d", h=nh_)
                S.recip(rs[0:L, h0:h0 + nh_], ov[:, :, 64])
                S.tt('dve', o_n[0:L, h0 * 64:(h0 + nh_) * 64].rearrange("p (h d) -> p h d", h=nh_), ov[:, :, 0:64],
                     bc(rs[0:L, h0:h0 + nh_], [[pstr(rs), L], [1, nh_], [0, 64]]), ALU.mult)
            S.tt('pool', og[0:L, :], o_n[0:L, :], gs2[0:L, sub, :], ALU.mult)
            bk = nb('tp')
            S.trs([(pbf(bk)[:, kc * L:(kc + 1) * L], og[0:L, kc * 128:(kc + 1) * 128]) for kc in range(8)],
                  identb[0:L, 0:L])
            S.copy('dve', ogT[:, :, sub * L:(sub + 1) * L], pbf(bk)[:, 0:8 * L].rearrange("p (a b) -> p a b", a=8))

        while conv_todo:
            conv_block(conv_todo.pop(0))
        stop(3, ti)
        stop(4, ti)
        for wi in range(2):
            wv = w_use('wa%d' % wi)
            for et in range(4):
                bk = nb('mm')
                out = pf(bk)[:, 0:T]
                S.mms([dict(out=out, lhsT=wv[:, kc, et * 128:(et + 1) * 128], rhs=ogT[:, kc, 0:T],
                            start=(kc == 0), stop=(kc == 7)) for kc in range(8)])
                S.tt('dve', accT[:, wi * 4 + et, 0:T], out, ga_sig[:, wi * 4 + et, 0:T], ALU.mult)

        stop(5, ti)
        for zi in range(4):
            wv = w_use('z%d' % zi)

            def evz(sub, out, zi=zi):
                th = nth()
                S.act(th[0:L, :], out, AF.Tanh, scale=0.5)
                S.stt('dve', zs2[0:L, sub, zi * 512:(zi + 1) * 512], th[0:L, :], 1.0, out, ALU.add, ALU.mult)
            tm_block(wv, 512, L, [0, 1], evz)
        for gi in range(2):
            wv = w_use('gs%d' % gi)

            def evgs(et, out, gi=gi):
                th = nth()
                S.act(th[:, 0:T], out, AF.Tanh, scale=0.5)
                S.act(gs_sig[:, gi * 4 + et, 0:T], th[:, 0:T], AF.Identity, scale=0.5, bias=0.5)
            fm_block(wv, 4, T, evgs)

        if ti + 1 < len(tiles):
            norm_tile(ti + 1)
        stop(6, ti)
        pf_fns = kvga(ti + 1, 'mm2') if pre[ti + 1] else []
        def cview(t, n):
            return bc(t, [[pstr(t), L], [L, n], [1, L]])
        Bmh_v, Bml_v, Eb_v, MT_v, cb_v = cview(Bmh, 8), cview(Bml, 8), cview(Eb, 8), cview(MTb, 8), cview(cbTm, 4)
        for sub in range(2):
            if kind == 'p':
                cs = sub * 128
            else:
                cs = sub * 67
                load_state(idx * 2 + sub)
            stop(6.2, ti)
            for hb in range(2):
                bk = nb('tp')
                S.trs([(pbf(bk)[0:L, j * 128:(j + 1) * 128], xraw[:, hb * 8 + j, cs:cs + L]) for j in range(8)],
                      identb)
                S.copy('act' if hb else 'dve', xs_tok[0:L, hb * 1024:(hb + 1) * 1024], pbf(bk)[0:L, 0:1024])
            bk = nb('tp')
            S.trs([(pbf(bk)[0:L, j * 128:(j + 1) * 128], xraw[:, 16 + j, cs:cs + L]) for j in range(4)], identb)
            S.copy('dve', B_tok[0:L, :], pbf(bk)[0:L, 0:512])
            stop(6.4, ti)
            sm = pf(SMB)
            ah, al = adt_hi[0:L, sub, :], adt_lo[0:L, sub, :]
            S.mms([dict(out=sm[0:L, 0:32], lhsT=tri[0:L, 0:L], rhs=ah, start=True, stop=False),
                   dict(out=sm[0:L, 0:32], lhsT=tri[0:L, 0:L], rhs=al, start=False, stop=True),
                   dict(out=sm[0:L, 32:64], lhsT=SU[0:L, 0:L], rhs=ah, start=True, stop=False),
                   dict(out=sm[0:L, 32:64], lhsT=SU[0:L, 0:L], rhs=al, start=False, stop=True),
                   dict(out=sm[:, 64:96], lhsT=onesb[0:L, :], rhs=ah, start=True, stop=False),
                   dict(out=sm[:, 64:96], lhsT=onesb[0:L, :], rhs=al, start=False, stop=True)])
            S.act(ee[0:L, :], sm[0:L, 0:64], AF.Exp)
            S.act(ee2, sm[:, 64:96], AF.Exp)
            cbb = pf(CBB)
            S.mms([dict(out=cbb[0:L, gq * L:(gq + 1) * L], lhsT=xraw[:, 16 + gq, cs:cs + L],
                        rhs=xraw[:, 20 + gq, cs:cs + L], start=True, stop=True) for gq in range(4)])
            S.tt('dve', cb_v, cbb[0:L, 0:4 * L].rearrange("p (g l) -> p g l", g=4),
                 bc(tri, [[pstr(tri), L], [0, 4], [1, L]]), ALU.mult)
            stop(6.6, ti)
            def b8(ap2):
                return bc(ap2, [[ap2.ap[0][0], L], [1, 8], [0, 64]])
            x3 = lambda a: a[0:L, :].rearrange("p (r d) -> p r d", r=8)
            nhalf = 2 if L == 128 else 1
            hh = 8 // nhalf

            def s0_pool(gq):
                xdt, xdt2, xsd = xdtb[gq % 2], xdt2b[gq % 2], xsdb[gq % 2]
                xsg = xs_tok[0:L, gq * 512:(gq + 1) * 512].rearrange("p (r d) -> p r d", r=8)
                S.tt('pool', x3(xdt), xsg, b8(dtt[0:L, sub, gq * 8:(gq + 1) * 8]), ALU.mult)
                S.tt('pool', x3(xdt2), x3(xdt), b8(ee[0:L, 32 + gq * 8:32 + (gq + 1) * 8]), ALU.mult)
                S.tt('pool', x3(xsd), xsg, b8(dsk_bc[0:L, gq * 8:(gq + 1) * 8]), ALU.mult)

            def s1a(gq):
                for (src, dst) in ((adt_hi, Bmh_v), (adt_lo, Bml_v)):
                    a8 = src[0:L, sub, gq * 8:(gq + 1) * 8]
                    S.tt('dve', dst, bc(a8, [[a8.ap[0][0], L], [1, 8], [0, L]]),
                         bc(tri, [[pstr(tri), L], [0, 8], [1, L]]), ALU.mult)

            def s1b(gq):
                items = []
                for hf in range(nhalf):
                    out = pf(SEGB[hf])[0:L, 0:hh * L]
                    items.append(dict(out=out, lhsT=SU[0:L, 0:L], rhs=Bmh_v[:, hf * hh:(hf + 1) * hh, :],
                                      start=True, stop=False))
                    items.append(dict(out=out, lhsT=SU[0:L, 0:L], rhs=Bml_v[:, hf * hh:(hf + 1) * hh, :],
                                      start=False, stop=True))
                S.mms(items)
                for hf in range(nhalf):
                    S.act(Eb_v[:, hf * hh:(hf + 1) * hh, :],
                          pf(SEGB[hf])[0:L, 0:hh * L].rearrange("p (r l) -> p r l", r=hh), AF.Exp)

            def s2(gq):
                xdt, xdt2, xsd = xdtb[gq % 2], xdt2b[gq % 2], xsdb[gq % 2]
                cg = cb_v[:, gq, :]
                S.tt('dve', MT_v, Eb_v, bc(cg, [[cg.ap[0][0], L], [0, 8], [1, L]]), ALU.mult)
                yb = pf(YB)
                S.mms([dict(out=yb[0:L, 0:512], lhsT=xraw[:, 20 + gq, cs:cs + L], rhs=hTb[:, gq * 512:(gq + 1) * 512],
                            start=True, stop=False)])
                S.tt('dve', yb[0:L, 0:512].rearrange("p (r d) -> p r d", r=8),
                     yb[0:L, 0:512].rearrange("p (r d) -> p r d", r=8), b8(ee[0:L, gq * 8:(gq + 1) * 8]), ALU.mult)
                items = [dict(out=yb[0:L, 0:512], lhsT=identb[0:L, 0:L], rhs=xsd[0:L, :], start=False, stop=False)]
                for r in range(8):
                    items.append(dict(out=yb[0:L, r * 64:(r + 1) * 64], lhsT=MT_v[:, r, :],
                                      rhs=xdt[0:L, r * 64:(r + 1) * 64], start=False, stop=(r == 7)))
                S.mms(items)
                ub = pf(UB)
                S.mms([dict(out=ub[:, 0:512], lhsT=B_tok[0:L, gq * 128:(gq + 1) * 128], rhs=xdt2[0:L, :],
                            start=True, stop=True)])

            def s3(gq):
                yb = pf(YB)
                ub = pf(UB)
                S.tt('dve', yz[0:L, gq, :], yb[0:L, 0:512], zs2[0:L, sub, gq * 512:(gq + 1) * 512], ALU.mult)
                S.act(junk[0:L, 0:512], yz[0:L, gq, :], AF.Square, accum_out=ssq[0:L, gq:gq + 1])
                hg = hT[:, gq * 512:(gq + 1) * 512]
                e8 = ee2[:, gq * 8:(gq + 1) * 8]
                S.tt('pool', hg.rearrange("p (r d) -> p r d", r=8), hg.rearrange("p (r d) -> p r d", r=8),
                     bc(e8, [[e8.ap[0][0], 128], [1, 8], [0, 64]]), ALU.mult)
                S.tt('dve', hg, hg, ub[:, 0:512], ALU.add)
                S.copy('act', hTb[:, gq * 512:(gq + 1) * 512], hg)

            s0_pool(0)
            s1a(0)
            s1b(0)
            for gq in range(4):
                if gq < 3:
                    s0_pool(gq + 1)
                s2(gq)
                if gq < 3:
                    s1a(gq + 1)
                    s1b(gq + 1)
                s3(gq)
                groups_left = (1 - sub) * 4 + (4 - gq)
                npop = -(-len(pf_fns) // groups_left) if pf_fns else 0
                for _ in range(npop):
                    pf_fns.pop(0)()
            stop(6.8, ti)
            S.ts('dve', rstd4[0:L, :], ssq[0:L, :], 1.0 / 512.0, 4.0 * EPS, ALU.mult, ALU.add)
            S.tt('pool', rstd4[0:L, :], rstd4[0:L, :], neghalf[0:L, :], ALU.pow)
            S.tt('pool', yz[0:L, :, :], yz[0:L, :, :], bc(rstd4, [[pstr(rstd4), L], [1, 4], [0, 512]]), ALU.mult)
            yzf = yz[0:L, :, :].rearrange("p a b -> p (a b)")
            for hb in range(2):
                bk = nb('tp')
                S.trs([(pbf(bk)[:, j * L:(j + 1) * L], yzf[:, (hb * 8 + j) * 128:(hb * 8 + j + 1) * 128])
                       for j in range(8)], identb[0:L, 0:L])
                nwv = nwssm[:, hb * 8:(hb + 1) * 8]
                S.tt('dve', yzT[:, hb * 8:(hb + 1) * 8, sub * L:(sub + 1) * L],
                     pbf(bk)[:, 0:8 * L].rearrange("p (a b) -> p a b", a=8),
                     bc(nwv, [[nwv.ap[0][0], 128], [1, 8], [0, L]]), ALU.mult)
            stop(6.9, ti)
            if kind == 's':
                store_state(ssm_s[idx * 2 + sub])
            elif last_p and sub == 1:
                store_state(ssm_p)

        stop(7, ti)
        while pf_fns:
            pf_fns.pop(0)()
        for wi in range(4):
            wv = w_use('ws%d' % wi)
            for et in range(2):
                e8 = wi * 2 + et
                bk = nb('mm')
                out = pf(bk)[:, 0:T]
                S.mms([dict(out=out, lhsT=wv[:, kc, et * 128:(et + 1) * 128], rhs=yzT[:, kc, 0:T],
                            start=(kc == 0), stop=(kc == 15)) for kc in range(16)])
                th = nth()
                S.tt('dve', th[:, 0:T], out, gs_sig[:, e8, 0:T], ALU.mult)
                S.tt('pool', mT[:, e8, 0:T], th[:, 0:T], accT[:, e8, 0:T], ALU.add)

        stop(8, ti)
        wvs = [w_use('wo0'), w_use('wo1', issue=False)]
        for sub in range(2):
            for hf in range(2):
                bk = nb('mm')
                out = pf(bk)[0:L, 0:512]
                S.mms([dict(out=out, lhsT=mT[:, kc, sub * L:(sub + 1) * L], rhs=wvs[hf][:, kc, :],
                            start=(kc == 0), stop=(kc == 7)) for kc in range(8)])
                S.tt('dve', ost[0:L, hf * 512:(hf + 1) * 512], out, xb[sub][0:L, hf * 512:(hf + 1) * 512], ALU.add)
            ssx = smallf[0:L, 2:3]
            S.act(junk[0:L, :], ost[0:L, :], AF.Square, accum_out=ssx)
            rms_rstd(ssx, 1024.0, EPS, smallf[0:L, 3:4], L)
            S.stt('dve', ost[0:L, :], ost[0:L, :], smallf[0:L, 3:4], fnw_bc[0:L, :], ALU.mult, ALU.mult)
            _, dsty = xrows(kind, idx, sub, L)
            S.dma('pool', dsty, ost[0:L, :], 'ost')

    if dbg is not None:
        dbg(S, locals())


_CACHE = {}


def _get_nc(SEQ):
    if SEQ not in _CACHE:
        nc = bass.Bass("TRN2", target_bir_lowering=False)
        build(nc, SEQ)
        _CACHE[SEQ] = nc
    return _CACHE[SEQ]


def kernel(x_prompt, x_sample, state_ssm, state_conv, cache_k, cache_v, norm_w, w_in, conv_w, conv_b, dt_bias,
           a_log, d_skip, ssm_norm_w, w_out_ssm, rel_bias, w_out_att, w_o, final_norm_w):
    f = lambda a: np.ascontiguousarray(np.asarray(a, dtype=np.float32))
    B, SEQ, _ = x_prompt.shape
    assert B == 8 and SEQ % 256 == 0
    KEEP = min(512, SEQ)
    nc = _get_nc(SEQ)
    shared = dict(norm_w=f(norm_w[0]), w_in=f(w_in[0]), conv_w=f(conv_w[0]), conv_b=f(conv_b[0]),
                  dt_bias=f(dt_bias[0]), a_log=f(a_log[0]), d_skip=f(d_skip[0]), ssm_norm_w=f(ssm_norm_w[0]),
                  w_out_ssm=f(w_out_ssm[0]), rel_bias=f(rel_bias[0]), w_out_att=f(w_out_att[0]), w_o=f(w_o[0]),
                  final_norm_w=f(final_norm_w))
    in_maps = []
    for c in range(_NCORES):
        m = dict(shared)
        m['x_p'] = f(x_prompt[c])
        m['x_s'] = f(x_sample[4 * c:4 * c + 4]).reshape(256, D)
        m['st_ssm'] = f(state_ssm[0, 4 * c:4 * c + 4]).reshape(4, 2048, 128)
        m['st_conv'] = f(state_conv[0, 4 * c:4 * c + 4])
        m['ck'] = f(cache_k[0, 4 * c:4 * c + 4]).reshape(4, 512, D)
        m['cv'] = f(cache_v[0, 4 * c:4 * c + 4]).reshape(4, 512, D)
        in_maps.append(m)
    res = run_bass_kernel_spmd(nc, in_maps, core_ids=list(range(_NCORES)))
    R = list(res.results) + [res.results[0]] * (8 - _NCORES)
    y_prompt = np.stack([R[c]['y_p'] for c in range(8)]).astype(np.float32)
    y_sample = np.concatenate([R[c]['y_s'].reshape(4, 64, D) for c in range(8)]).astype(np.float32)
    ssm_p = np.stack([R[c]['ssm_p'].reshape(NH, 64, 128) for c in range(8)])[None].astype(np.float32)
    conv_p = np.stack([R[c]['conv_p'] for c in range(8)])[None].astype(np.float32)
    k_p = np.stack([R[c]['k_p'].reshape(KEEP, AH, 64) for c in range(8)])[None].astype(np.float32)
    v_p = np.stack([R[c]['v_p'].reshape(KEEP, AH, 64) for c in range(8)])[None].astype(np.float32)
    ssm_s = np.concatenate([R[c]['ssm_s'].reshape(4, NH, 64, 128) for c in range(8)])[None].astype(np.float32)
    conv_s = np.concatenate([R[c]['conv_s'] for c in range(8)])[None].astype(np.float32)
    k_s = np.concatenate([R[c]['k_s'].reshape(4, 512, AH, 64) for c in range(8)])[None].astype(np.float32)
    v_s = np.concatenate([R[c]['v_s'].reshape(4, 512, AH, 64) for c in range(8)])[None].astype(np.float32)
    return (y_prompt, y_sample, ssm_p, conv_p, k_p, v_p, ssm_s, conv_s, k_s, v_s)
```
